# Optimizing a Trainium2 kernel written in Bass

```python
import math
import jax, jax.numpy as jnp
from jax import lax
import numpy as np

D_MODEL = 1024
BATCH = 32
SEQ = 2048
DEPTH = 4

ATTN_HEADS = 8
ATTN_KV_HEADS = 2
HEAD_DIM = 64
ATTN_WIDTH = ATTN_HEADS * HEAD_DIM
KV_WIDTH = ATTN_KV_HEADS * HEAD_DIM
WINDOW = 128
ATTN_BLOCK = 128
S5_WIDTH = D_MODEL // 2
S5_GROUP = 16
S5_GROUPS = S5_WIDTH // S5_GROUP
S5_STATE = 64
EVEN_IN = ATTN_WIDTH + 2 * KV_WIDTH + S5_WIDTH
EVEN_MIX = ATTN_WIDTH + S5_WIDTH
MLSTM_HEADS = 8
MLSTM_WIDTH = D_MODEL
MLSTM_HEAD_DIM = MLSTM_WIDTH // MLSTM_HEADS
MLSTM_CHUNK = 128
ODD_IN = 4 * MLSTM_WIDTH + 4 * MLSTM_HEADS
CONV_WIDTH = 3
D_FF = 2816
N_EVEN = (DEPTH + 1) // 2
N_ODD = DEPTH // 2
DEEPNORM_ALPHA = (2 * DEPTH) ** 0.25
DEEPNORM_BETA = (8 * DEPTH) ** -0.25
LN_EPS = 1e-5

kernel_name = 'hybrid_swa_s5_mlstm_convffn_deepnorm'


def alibi_slopes(n_heads):
    return 2.0 ** (-8.0 * jnp.arange(1, n_heads + 1, dtype=jnp.float32) / n_heads)


def layer_norm(x, g, b):
    xf = x.astype(jnp.float32)
    mu = xf.mean(-1, keepdims=True)
    var = jnp.mean(jnp.square(xf - mu), -1, keepdims=True)
    y = (xf - mu) * lax.rsqrt(var + LN_EPS) * g.astype(jnp.float32) + b.astype(jnp.float32)
    return y.astype(x.dtype)


def dwconv3(x, w, b):
    c = x.shape[-1]
    y = lax.conv_general_dilated(x, w[:, None, :].astype(x.dtype), window_strides=(1,),
                                 padding=((1, 1),), dimension_numbers=('NWC', 'WIO', 'NWC'),
                                 feature_group_count=c)
    return y + b.astype(x.dtype)


def windowed_gqa(q, k, v, sink):
    bsz, seq = q.shape[0], q.shape[1]
    nb = seq // ATTN_BLOCK
    grp = ATTN_HEADS // ATTN_KV_HEADS
    qb = q.reshape(bsz, nb, ATTN_BLOCK, ATTN_KV_HEADS, grp, HEAD_DIM)

    def windows(t):
        tp = jnp.pad(t, ((0, 0), (WINDOW, WINDOW), (0, 0), (0, 0)))
        tb = tp.reshape(bsz, nb + 2, ATTN_BLOCK, ATTN_KV_HEADS, HEAD_DIM)
        return jnp.concatenate([tb[:, :-2], tb[:, 1:-1], tb[:, 2:]], axis=2)

    kw, vw = windows(k), windows(v)
    s = jnp.einsum('bnqkgd,bnckd->bnkgqc', qb, kw).astype(jnp.float32) * (HEAD_DIM ** -0.5)
    blk = jnp.arange(nb)[:, None, None] * ATTN_BLOCK
    qpos = blk + jnp.arange(ATTN_BLOCK)[None, :, None]
    kpos = blk - WINDOW + jnp.arange(3 * ATTN_BLOCK)[None, None, :]
    dist = jnp.abs(qpos - kpos)
    valid = (dist <= WINDOW) & (kpos >= 0) & (kpos < seq)
    slopes = alibi_slopes(ATTN_HEADS).reshape(ATTN_KV_HEADS, grp)
    bias = -slopes[None, :, :, None, None] * dist[:, None, None].astype(jnp.float32)
    s = jnp.where(valid[:, None, None], s + bias, -jnp.inf)
    sink_l = sink.astype(jnp.float32).reshape(ATTN_KV_HEADS, grp, 1, 1)
    m = jnp.maximum(s.max(-1, keepdims=True), sink_l)
    e = jnp.exp(s - m)
    p = e / (e.sum(-1, keepdims=True) + jnp.exp(sink_l - m))
    o = jnp.einsum('bnkgqc,bnckd->bnqkgd', p.astype(v.dtype), vw)
    return o.reshape(bsz, seq, ATTN_WIDTH)


def complex_affine_combine(e1, e2):
    a1r, a1i, b1r, b1i = e1
    a2r, a2i, b2r, b2i = e2
    ar = a2r * a1r - a2i * a1i
    ai = a2r * a1i + a2i * a1r
    br = a2r * b1r - a2i * b1i + b2r
    bi = a2r * b1i + a2i * b1r + b2i
    return (ar, ai, br, bi)


def s5_direction(u_g, a_re, a_im, log_dt, b_re, b_im, c_re, c_im, reverse):
    f32 = jnp.float32
    a_re, a_im = a_re.astype(f32), a_im.astype(f32)
    b_re, b_im = b_re.astype(f32), b_im.astype(f32)
    c_re, c_im = c_re.astype(f32), c_im.astype(f32)
    dt = jnp.exp(log_dt.astype(f32))[:, None]
    mag = jnp.exp(a_re * dt)
    lr, li = mag * jnp.cos(a_im * dt), mag * jnp.sin(a_im * dt)
    den = a_re * a_re + a_im * a_im
    xr, xi = lr - 1.0, li
    zr = (xr * a_re + xi * a_im) / den
    zi = (xi * a_re - xr * a_im) / den
    bbr = zr[..., None] * b_re - zi[..., None] * b_im
    bbi = zr[..., None] * b_im + zi[..., None] * b_re
    bu_r = jnp.einsum('blgh,gph->blgp', u_g, bbr)
    bu_i = jnp.einsum('blgh,gph->blgp', u_g, bbi)
    seq = u_g.shape[1]
    lam_r = jnp.broadcast_to(lr[None, None], (1, seq) + lr.shape)
    lam_i = jnp.broadcast_to(li[None, None], (1, seq) + li.shape)
    _, _, st_r, st_i = lax.associative_scan(complex_affine_combine, (lam_r, lam_i, bu_r, bu_i),
                                            reverse=reverse, axis=1)
    return jnp.einsum('ghp,blgp->blgh', c_re, st_r) - jnp.einsum('ghp,blgp->blgh', c_im, st_i)


def s5_mixer(u, a_re, a_im, log_dt, b_re, b_im, c_re, c_im, d_skip, w_glu):
    bsz, seq = u.shape[0], u.shape[1]
    u_g = u.astype(jnp.float32).reshape(bsz, seq, S5_GROUPS, S5_GROUP)
    y = (s5_direction(u_g, a_re[0], a_im[0], log_dt[0], b_re[0], b_im[0], c_re[0], c_im[0], False)
         + s5_direction(u_g, a_re[1], a_im[1], log_dt[1], b_re[1], b_im[1], c_re[1], c_im[1], True)
         + d_skip.astype(jnp.float32).reshape(S5_GROUPS, S5_GROUP) * u_g)
    h = jax.nn.gelu(y.reshape(bsz, seq, S5_WIDTH))
    return h * jax.nn.sigmoid(h @ w_glu.astype(jnp.float32))


def attn_s5_mixer(x, w_in, sink, a_re, a_im, log_dt, b_re, b_im, c_re, c_im, d_skip, w_glu, w_out):
    bsz, seq = x.shape[0], x.shape[1]
    z = x @ w_in
    q, k, v, u = jnp.split(z, [ATTN_WIDTH, ATTN_WIDTH + KV_WIDTH, ATTN_WIDTH + 2 * KV_WIDTH], axis=-1)
    y_attn = windowed_gqa(q.reshape(bsz, seq, ATTN_HEADS, HEAD_DIM),
                          k.reshape(bsz, seq, ATTN_KV_HEADS, HEAD_DIM),
                          v.reshape(bsz, seq, ATTN_KV_HEADS, HEAD_DIM), sink)
    y_s5 = s5_mixer(u, a_re, a_im, log_dt, b_re, b_im, c_re, c_im, d_skip, w_glu)
    y = jnp.concatenate([y_attn.astype(x.dtype), y_s5.astype(x.dtype)], axis=-1)
    return y @ w_out


def mlstm_chunkwise(q, k, v, i_pre, f_pre):
    bsz, nh, seq, dh = q.shape
    nc = seq // MLSTM_CHUNK
    shp = (bsz, nh, nc, MLSTM_CHUNK, dh)
    q = q.reshape(shp)
    k = k.reshape(shp) * (dh ** -0.5)
    v = v.reshape(shp)
    ig = i_pre.reshape(bsz, nh, nc, MLSTM_CHUNK)
    b = jnp.cumsum(jax.nn.log_sigmoid(f_pre).reshape(bsz, nh, nc, MLSTM_CHUNK), axis=-1)
    b_end = b[..., -1]
    w_end = b_end[..., None] - b + ig
    m_loc = w_end.max(-1)
    e_end = jnp.exp(w_end - m_loc[..., None])
    c_loc = jnp.einsum('bhcsd,bhcse->bhcde', e_end[..., None] * k, v)
    n_loc = jnp.einsum('bhcs,bhcsd->bhcd', e_end, k)

    def step(carry, xs):
        c_prev, n_prev, m_prev = carry
        c_l, n_l, m_l, b_l = xs
        m_new = jnp.maximum(b_l + m_prev, m_l)
        s_prev = jnp.exp(b_l + m_prev - m_new)
        s_loc = jnp.exp(m_l - m_new)
        c_new = s_prev[..., None, None] * c_prev + s_loc[..., None, None] * c_l
        n_new = s_prev[..., None] * n_prev + s_loc[..., None] * n_l
        return (c_new, n_new, m_new), (c_prev, n_prev, m_prev)

    init = (jnp.zeros((bsz, nh, dh, dh), jnp.float32), jnp.zeros((bsz, nh, dh), jnp.float32),
            jnp.zeros((bsz, nh), jnp.float32))
    to_c = lambda t: jnp.moveaxis(t, 2, 0)
    _, (c_in, n_in, m_in) = lax.scan(step, init, (to_c(c_loc), to_c(n_loc), to_c(m_loc), to_c(b_end)))
    c_in, n_in, m_in = jnp.moveaxis(c_in, 0, 2), jnp.moveaxis(n_in, 0, 2), jnp.moveaxis(m_in, 0, 2)

    log_inter = b + m_in[..., None]
    causal = jnp.tril(jnp.ones((MLSTM_CHUNK, MLSTM_CHUNK), dtype=bool))
    d_log = jnp.where(causal, b[..., :, None] - b[..., None, :] + ig[..., None, :], -jnp.inf)
    m_t = jnp.maximum(d_log.max(-1), log_inter)
    s = jnp.einsum('bhctd,bhcsd->bhcts', q, k) * jnp.exp(d_log - m_t[..., None])
    inter = jnp.exp(log_inter - m_t)
    num = (jnp.einsum('bhcts,bhcse->bhcte', s, v)
           + inter[..., None] * jnp.einsum('bhctd,bhcde->bhcte', q, c_in))
    den = s.sum(-1) + inter * jnp.einsum('bhctd,bhcd->bhct', q, n_in)
    h = num / jnp.maximum(jnp.abs(den), jnp.exp(-m_t))[..., None]
    return h.reshape(bsz, nh, seq, dh)


def mlstm_mixer(x, w_in, gate_b, conv_w, conv_b, norm_g, w_out):
    bsz, seq = x.shape[0], x.shape[1]
    f32 = jnp.float32
    z = x @ w_in
    qk, v, o, gates = jnp.split(z, [2 * MLSTM_WIDTH, 3 * MLSTM_WIDTH, 4 * MLSTM_WIDTH], axis=-1)
    qk = jax.nn.silu(dwconv3(qk, conv_w, conv_b))
    q, k = jnp.split(qk, 2, axis=-1)
    heads = lambda t: t.reshape(bsz, seq, MLSTM_HEADS, MLSTM_HEAD_DIM).transpose(0, 2, 1, 3).astype(f32)
    q, k, v = heads(q), heads(k), heads(v)
    g = (gates + gate_b).astype(f32).reshape(bsz, seq, 4, MLSTM_HEADS).transpose(2, 0, 3, 1)
    flip = lambda t: jnp.flip(t, axis=2)
    h_fwd = mlstm_chunkwise(q, k, v, g[0], g[1])
    h_bwd = flip(mlstm_chunkwise(flip(q), flip(k), flip(v), flip(g[2]), flip(g[3])))
    h = h_fwd + h_bwd
    mu = h.mean(-1, keepdims=True)
    var = jnp.mean(jnp.square(h - mu), -1, keepdims=True)
    h = (h - mu) * lax.rsqrt(var + LN_EPS) * norm_g.astype(f32).reshape(MLSTM_HEADS, 1, MLSTM_HEAD_DIM)
    h = h.transpose(0, 2, 1, 3).reshape(bsz, seq, MLSTM_WIDTH)
    h = h * jax.nn.sigmoid(o.astype(f32))
    return h.astype(x.dtype) @ w_out


def conv_ffn(x, w_up, conv_w, conv_b, w_down):
    a, g = jnp.split(x @ w_up, 2, axis=-1)
    h = jax.nn.gelu(dwconv3(a, conv_w, conv_b)) * g
    return h @ w_down


def setup_inputs(seed: int = 0) -> dict:
    key = jax.random.key(seed)
    ks = jax.random.split(key, 26)
    f32 = jnp.float32
    nrm = lambda k, shape: jax.random.normal(k, shape, f32)
    beta = DEEPNORM_BETA
    even_col_scale = jnp.concatenate([jnp.ones((ATTN_WIDTH + KV_WIDTH,), f32),
                                      jnp.full((KV_WIDTH,), beta, f32),
                                      jnp.ones((S5_WIDTH,), f32)])
    odd_col_scale = jnp.concatenate([jnp.ones((2 * MLSTM_WIDTH,), f32),
                                     jnp.full((MLSTM_WIDTH,), beta, f32),
                                     jnp.ones((MLSTM_WIDTH + 4 * MLSTM_HEADS,), f32)])
    fbias = jnp.linspace(3.0, 6.0, MLSTM_HEADS, dtype=f32)
    zh = jnp.zeros((MLSTM_HEADS,), f32)
    gate_base = jnp.concatenate([zh, fbias, zh, fbias])
    sp = (N_EVEN, 2, S5_GROUPS)
    return {
        'x': nrm(ks[0], (BATCH, SEQ, D_MODEL)),
        'ev_w_in': nrm(ks[1], (N_EVEN, D_MODEL, EVEN_IN)) * D_MODEL ** -0.5 * even_col_scale,
        'ev_sink': 0.5 * nrm(ks[2], (N_EVEN, ATTN_HEADS)),
        's5_a_re': -0.5 + 0.01 * nrm(ks[3], sp + (S5_STATE,)),
        's5_a_im': jnp.pi * jnp.arange(S5_STATE, dtype=f32) + 0.01 * nrm(ks[4], sp + (S5_STATE,)),
        's5_log_dt': jax.random.uniform(ks[5], sp, f32, math.log(1e-3), math.log(1e-1)),
        's5_b_re': nrm(ks[6], sp + (S5_STATE, S5_GROUP)) * (2 * S5_GROUP) ** -0.5,
        's5_b_im': nrm(ks[7], sp + (S5_STATE, S5_GROUP)) * (2 * S5_GROUP) ** -0.5,
        's5_c_re': nrm(ks[8], sp + (S5_GROUP, S5_STATE)) * S5_STATE ** -0.5,
        's5_c_im': nrm(ks[9], sp + (S5_GROUP, S5_STATE)) * S5_STATE ** -0.5,
        's5_d': nrm(ks[10], (N_EVEN, S5_WIDTH)),
        's5_w_glu': nrm(ks[11], (N_EVEN, S5_WIDTH, S5_WIDTH)) * S5_WIDTH ** -0.5,
        'ev_w_out': nrm(ks[12], (N_EVEN, EVEN_MIX, D_MODEL)) * EVEN_MIX ** -0.5 * beta,
        'od_w_in': nrm(ks[13], (N_ODD, D_MODEL, ODD_IN)) * D_MODEL ** -0.5 * odd_col_scale,
        'od_gate_b': gate_base + 0.1 * nrm(ks[14], (N_ODD, 4 * MLSTM_HEADS)),
        'od_conv_w': nrm(ks[15], (N_ODD, CONV_WIDTH, 2 * MLSTM_WIDTH)) * CONV_WIDTH ** -0.5,
        'od_conv_b': 0.02 * nrm(ks[16], (N_ODD, 2 * MLSTM_WIDTH)),
        'od_norm_g': 1.0 + 0.02 * nrm(ks[17], (N_ODD, MLSTM_WIDTH)),
        'od_w_out': nrm(ks[18], (N_ODD, MLSTM_WIDTH, D_MODEL)) * MLSTM_WIDTH ** -0.5 * beta,
        'ffn_w_up': nrm(ks[19], (DEPTH, D_MODEL, 2 * D_FF)) * D_MODEL ** -0.5,
        'ffn_conv_w': nrm(ks[20], (DEPTH, CONV_WIDTH, D_FF)) * CONV_WIDTH ** -0.5,
        'ffn_conv_b': 0.02 * nrm(ks[21], (DEPTH, D_FF)),
        'ffn_w_down': nrm(ks[22], (DEPTH, D_FF, D_MODEL)) * D_FF ** -0.5 * beta,
        'ln_g': 1.0 + 0.02 * nrm(ks[23], (DEPTH, 2, D_MODEL)),
        'ln_b': 0.02 * nrm(ks[24], (DEPTH, 2, D_MODEL)),
    }


def reference(x, ev_w_in, ev_sink, s5_a_re, s5_a_im, s5_log_dt, s5_b_re, s5_b_im, s5_c_re, s5_c_im,
              s5_d, s5_w_glu, ev_w_out, od_w_in, od_gate_b, od_conv_w, od_conv_b, od_norm_g, od_w_out,
              ffn_w_up, ffn_conv_w, ffn_conv_b, ffn_w_down, ln_g, ln_b):
    for layer in range(DEPTH):
        j = layer // 2
        if layer % 2 == 0:
            y = attn_s5_mixer(x, ev_w_in[j], ev_sink[j], s5_a_re[j], s5_a_im[j], s5_log_dt[j],
                              s5_b_re[j], s5_b_im[j], s5_c_re[j], s5_c_im[j], s5_d[j], s5_w_glu[j],
                              ev_w_out[j])
        else:
            y = mlstm_mixer(x, od_w_in[j], od_gate_b[j], od_conv_w[j], od_conv_b[j], od_norm_g[j],
                            od_w_out[j])
        x = layer_norm(DEEPNORM_ALPHA * x + y, ln_g[layer, 0], ln_b[layer, 0])
        f = conv_ffn(x, ffn_w_up[layer], ffn_conv_w[layer], ffn_conv_b[layer], ffn_w_down[layer])
        x = layer_norm(DEEPNORM_ALPHA * x + f, ln_g[layer, 1], ln_b[layer, 1])
    return x
```

```python
import math
from contextlib import ExitStack, contextmanager
import numpy as np
import ml_dtypes
import concourse.bass as bass
import concourse.mybir as mybir
from concourse.bass_utils import run_bass_kernel_spmd

F32 = mybir.dt.float32
BF16 = mybir.dt.bfloat16
AF = mybir.ActivationFunctionType
ALU = mybir.AluOpType

ENGS = ("pe", "act", "dve", "pool", "sp")
DQ = ("sp", "act", "pool")

D = 1024
L = 2048
NB = 16
NTT = 4
DEPTH = 4
DFF = 2816
NJ = 22
ALPHA = (2 * DEPTH) ** 0.25
EPS = 1e-5
NEG = -30000.0


class Prog:
    NSLOT = 8

    def __init__(self, nc):
        self.nc = nc
        self.es = ExitStack()
        self.ops = {e: [] for e in ENGS}
        self.sem = {"s_" + e: self.es.enter_context(nc.semaphore("s_" + e)) for e in ENGS}
        self.cnt = {e: 0 for e in ENGS}
        self.dcnt = {}
        for q in DQ:
            for i in range(self.NSLOT):
                n = "d_%s%d" % (q, i)
                self.sem[n] = self.es.enter_context(nc.semaphore(n))
                self.dcnt[n] = 0
        self.dnext = {q: 0 for q in DQ}
        self.seen = {e: {} for e in ENGS}
        self.lastw = {}
        self.readers = {}
        self.stack = [self.es]
        self.uid = 0

    def sb(self, name, shape, dt=F32):
        self.uid += 1
        return self.stack[-1].enter_context(self.nc.sbuf_tensor("%s_%d" % (name, self.uid), list(shape), dt))

    def ps(self, name, shape, dt=F32):
        self.uid += 1
        return self.stack[-1].enter_context(self.nc.psum_tensor("%s_%d" % (name, self.uid), list(shape), dt))

    @contextmanager
    def phase(self):
        es = ExitStack()
        self.stack.append(es)
        try:
            yield
        finally:
            self.barrier()
            self.flush()
            self.stack.pop()
            es.close()

    def _need(self, eng, tok, waits):
        if tok is None:
            return
        sname, val, teng = tok
        if teng == eng and eng == "pe":
            return
        if self.seen[eng].get(sname, 0) >= val:
            return
        self.seen[eng][sname] = val
        waits.append((sname, val))

    def _deps(self, eng, reads, writes):
        waits = []
        for k in reads:
            self._need(eng, self.lastw.get(k), waits)
        for k in writes:
            self._need(eng, self.lastw.get(k), waits)
            for t in self.readers.get(k, ()):
                self._need(eng, t, waits)
        return waits

    def _commit(self, tok, reads, writes):
        for k in reads:
            self.readers.setdefault(k, []).append(tok)
        for k in writes:
            self.lastw[k] = tok
            self.readers[k] = []

    @staticmethod
    def _isps(k):
        k0 = k[0] if isinstance(k, tuple) else k
        return isinstance(k0, str) and k0.startswith("ps")

    def op(self, eng, fn, reads=(), writes=()):
        psr = [k for k in reads if self._isps(k)]
        if psr:
            reads = [k for k in reads if not self._isps(k)]
            writes = list(writes) + psr
        waits = self._deps(eng, reads, writes)
        self.cnt[eng] += 1
        tok = ("s_" + eng, self.cnt[eng], eng)
        self.ops[eng].append((waits, fn, ("s_" + eng, 1)))
        self._commit(tok, reads, writes)
        return tok

    def dma(self, q, out, in_, reads=(), writes=(), **kw):
        waits = self._deps(q, reads, writes)
        s = self.dnext[q]
        self.dnext[q] = (s + 1) % self.NSLOT
        sname = "d_%s%d" % (q, s)
        prev = self.dcnt[sname]
        if prev > 0 and self.seen[q].get(sname, 0) < prev:
            self.seen[q][sname] = prev
            waits.append((sname, prev))
        self.dcnt[sname] = prev + 16
        tok = (sname, prev + 16, "dma_" + q)
        self.ops[q].append((waits, lambda e: e.dma_start(out=out, in_=in_, **kw), (sname, 16)))
        self._commit(tok, reads, writes)
        return tok

    def barrier(self):
        for e in ENGS:
            waits = []
            for n, v in self.dcnt.items():
                if v > 0 and self.seen[e].get(n, 0) < v:
                    self.seen[e][n] = v
                    waits.append((n, v))
            for e2 in ENGS:
                n = "s_" + e2
                v = self.cnt[e2]
                if e2 != e and v > 0 and self.seen[e].get(n, 0) < v:
                    self.seen[e][n] = v
                    waits.append((n, v))
            if waits:
                self.ops[e].append((waits, None, None))
        self.lastw = {}
        self.readers = {}

    def flush(self):
        nc = self.nc
        handles = {"pe": "tensor", "act": "scalar", "dve": "vector", "pool": "gpsimd", "sp": "sync"}
        if not any(self.ops[e] for e in ENGS):
            return
        with nc.Block() as block:
            for e in ENGS:
                lst = self.ops[e]
                if not lst:
                    continue

                def body(eng, lst=lst):
                    for waits, fn, inc in lst:
                        for sname, val in waits:
                            eng.wait_ge(self.sem[sname], val)
                        if fn is not None:
                            fn(eng).then_inc(self.sem[inc[0]], inc[1])
                getattr(block, handles[e])(body)
        self.ops = {e: [] for e in ENGS}

    def finish(self):
        self.barrier()
        self.flush()
        self.es.close()


def host_consts():
    c = {}
    c["ident"] = np.eye(128, dtype=np.float32)
    s = np.arange(128)[:, None]
    t = np.arange(512)[None, :]
    mf = np.stack([(t >= jj * 128 + s) for jj in range(4)], 1).astype(np.float32)
    mb = np.stack([(t <= jj * 128 + s) for jj in range(4)], 1).astype(np.float32)
    s1 = np.arange(128)[:, None]
    t1 = np.arange(128)[None, :]
    c["tri"] = np.stack([(t1 >= s1), (t1 <= s1)], 1).astype(np.float32)
    slopes = 2.0 ** (-8.0 * np.arange(1, 9) / 8)
    kk = np.arange(128)[:, None]
    qq = np.arange(128)[None, :]
    ab = np.zeros((128, 8, 3, 128), np.float32)
    for h in range(8):
        for r in range(3):
            dist = np.abs(qq - (kk + (r - 1) * 128))
            ab[:, h, r, :] = np.where(dist <= 128, -slopes[h] * dist * 8.0, NEG)
    c["abias"] = ab
    sel = np.zeros((40, 8, 128), np.float32)
    for h in range(8):
        sel[h, h, :] = 1.0
        sel[32 + h, h, :] = 1.0
    c["sel"] = sel
    return c


CONST_SHAPES = {"ident": [128, 128], "tri": [128, 2, 128],
                "abias": [128, 8, 3, 128], "sel": [40, 8, 128]}

WEIGHT_SHAPES = {
    "ev_w_in": [2, 1024, 1280], "ev_sink": [2, 8], "s5_a_re": [2, 2, 32, 64], "s5_a_im": [2, 2, 32, 64],
    "s5_log_dt": [2, 2, 32], "s5_b_re": [2, 2, 32, 64, 16], "s5_b_im": [2, 2, 32, 64, 16],
    "s5_c_re": [2, 2, 32, 16, 64], "s5_c_im": [2, 2, 32, 16, 64], "s5_d": [2, 512],
    "s5_w_glu": [2, 512, 512], "ev_w_out": [2, 1024, 1024], "od_w_in": [2, 1024, 4128],
    "od_gate_b": [2, 32], "od_conv_w": [2, 3, 2048], "od_conv_b": [2, 2048], "od_norm_g": [2, 1024],
    "od_w_out": [2, 1024, 1024], "ffn_w_up": [4, 1024, 5632], "ffn_conv_w": [4, 3, 2816],
    "ffn_conv_b": [4, 2816], "ffn_w_down": [4, 2816, 1024], "ln_g": [4, 2, 1024], "ln_b": [4, 2, 1024],
}


class Ctx:
    pass


def kc_view(w2d):
    return w2d.rearrange("(kc k) m -> k kc m", k=128)


def emit_const_setup(C):
    P = C.P
    W = C.W
    C.ident = P.sb("ident", [128, 128])
    C.identb = P.sb("identb", [128, 128], BF16)
    C.onesf = P.sb("onesf", [128, 128])
    C.ones1k = P.sb("ones1k", [128, 128])
    C.onesb = P.sb("onesb", [128, 128], BF16)
    C.sel = P.sb("sel", [40, 8, 128])
    C.one8 = P.sb("one8", [128, 1])
    P.dma("sp", C.ident[:], C.consts["ident"], writes=["ident"])
    P.dma("sp", C.sel[:], C.consts["sel"], writes=["sel"])
    P.op("dve", lambda e: e.tensor_copy(out=C.identb[:], in_=C.ident[:]), reads=["ident"], writes=["identb"])
    P.op("pool", lambda e: e.memset(C.onesf[:], 1.0 / 128), writes=["onesf"])
    P.op("pool", lambda e: e.memset(C.ones1k[:], 1.0 / 1024), writes=["ones1k"])
    P.op("pool", lambda e: e.memset(C.onesb[:], 1.0), writes=["onesb"])
    P.op("pool", lambda e: e.memset(C.one8[:], 1.0), writes=["one8"])
    C.epsc = P.sb("epsc", [128, 1])
    P.op("pool", lambda e: e.memset(C.epsc[:], EPS), writes=["epsc"])
    C.lng = P.sb("lng", [128, 4, 2, 8])
    C.lnb = P.sb("lnb", [128, 4, 2, 8])
    import os
    for l in range(4 if os.environ.get("KNOLN") is None else 0):
        for j in range(2):
            P.dma("sp", C.lng[:, l, j, :], W["ln_g"][l, j].rearrange("(c p) -> p c", p=128), writes=["lng"],
                  allow_slow_non_contiguous=True)
            P.dma("act", C.lnb[:, l, j, :], W["ln_b"][l, j].rearrange("(c p) -> p c", p=128), writes=["lnb"],
                  allow_slow_non_contiguous=True)


def load_x_tm(C, x_seq):
    P = C.P
    with P.phase():
        stg = [P.sb("xstg", [128, 1024]) for _ in range(2)]
        pt = [P.ps("xps", [128, 4, 128]) for _ in range(2)]
        for nb in range(NB):
            s = stg[nb % 2]
            P.dma("sp" if nb % 2 == 0 else "act", s[:], x_seq[nb * 128:(nb + 1) * 128, :], writes=[("xstg", nb % 2)])
            for half in range(2):
                pp = pt[half]
                for i in range(4):
                    c = half * 4 + i
                    P.op("pe", lambda e, pp=pp, i=i, c=c, s=s: e.transpose(pp[:, i, :], s[:, c * 128:(c + 1) * 128], C.ident[:]),
                         reads=[("xstg", nb % 2), "ident"], writes=[("psxp", half)])
                P.op("act", lambda e, pp=pp, half=half, nb=nb: e.activation(
                    out=C.xres[:, half * 4:half * 4 + 4, nb * 128:(nb + 1) * 128], in_=pp[:], func=AF.Copy),
                    reads=[("psxp", half)], writes=[("xres", nb, half)])
                P.op("dve", lambda e, pp=pp, half=half, nb=nb: e.tensor_copy(
                    out=C.xbf[:, half * 4:half * 4 + 4, nb * 128:(nb + 1) * 128], in_=pp[:]),
                    reads=[("psxp", half)], writes=[("xbf", nb, half)])


def load_x_fm(C, xT_seq):
    P = C.P
    with P.phase():
        for c in range(8):
            P.dma("sp" if c % 2 == 0 else "act", C.xres[:, c, :], xT_seq[c * 128:(c + 1) * 128, :], writes=[("xres", c)])
            P.op("dve" if c % 2 == 0 else "pool", lambda e, c=c: e.tensor_copy(out=C.xbf[:, c, :], in_=C.xres[:, c, :]),
                 reads=[("xres", c)], writes=[("xbf", c)])


def store_x_fm(C, xT_seq):
    P = C.P
    with P.phase():
        for c in range(8):
            P.dma("sp" if c % 2 == 0 else "act", xT_seq[c * 128:(c + 1) * 128, :], C.xres[:, c, :], reads=[("xres", c)])


def store_x_tm(C, y_seq):
    P = C.P
    with P.phase():
        stg = [P.sb("ystg", [128, 1024]) for _ in range(2)]
        pt = [P.ps("yps", [128, 4, 128]) for _ in range(2)]
        for nb in range(NB):
            s = stg[nb % 2]
            for half in range(2):
                pp = pt[half]
                for i in range(4):
                    c = half * 4 + i
                    P.op("pe", lambda e, pp=pp, i=i, c=c, nb=nb: e.transpose(pp[:, i, :], C.xres[:, c, nb * 128:(nb + 1) * 128], C.ident[:]),
                         reads=["ident"], writes=[("psyp", half)])
                eng = "act" if half == 0 else "dve"
                if eng == "act":
                    P.op("act", lambda e, pp=pp, s=s, half=half: e.activation(out=s[:, half * 512:(half + 1) * 512], in_=pp[:].rearrange("p a b -> p (a b)"), func=AF.Copy),
                         reads=[("psyp", half)], writes=[("ystg", nb % 2, half)])
                else:
                    P.op("dve", lambda e, pp=pp, s=s, half=half: e.tensor_copy(out=s[:, half * 512:(half + 1) * 512], in_=pp[:].rearrange("p a b -> p (a b)")),
                         reads=[("psyp", half)], writes=[("ystg", nb % 2, half)])
            P.dma("sp" if nb % 2 == 0 else "act", y_seq[nb * 128:(nb + 1) * 128, :], s[:],
                  reads=[("ystg", nb % 2, 0), ("ystg", nb % 2, 1)])


def emit_ln(C, l, j, tt, T):
    P = C.P
    ts = slice(tt * 512, (tt + 1) * 512)
    rk = [("xres", c, tt) for c in range(8)]
    for c in range(8):
        P.op("pe", lambda e, c=c: e.matmul(T["psm"][:], lhsT=C.ones1k[:], rhs=C.xres[:, c, ts], start=(c == 0), stop=(c == 7)),
             reads=[rk[c], "ones1k"], writes=["psm"])
    for c in range(8):
        sq = T["sq"][c % 2]
        P.op("act", lambda e, c=c, sq=sq: e.activation(out=sq[:], in_=C.xres[:, c, ts], func=AF.Square),
             reads=[rk[c]], writes=[("sq", c % 2)])
        P.op("pe", lambda e, c=c, sq=sq: e.matmul(T["psv"][:], lhsT=C.ones1k[:], rhs=sq[:], start=(c == 0), stop=(c == 7)),
             reads=[("sq", c % 2), "ones1k"], writes=["psv"])
    mean, rstd = T["mean"], T["rstd"]
    P.op("act", lambda e: e.activation(out=mean[:], in_=T["psm"][:], func=AF.Copy), reads=["psm"], writes=["mean"])
    P.op("dve", lambda e: e.tensor_tensor(out=rstd[:], in0=mean[:], in1=mean[:], op=ALU.mult), reads=["mean"], writes=["rstd"])
    P.op("dve", lambda e: e.tensor_tensor(out=rstd[:], in0=T["psv"][:], in1=rstd[:], op=ALU.subtract), reads=["psv", "rstd"], writes=["rstd"])
    P.op("act", lambda e: e.activation(out=rstd[:], in_=rstd[:], func=AF.Sqrt, bias=C.epsc[:, 0:1], scale=1.0), reads=["rstd", "epsc"], writes=["rstd"])
    P.op("dve", lambda e: e.reciprocal(out=rstd[:], in_=rstd[:]), reads=["rstd"], writes=["rstd"])
    for c in range(8):
        tmp = T["tmp"][c % 2]
        P.op("pool", lambda e, c=c, tmp=tmp: e.tensor_tensor(out=tmp[:], in0=C.xres[:, c, ts], in1=mean[:], op=ALU.subtract),
             reads=[rk[c], "mean"], writes=[("lntmp", c % 2)])
        P.op("dve", lambda e, c=c, tmp=tmp: e.tensor_tensor(out=tmp[:], in0=tmp[:], in1=rstd[:], op=ALU.mult),
             reads=[("lntmp", c % 2), "rstd"], writes=[("lntmp", c % 2)])
        P.op("act", lambda e, c=c, tmp=tmp: e.activation(out=C.xres[:, c, ts], in_=tmp[:], func=AF.Identity,
                                                         bias=C.lnb[:, l, j, c:c + 1], scale=C.lng[:, l, j, c:c + 1]),
             reads=[("lntmp", c % 2), "lng", "lnb"], writes=[rk[c]])
        P.op("pool", lambda e, c=c: e.tensor_copy(out=C.xbf[:, c, ts], in_=C.xres[:, c, ts]),
             reads=[rk[c]], writes=[("xbf", c, tt)])


def ln_temps(P):
    return {"sq": [P.sb("lnsq", [128, 512]) for _ in range(2)], "tmp": [P.sb("lntmp", [128, 512]) for _ in range(2)],
            "mean": P.sb("lnmean", [128, 512]), "rstd": P.sb("lnrstd", [128, 512]),
            "psm": P.ps("psm", [128, 512]), "psv": P.ps("psv", [128, 512])}


def emit_conv3(C, P, asb, out, wv, bv, ncol, key_in, key_out):
    P.op("dve", lambda e: e.tensor_scalar(out=out, in0=asb[:, 1:ncol + 1], scalar1=wv[:, 1:2], scalar2=bv, op0=ALU.mult, op1=ALU.add),
         reads=[key_in], writes=[key_out])
    P.op("dve", lambda e: e.scalar_tensor_tensor(out=out, in0=asb[:, 0:ncol], scalar=wv[:, 0:1], in1=out, op0=ALU.mult, op1=ALU.add),
         reads=[key_in, key_out], writes=[key_out])
    P.op("dve", lambda e: e.scalar_tensor_tensor(out=out, in0=asb[:, 2:ncol + 2], scalar=wv[:, 2:3], in1=out, op0=ALU.mult, op1=ALU.add),
         reads=[key_in, key_out], writes=[key_out])


def emit_ffn(C, l):
    P = C.P
    W = C.W
    wup = kc_view(W["ffn_w_up"][l])
    wdn = W["ffn_w_down"][l].rearrange("(j p) m -> p j m", p=128)
    HL = 1024
    with P.phase():
        cw = P.sb("fcw", [128, NJ, 3])
        cb = P.sb("fcb", [128, NJ])
        for k in range(3):
            P.dma("sp", cw[:, :, k], W["ffn_conv_w"][l, k].rearrange("(j p) -> p j", p=128), writes=["fcw"], allow_slow_non_contiguous=True)
        P.dma("act", cb[:], W["ffn_conv_b"][l].rearrange("(j p) -> p j", p=128), writes=["fcw"], allow_slow_non_contiguous=True)
        hT = P.sb("hT", [128, NJ, HL], BF16)
        wa = [P.sb("wa", [128, 8, 128], BF16) for _ in range(2)]
        wg = [P.sb("wg", [128, 8, 128], BF16) for _ in range(2)]
        wd = [P.sb("wd", [128, NJ, 128], BF16) for _ in range(2)]
        asb = [P.sb("asb", [128, HL + 2]) for _ in range(2)]
        cv = [P.sb("cv", [128, HL]) for _ in range(2)]
        psa = [P.ps("psa", [128, 512]) for _ in range(2)]
        psg = [P.ps("psg", [128, 512]) for _ in range(2)]
        psh = P.ps("psh", [128, 512])
        psd = [P.ps("psd", [128, 512]) for _ in range(1)]
        T = {"sq": [cv[0][:, 0:512], cv[0][:, 512:1024]], "tmp": [cv[1][:, 0:512], cv[1][:, 512:1024]],
             "mean": asb[0][:, 0:512], "rstd": asb[0][:, 512:1024], "psm": psa[0], "psv": psa[1]}
        T = {k: (v if isinstance(v, list) else v) for k, v in T.items()}
        it = 0
        for half in range(2):
            t0 = half * HL
            for j in range(NJ):
                b = it % 2
                it += 1
                P.dma("pool", wa[b][:], wup[:, :, j * 128:(j + 1) * 128], writes=[("wa", b)])
                P.dma("pool", wg[b][:], wup[:, :, DFF + j * 128:DFF + (j + 1) * 128], writes=[("wg", b)])
                A = asb[b]
                hcols = []
                if half == 0:
                    P.op("pool", lambda e, A=A: e.memset(A[:, 0:1], 0.0), writes=[("asbh0", b)])
                    hcols.append((HL + 1, t0 + HL))
                else:
                    P.op("pool", lambda e, A=A: e.memset(A[:, HL + 1:HL + 2], 0.0), writes=[("asbh1", b)])
                    hcols.append((0, t0 - 1))
                for (dst, tok) in hcols:
                    for kc in range(8):
                        P.op("pe", lambda e, kc=kc, tok=tok, b=b: e.matmul(psh[:, 0:1], lhsT=wa[b][:, kc, :], rhs=C.xbf[:, kc, tok:tok + 1], start=(kc == 0), stop=(kc == 7)),
                             reads=[("wa", b), ("xbf", kc, tok // 512)], writes=["psh"])
                    P.op("act", lambda e, A=A, dst=dst: e.activation(out=A[:, dst:dst + 1], in_=psh[:, 0:1], func=AF.Copy),
                         reads=["psh"], writes=[("asbh%d" % (1 if dst > 0 else 0), b)])
                for q in range(2):
                    tt = half * 2 + q
                    for kc in range(8):
                        P.op("pe", lambda e, kc=kc, tt=tt, b=b, q=q: e.matmul(psa[q][:], lhsT=wa[b][:, kc, :], rhs=C.xbf[:, kc, tt * 512:(tt + 1) * 512], start=(kc == 0), stop=(kc == 7)),
                             reads=[("wa", b), ("xbf", kc, tt)], writes=[("psa", q)])
                    P.op("act", lambda e, A=A, q=q: e.activation(out=A[:, 1 + q * 512:1 + (q + 1) * 512], in_=psa[q][:], func=AF.Copy),
                         reads=[("psa", q)], writes=[("asb", b, q)])
                    for kc in range(8):
                        P.op("pe", lambda e, kc=kc, tt=tt, b=b, q=q: e.matmul(psg[q][:], lhsT=wg[b][:, kc, :], rhs=C.xbf[:, kc, tt * 512:(tt + 1) * 512], start=(kc == 0), stop=(kc == 7)),
                             reads=[("wg", b), ("xbf", kc, tt)], writes=[("psg", q)])
                kin = ("asball", b)
                P.op("dve", lambda e, b=b, j=j: e.tensor_scalar(out=cv[b][:], in0=asb[b][:, 1:HL + 1], scalar1=cw[:, j, 1:2], scalar2=cb[:, j:j + 1], op0=ALU.mult, op1=ALU.add),
                     reads=[("asb", b, 0), ("asb", b, 1), "fcw"], writes=[("cv", b)])
                P.op("dve", lambda e, b=b, j=j: e.scalar_tensor_tensor(out=cv[b][:], in0=asb[b][:, 0:HL], scalar=cw[:, j, 0:1], in1=cv[b][:], op0=ALU.mult, op1=ALU.add),
                     reads=[("asb", b, 0), ("asb", b, 1), ("asbh0", b), ("cv", b)], writes=[("cv", b)])
                P.op("dve", lambda e, b=b, j=j: e.scalar_tensor_tensor(out=cv[b][:], in0=asb[b][:, 2:HL + 2], scalar=cw[:, j, 2:3], in1=cv[b][:], op0=ALU.mult, op1=ALU.add),
                     reads=[("asb", b, 0), ("asb", b, 1), ("asbh1", b), ("cv", b)], writes=[("cv", b)])
                P.op("act", lambda e, b=b: e.activation(out=cv[b][:], in_=cv[b][:], func=AF.Gelu_apprx_tanh),
                     reads=[("cv", b)], writes=[("cv", b)])
                for q in range(2):
                    P.op("dve", lambda e, b=b, q=q, j=j: e.tensor_tensor(out=hT[:, j, q * 512:(q + 1) * 512], in0=cv[b][:, q * 512:(q + 1) * 512], in1=psg[q][:], op=ALU.mult),
                         reads=[("cv", b), ("psg", q)], writes=[("hT", j)])
            for m in range(8):
                b = m % 2
                P.dma("pool", wd[b][:], wdn[:, :, m * 128:(m + 1) * 128], writes=[("wd", b)])
                for q in range(2):
                    tt = half * 2 + q
                    for j in range(NJ):
                        P.op("pe", lambda e, j=j, b=b, q=q: e.matmul(psd[0][:], lhsT=wd[b][:, j, :], rhs=hT[:, j, q * 512:(q + 1) * 512], start=(j == 0), stop=(j == NJ - 1)),
                             reads=[("wd", b), ("hT", j)], writes=["psd"])
                    P.op("dve", lambda e, m=m, tt=tt: e.scalar_tensor_tensor(out=C.xres[:, m, tt * 512:(tt + 1) * 512], in0=C.xres[:, m, tt * 512:(tt + 1) * 512], scalar=ALPHA, in1=psd[0][:], op0=ALU.mult, op1=ALU.add),
                         reads=["psd", ("xres", m, tt)], writes=[("xres", m, tt)])
        P.barrier()
        for tt in range(NTT):
            emit_ln(C, l, 1, tt, T)


def emit_mixer_out(C, l, yT, wout_dram, nk, T):
    P = C.P
    wv = wout_dram.rearrange("(kc k) m -> k kc m", k=128)
    wo = [P.sb("wo", [128, nk, 128], BF16) for _ in range(2)]
    pso = [P.ps("pso", [128, 512]) for _ in range(2)]
    for m in range(8):
        b = m % 2
        P.dma("pool", wo[b][:], wv[:, :, m * 128:(m + 1) * 128], writes=[("wo", b)])
        for tt in range(NTT):
            pb = pso[tt % 2]
            for k in range(nk):
                P.op("pe", lambda e, k=k, b=b, tt=tt, pb=pb: e.matmul(pb[:], lhsT=wo[b][:, k, :], rhs=yT[:, k, tt * 512:(tt + 1) * 512], start=(k == 0), stop=(k == nk - 1)),
                     reads=[("wo", b), ("yT", k)], writes=[("pso", tt % 2)])
            P.op("dve", lambda e, m=m, tt=tt, pb=pb: e.scalar_tensor_tensor(out=C.xres[:, m, tt * 512:(tt + 1) * 512], in0=C.xres[:, m, tt * 512:(tt + 1) * 512], scalar=ALPHA, in1=pb[:], op0=ALU.mult, op1=ALU.add),
                 reads=[("pso", tt % 2), ("xres", m, tt)], writes=[("xres", m, tt)])
    for tt in range(NTT):
        emit_ln(C, l, 0, tt, T)


def emit_mlstm(C, l):
    P = C.P
    W = C.W
    j = l // 2
    win = kc_view(W["od_w_in"][j])
    SCALE = 128.0 ** -0.5
    RB = (0, 32)
    with P.phase():
        hTf = P.sb("hTfin", [128, 8, L], BF16)
        with P.phase():
            gb = P.sb("gb", [40, 2])
            for d in range(2):
                P.dma("sp", gb[RB[d]:RB[d] + 8, :], W["od_gate_b"][j][16 * d:16 * d + 16].rearrange("(q h) -> h q", h=8), writes=["gb"], allow_slow_non_contiguous=True)
            cwq = P.sb("mcw", [128, 16, 3])
            cbq = P.sb("mcb", [128, 16])
            for k in range(3):
                P.dma("sp", cwq[:, :, k], W["od_conv_w"][j, k].rearrange("(c p) -> p c", p=128), writes=["mcw"], allow_slow_non_contiguous=True)
            P.dma("act", cbq[:], W["od_conv_b"][j].rearrange("(c p) -> p c", p=128), writes=["mcw"], allow_slow_non_contiguous=True)
            ng = P.sb("ng", [128, 8])
            P.dma("sp", ng[:], W["od_norm_g"][j].rearrange("(c p) -> p c", p=128), writes=["ng"], allow_slow_non_contiguous=True)
            tri = P.sb("tri", [128, 2, 128], BF16)
            P.dma("pool", tri[:], C.consts["tri"], writes=["tri"])
            nmrel = P.sb("gnm", [40, L])
            negm = P.sb("gm", [40, L])
            colT = P.sb("colT", [128, 2, NB, 8])
            with P.phase():
                gw = P.sb("gw", [128, 8, 32], BF16)
                P.dma("pool", gw[:], win[:, :, 4096:4128], writes=["gw"])
                ci = P.sb("gci", [40, L])
                t1 = P.sb("gt1", [40, L])
                t2 = P.sb("gt2", [40, L])
                tot = P.sb("gtot", [40, 1])
                psg = [P.ps("psgate", [128, 512]) for _ in range(2)]
                pct = P.ps("pspct", [128, 2, NB, 16])
                n = 0
                for q in range(4):
                    d = q // 2
                    dst = ci if q % 2 == 0 else nmrel
                    for tt in range(NTT):
                        pb = psg[n % 2]
                        for kc in range(8):
                            P.op("pe", lambda e, kc=kc, q=q, tt=tt, pb=pb, d=d: e.matmul(pb[RB[d]:RB[d] + 8, :], lhsT=gw[:, kc, q * 8:(q + 1) * 8], rhs=C.xbf[:, kc, tt * 512:(tt + 1) * 512], start=(kc == 0), stop=(kc == 7)),
                                 reads=["gw", ("xbf", kc, tt)], writes=[("psgate", n % 2)])
                        P.op("act", lambda e, q=q, tt=tt, pb=pb, d=d, dst=dst: e.activation(out=dst[RB[d]:RB[d] + 8, tt * 512:(tt + 1) * 512], in_=pb[RB[d]:RB[d] + 8, :], func=AF.Identity, bias=gb[RB[d]:RB[d] + 8, (q % 2):(q % 2) + 1], scale=1.0),
                             reads=[("psgate", n % 2), "gb"], writes=[("gpre", q)])
                        n += 1
                for d in range(2):
                    r = slice(RB[d], RB[d] + 8)
                    kg, kf = ("gpre", 2 * d), ("gpre", 2 * d + 1)
                    K = lambda s, d=d: (s, d)
                    P.op("act", lambda e, r=r: e.activation(out=t1[r, :], in_=nmrel[r, :], func=AF.Exp, scale=-1.0), reads=[kf], writes=[K("t1")])
                    P.op("act", lambda e, r=r: e.activation(out=nmrel[r, :], in_=t1[r, :], func=AF.Ln, bias=1.0, scale=1.0), reads=[K("t1")], writes=[K("lf")])
                    P.op("dve", lambda e, r=r: e.tensor_tensor_scan(out=negm[r, :], data0=C.one8[r, 0:1].to_broadcast([8, L]), data1=nmrel[r, :], initial=0.0, op0=ALU.mult, op1=ALU.add),
                         reads=[K("lf"), "one8"], writes=[K("G")])
                    if d == 1:
                        P.op("dve", lambda e, r=r: e.tensor_copy(out=tot[r, :], in_=negm[r, L - 1:L]), reads=[K("G")], writes=[K("tot")])
                        P.op("dve", lambda e, r=r: e.tensor_tensor(out=t1[r, :], in0=nmrel[r, :], in1=negm[r, :], op=ALU.subtract), reads=[K("lf"), K("G"), K("t1")], writes=[K("t1")])
                        P.op("dve", lambda e, r=r: e.tensor_scalar(out=negm[r, :], in0=t1[r, :], scalar1=tot[r, 0:1], scalar2=None, op0=ALU.add), reads=[K("t1"), K("tot")], writes=[K("G")])
                    P.op("dve", lambda e, r=r: e.tensor_tensor(out=ci[r, :], in0=ci[r, :], in1=negm[r, :], op=ALU.add), reads=[kg, K("G")], writes=[K("c")])
                    if d == 0:
                        P.op("dve", lambda e, r=r: e.tensor_tensor_scan(out=t1[r, :], data0=C.one8[r, 0:1].to_broadcast([8, L]), data1=ci[r, :], initial=-1e30, op0=ALU.mult, op1=ALU.max),
                             reads=[K("c"), "one8", K("t1")], writes=[K("t1")])
                        cm, kcm = t1, K("t1")
                    else:
                        P.op("dve", lambda e, r=r: e.tensor_copy(out=t1[r, :], in_=ci[r, :]), reads=[K("c"), K("t1")], writes=[K("t1")])
                        src, dst, ks, kd = t1, t2, K("t1"), K("t2")
                        s = 1
                        while s < L:
                            P.op("dve", lambda e, src=src, dst=dst, s=s, r=r: e.tensor_tensor(out=dst[r, 0:L - s], in0=src[r, 0:L - s], in1=src[r, s:L], op=ALU.max), reads=[ks], writes=[kd])
                            P.op("dve", lambda e, src=src, dst=dst, s=s, r=r: e.tensor_copy(out=dst[r, L - s:L], in_=src[r, L - s:L]), reads=[ks], writes=[kd])
                            src, dst, ks, kd = dst, src, kd, ks
                            s *= 2
                        cm, kcm = src, ks
                    P.op("dve", lambda e, cm=cm, r=r: e.tensor_scalar(out=nmrel[r, :], in0=cm[r, :], scalar1=-1.0, scalar2=0.0, op0=ALU.mult, op1=ALU.min),
                         reads=[kcm, K("lf")], writes=[("gnm", d)])
                    P.op("dve", lambda e, r=r: e.tensor_tensor(out=negm[r, :], in0=negm[r, :], in1=nmrel[r, :], op=ALU.add), reads=[K("G"), ("gnm", d)], writes=[("gm", d)])
                    for nb in range(NB):
                        P.op("pe", lambda e, d=d, nb=nb, r=r: e.transpose(pct[:, d, nb, 0:8], ci[r, nb * 128:(nb + 1) * 128], C.ident[r, r]),
                             reads=[K("c"), "ident"], writes=["pspct"])
                P.op("dve", lambda e: e.tensor_copy(out=colT[:], in_=pct[:, :, :, 0:8]), reads=["pspct"], writes=["colT"])
            wb = [P.sb("wbuf", [128, 8, 128], BF16) for _ in range(2)]
            qT = P.sb("qT", [128, L], BF16)
            kT = P.sb("kT", [128, L], BF16)
            vt = P.sb("vt", [128, NB, 128], BF16)
            sgo = P.sb("sgo", [128, 512], BF16)
            bufA = P.sb("bufA", [128, L + 2])
            bufB = P.sb("bufB", [128, L])
            Dt = [P.sb("Dt", [128, 512], BF16) for _ in range(2)]
            Pt = [P.sb("Pt", [128, 512], BF16) for _ in range(2)]
            asb, cvt = bufA, bufB
            Rbc = [bufB[:, 0:512], bufB[:, 512:1024]]
            Mex = [bufB[:, 1024:1536], bufB[:, 1536:2048]]
            hsum, e1, e2 = bufA[:, 0:512], bufA[:, 512:1024], bufA[:, 1024:1536]
            pss = [P.ps("pss", [128, 512]) for _ in range(2)]
            psn = P.ps("psn", [128, 512])
            psdn = P.ps("psdn", [128, 512])
            psx = [P.ps("psx", [128, 512]) for _ in range(2)]
            nx = 0
            nw = 0
            for h in range(8):
                P.op("pool", lambda e: e.memset(asb[:, 0:1], 0.0), writes=["masbh"])
                P.op("pool", lambda e: e.memset(asb[:, L + 1:L + 2], 0.0), writes=["masbh"])
                for (col0, dstT, ci_, kd_) in ((h * 128, qT, h, "qT"), (1024 + h * 128, kT, 8 + h, "kT")):
                    wt = wb[nw % 2]
                    wkey = ("wbuf", nw % 2)
                    nw += 1
                    P.dma("pool", wt[:], win[:, :, col0:col0 + 128], writes=[wkey])
                    for tt in range(NTT):
                        pb = psx[nx % 2]
                        for kc in range(8):
                            P.op("pe", lambda e, kc=kc, tt=tt, pb=pb, wt=wt: e.matmul(pb[:], lhsT=wt[:, kc, :], rhs=C.xbf[:, kc, tt * 512:(tt + 1) * 512], start=(kc == 0), stop=(kc == 7)),
                                 reads=[wkey, ("xbf", kc, tt)], writes=[("psx", nx % 2)])
                        P.op("act", lambda e, tt=tt, pb=pb: e.activation(out=asb[:, 1 + tt * 512:1 + (tt + 1) * 512], in_=pb[:], func=AF.Copy),
                             reads=[("psx", nx % 2)], writes=["masb"])
                        nx += 1
                    P.op("dve", lambda e, ci_=ci_: e.tensor_scalar(out=cvt[:], in0=asb[:, 1:L + 1], scalar1=cwq[:, ci_, 1:2], scalar2=cbq[:, ci_:ci_ + 1], op0=ALU.mult, op1=ALU.add),
                         reads=["masb", "mcw"], writes=["mcv"])
                    P.op("dve", lambda e, ci_=ci_: e.scalar_tensor_tensor(out=cvt[:], in0=asb[:, 0:L], scalar=cwq[:, ci_, 0:1], in1=cvt[:], op0=ALU.mult, op1=ALU.add),
                         reads=["masb", "masbh", "mcv"], writes=["mcv"])
                    P.op("dve", lambda e, ci_=ci_: e.scalar_tensor_tensor(out=cvt[:], in0=asb[:, 2:L + 2], scalar=cwq[:, ci_, 2:3], in1=cvt[:], op0=ALU.mult, op1=ALU.add),
                         reads=["masb", "masbh", "mcv"], writes=["mcv"])
                    P.op("act", lambda e, dstT=dstT: e.activation(out=dstT[:], in_=cvt[:], func=AF.Silu), reads=["mcv"], writes=[kd_])
                wt = wb[nw % 2]
                wkey = ("wbuf", nw % 2)
                nw += 1
                P.dma("pool", wt[:], win[:, :, 2048 + h * 128:2048 + (h + 1) * 128], writes=[wkey])
                for nb in range(NB):
                    pb = psx[nx % 2]
                    for kc in range(8):
                        P.op("pe", lambda e, kc=kc, nb=nb, pb=pb, wt=wt: e.matmul(pb[:, 0:128], lhsT=C.xbf[:, kc, nb * 128:(nb + 1) * 128], rhs=wt[:, kc, :], start=(kc == 0), stop=(kc == 7)),
                             reads=[wkey, ("xbf", kc, nb // 4)], writes=[("psx", nx % 2)])
                    if nb % 2 == 0:
                        P.op("act", lambda e, nb=nb, pb=pb: e.activation(out=vt[:, nb, :], in_=pb[:, 0:128], func=AF.Copy), reads=[("psx", nx % 2)], writes=["vt"])
                    else:
                        P.op("dve", lambda e, nb=nb, pb=pb: e.tensor_copy(out=vt[:, nb, :], in_=pb[:, 0:128]), reads=[("psx", nx % 2)], writes=["vt"])
                    nx += 1
                wo = wb[nw % 2]
                wokey = ("wbuf", nw % 2)
                nw += 1
                P.dma("pool", wo[:], win[:, :, 3072 + h * 128:3072 + (h + 1) * 128], writes=[wokey])
                P.barrier()
                it = 0
                for tt in range(NTT):
                    ts = slice(tt * 512, (tt + 1) * 512)
                    for d in range(2):
                        r = slice(RB[d], RB[d] + 8)
                        pb = psx[nx % 2]
                        P.op("pe", lambda e, ts=ts, pb=pb, h=h, r=r: e.matmul(pb[:], lhsT=C.sel[r, h, :], rhs=nmrel[r, ts], start=True, stop=True),
                             reads=["sel", ("gnm", d)], writes=[("psx", nx % 2)])
                        P.op("dve", lambda e, d=d, pb=pb: e.tensor_copy(out=Rbc[d], in_=pb[:]), reads=[("psx", nx % 2)], writes=[("Rbc", d)])
                        nx += 1
                        pb = psx[nx % 2]
                        P.op("pe", lambda e, ts=ts, pb=pb, h=h, r=r: e.matmul(pb[:], lhsT=C.sel[r, h, :], rhs=negm[r, ts], start=True, stop=True),
                             reads=["sel", ("gm", d)], writes=[("psx", nx % 2)])
                        P.op("act", lambda e, d=d, pb=pb: e.activation(out=Mex[d], in_=pb[:], func=AF.Exp), reads=[("psx", nx % 2)], writes=[("Mex", d)])
                        nx += 1
                        if d == 0:
                            jl = list(range(0, 4 * tt + 4))
                        else:
                            jl = list(range(NB - 1, 4 * tt - 1, -1))
                        for ji, jb in enumerate(jl):
                            b = it % 2
                            it += 1
                            jj = jb - 4 * tt
                            if d == 0:
                                c0, c1 = (max(jj, 0) * 128, 512)
                                dc = c0 if 0 <= jj < 4 else None
                            else:
                                c0, c1 = (0, (min(jj, 3) + 1) * 128)
                                dc = c1 - 128 if 0 <= jj < 4 else None
                            cs = slice(c0, c1)
                            qs = slice(tt * 512 + c0, tt * 512 + c1)
                            P.op("pe", lambda e, jb=jb, b=b, cs=cs, qs=qs: e.matmul(pss[b][:, cs], lhsT=kT[:, jb * 128:(jb + 1) * 128], rhs=qT[:, qs], start=True, stop=True),
                                 reads=["qT", "kT"], writes=[("pss", b)])
                            P.op("act", lambda e, jb=jb, b=b, d=d, h=h, cs=cs: e.activation(out=Dt[b][:, cs], in_=Rbc[d][:, cs], func=AF.Exp, bias=colT[:, d, jb, h:h + 1], scale=1.0),
                                 reads=[("Rbc", d), "colT"], writes=[("Dt", b)])
                            if dc is not None:
                                P.op("pool", lambda e, b=b, d=d, dc=dc: e.tensor_tensor(out=Dt[b][:, dc:dc + 128], in0=Dt[b][:, dc:dc + 128], in1=tri[:, d, :], op=ALU.mult),
                                     reads=[("Dt", b), "tri"], writes=[("Dt", b)])
                            P.op("dve", lambda e, b=b, cs=cs: e.scalar_tensor_tensor(out=Pt[b][:, cs], in0=pss[b][:, cs], scalar=SCALE, in1=Dt[b][:, cs], op0=ALU.mult, op1=ALU.mult),
                                 reads=[("pss", b), ("Dt", b)], writes=[("Pt", b)])
                            P.op("pe", lambda e, jb=jb, b=b, ji=ji, jl=jl, cs=cs: e.matmul(psn[:, cs], lhsT=vt[:, jb, :], rhs=Pt[b][:, cs], start=(ji == 0), stop=(ji == len(jl) - 1)),
                                 reads=["vt", ("Pt", b)], writes=["psn"])
                            P.op("pe", lambda e, jb=jb, b=b, ji=ji, jl=jl, cs=cs: e.matmul(psdn[:, cs], lhsT=C.onesb[:], rhs=Pt[b][:, cs], start=(ji == 0), stop=(ji == len(jl) - 1)),
                                 reads=["onesb", ("Pt", b)], writes=["psdn"])
                        P.op("act", lambda e: e.activation(out=e1, in_=psdn[:], func=AF.Abs), reads=["psdn"], writes=["e1"])
                        P.op("dve", lambda e, d=d: e.tensor_tensor(out=e1, in0=e1, in1=Mex[d], op=ALU.max), reads=["e1", ("Mex", d)], writes=["e1"])
                        P.op("dve", lambda e: e.reciprocal(out=e1, in_=e1), reads=["e1"], writes=["e1"])
                        if d == 0:
                            P.op("dve", lambda e: e.tensor_tensor(out=hsum, in0=psn[:], in1=e1, op=ALU.mult), reads=["psn", "e1"], writes=["hsum"])
                        else:
                            P.op("dve", lambda e: e.tensor_tensor(out=e2, in0=psn[:], in1=e1, op=ALU.mult), reads=["psn", "e1"], writes=["e2"])
                            P.op("pool", lambda e: e.tensor_tensor(out=hsum, in0=hsum, in1=e2, op=ALU.add), reads=["e2", "hsum"], writes=["hsum"])
                    pb = psx[nx % 2]
                    for kc in range(8):
                        P.op("pe", lambda e, kc=kc, ts=ts, pb=pb, wo=wo: e.matmul(pb[:], lhsT=wo[:, kc, :], rhs=C.xbf[:, kc, ts], start=(kc == 0), stop=(kc == 7)),
                             reads=[wokey, ("xbf", kc, tt)], writes=[("psx", nx % 2)])
                    P.op("act", lambda e, pb=pb: e.activation(out=sgo[:], in_=pb[:], func=AF.Sigmoid), reads=[("psx", nx % 2)], writes=["sgo"])
                    nx += 1
                    P.op("pe", lambda e: e.matmul(psn[:], lhsT=C.onesf[:], rhs=hsum, start=True, stop=True), reads=["hsum", "onesf"], writes=["psn"])
                    P.op("act", lambda e: e.activation(out=e1, in_=hsum, func=AF.Square), reads=["hsum", "e1"], writes=["e1"])
                    P.op("pe", lambda e: e.matmul(psdn[:], lhsT=C.onesf[:], rhs=e1, start=True, stop=True), reads=["e1", "onesf"], writes=["psdn"])
                    P.op("act", lambda e: e.activation(out=e2, in_=psn[:], func=AF.Copy), reads=["psn", "e2"], writes=["e2"])
                    P.op("dve", lambda e: e.tensor_tensor(out=e1, in0=e2, in1=e2, op=ALU.mult), reads=["e2", "e1"], writes=["e1"])
                    P.op("dve", lambda e: e.tensor_tensor(out=e1, in0=psdn[:], in1=e1, op=ALU.subtract), reads=["psdn", "e1"], writes=["e1"])
                    P.op("act", lambda e: e.activation(out=e1, in_=e1, func=AF.Sqrt, bias=C.epsc[:, 0:1], scale=1.0), reads=["e1", "epsc"], writes=["e1"])
                    P.op("dve", lambda e: e.reciprocal(out=e1, in_=e1), reads=["e1"], writes=["e1"])
                    P.op("pool", lambda e: e.tensor_tensor(out=e2, in0=hsum, in1=e2, op=ALU.subtract), reads=["hsum", "e2"], writes=["e2"])
                    P.op("dve", lambda e: e.tensor_tensor(out=e2, in0=e2, in1=e1, op=ALU.mult), reads=["e1", "e2"], writes=["e2"])
                    P.op("dve", lambda e, ts=ts, h=h: e.scalar_tensor_tensor(out=hTf[:, h, ts], in0=e2, scalar=ng[:, h:h + 1], in1=sgo[:], op0=ALU.mult, op1=ALU.mult),
                         reads=["e2", "ng", "sgo"], writes=[("yT", h)])
                P.barrier()
        with P.phase():
            T = ln_temps(P)
            emit_mixer_out(C, l, hTf, W["od_w_out"][j], 8, T)


def emit_even(C, l):
    P = C.P
    W = C.W
    j = l // 2
    win = kc_view(W["ev_w_in"][j])
    PI = math.pi
    with P.phase():
        yA = P.sb("yA", [128, 4, L], BF16)
        hS = P.sb("hS", [128, 4, L], BF16)
        with P.phase():
            qT = P.sb("qT", [128, 4, L], BF16)
            kT = P.sb("kT", [128, L], BF16)
            vt = P.sb("vt", [128, NB, 128], BF16)
            ab8 = P.sb("ab8", [128, 8, 3, 128], BF16)
            P.dma("pool", ab8[:], C.consts["abias"], writes=["ab8"])
            esk = P.sb("esk", [128, 4])
            for hf in range(2):
                P.dma("sp", esk[hf * 64:(hf + 1) * 64, :], W["ev_sink"][j:j + 1, hf * 4:hf * 4 + 4].partition_broadcast(64), writes=["esk"])
            P.op("act", lambda e: e.activation(out=esk[:], in_=esk[:], func=AF.Exp), reads=["esk"], writes=["esk"])
            wb = [P.sb("wbuf", [128, 8, 128], BF16) for _ in range(2)]
            psx = [P.ps("psx", [128, 512]) for _ in range(2)]
            nx = 0
            nw = 0
            wq_view = win[:, :, 0:512].rearrange("k kc (hf c d) -> k kc c hf d", hf=2, c=4)
            for c in range(5):
                wt = wb[nw % 2]
                wkey = ("wbuf", nw % 2)
                nw += 1
                if c < 4:
                    for kc in range(8):
                        P.dma("pool", wt[:, kc, :].rearrange("k (hf d) -> k hf d", hf=2), wq_view[:, kc, c, :, :], writes=[wkey])
                    dst = qT[:, c, :]
                else:
                    P.dma("pool", wt[:], win[:, :, 512:640], writes=[wkey])
                    dst = kT[:, :]
                for tt in range(NTT):
                    pb = psx[nx % 2]
                    for kc in range(8):
                        P.op("pe", lambda e, kc=kc, tt=tt, pb=pb, wt=wt: e.matmul(pb[:], lhsT=wt[:, kc, :], rhs=C.xbf[:, kc, tt * 512:(tt + 1) * 512], start=(kc == 0), stop=(kc == 7)),
                             reads=[wkey, ("xbf", kc, tt)], writes=[("psx", nx % 2)])
                    if tt % 2 == 0:
                        P.op("act", lambda e, tt=tt, pb=pb, dst=dst: e.activation(out=dst[:, tt * 512:(tt + 1) * 512], in_=pb[:], func=AF.Copy), reads=[("psx", nx % 2)], writes=["qk"])
                    else:
                        P.op("dve", lambda e, tt=tt, pb=pb, dst=dst: e.tensor_copy(out=dst[:, tt * 512:(tt + 1) * 512], in_=pb[:]), reads=[("psx", nx % 2)], writes=["qk"])
                    nx += 1
            wt = wb[nw % 2]
            wkey = ("wbuf", nw % 2)
            nw += 1
            P.dma("pool", wt[:], win[:, :, 640:768], writes=[wkey])
            for nb in range(NB):
                pb = psx[nx % 2]
                for kc in range(8):
                    P.op("pe", lambda e, kc=kc, nb=nb, pb=pb, wt=wt: e.matmul(pb[:, 0:128], lhsT=C.xbf[:, kc, nb * 128:(nb + 1) * 128], rhs=wt[:, kc, :], start=(kc == 0), stop=(kc == 7)),
                         reads=[wkey, ("xbf", kc, nb // 4)], writes=[("psx", nx % 2)])
                if nb % 2 == 0:
                    P.op("act", lambda e, nb=nb, pb=pb: e.activation(out=vt[:, nb, :], in_=pb[:, 0:128], func=AF.Copy), reads=[("psx", nx % 2)], writes=["vt"])
                else:
                    P.op("dve", lambda e, nb=nb, pb=pb: e.tensor_copy(out=vt[:, nb, :], in_=pb[:, 0:128]), reads=[("psx", nx % 2)], writes=["vt"])
                nx += 1
            PT = [P.sb("PT", [128, 3, 128], BF16) for _ in range(2)]
            dn = P.sb("dn", [128, 128])
            pss = [P.ps("pss", [128, 512]) for _ in range(2)]
            psn = P.ps("psn", [128, 512])
            psdn = P.ps("psdn", [128, 512])
            it = 0
            for c in range(4):
                for qb in range(NB):
                    qs = slice(qb * 128, (qb + 1) * 128)
                    for hf in range(2):
                        h = hf * 4 + c
                        r = slice(hf * 64, (hf + 1) * 64)
                        b = it % 2
                        it += 1
                        rl = [r3 for r3 in range(3) if 0 <= qb + r3 - 1 < NB]
                        for r3 in rl:
                            kb = qb + r3 - 1
                            P.op("pe", lambda e, b=b, r3=r3, kb=kb, r=r, c=c, qs=qs: e.matmul(pss[b][:, r3 * 128:(r3 + 1) * 128], lhsT=kT[r, kb * 128:(kb + 1) * 128], rhs=qT[r, c, qs], start=True, stop=False),
                                 reads=["qk"], writes=[("pss", b)])
                            P.op("pe", lambda e, b=b, r3=r3, h=h: e.matmul(pss[b][:, r3 * 128:(r3 + 1) * 128], lhsT=C.identb[:], rhs=ab8[:, h, r3, :], start=False, stop=True),
                                 reads=["identb", "ab8"], writes=[("pss", b)])
                        c0, c1 = rl[0] * 128, (rl[-1] + 1) * 128
                        P.op("act", lambda e, b=b, c0=c0, c1=c1: e.activation(out=PT[b][:].rearrange("p a b -> p (a b)")[:, c0:c1], in_=pss[b][:, c0:c1], func=AF.Exp, scale=0.125),
                             reads=[("pss", b)], writes=[("PT", b)])
                        for i3, r3 in enumerate(rl):
                            kb = qb + r3 - 1
                            P.op("pe", lambda e, b=b, r3=r3, kb=kb, r=r, hf=hf, i3=i3, rl=rl: e.matmul(psn[r, 0:128], lhsT=vt[:, kb, hf * 64:(hf + 1) * 64], rhs=PT[b][:, r3, :], start=(i3 == 0), stop=(i3 == len(rl) - 1)),
                                 reads=["vt", ("PT", b)], writes=["psn"])
                        for i3, r3 in enumerate(rl):
                            P.op("pe", lambda e, b=b, r3=r3, r=r, i3=i3, rl=rl: e.matmul(psdn[r, 0:128], lhsT=C.onesb[:, 0:64], rhs=PT[b][:, r3, :], start=(i3 == 0), stop=(i3 == len(rl) - 1)),
                                 reads=["onesb", ("PT", b)], writes=["psdn"])
                    P.op("dve", lambda e, c=c: e.tensor_scalar(out=dn[:], in0=psdn[:, 0:128], scalar1=esk[:, c:c + 1], scalar2=None, op0=ALU.add), reads=["psdn", "esk"], writes=["dn"])
                    P.op("dve", lambda e: e.reciprocal(out=dn[:], in_=dn[:]), reads=["dn"], writes=["dn"])
                    P.op("dve", lambda e, c=c, qs=qs: e.tensor_tensor(out=yA[:, c, qs], in0=psn[:, 0:128], in1=dn[:], op=ALU.mult), reads=["psn", "dn"], writes=[("yA", c)])
        with P.phase():
            NPW = 22
            pw_exp = list(range(1, 17)) + [32, 64, 128, 256, 512, 1024]
            pwr = P.sb("pwr", [128, 32, NPW])
            pwi = P.sb("pwi", [128, 32, NPW])
            pni = P.sb("pni", [128, 32, NPW])
            BTp = P.sb("BTp", [128, 4, 2, 2, 2, 128], BF16)
            CTp = P.sb("CTp", [128, 16, 2, 2, 64], BF16)
            dsk = P.sb("dsk", [128, 4])
            P.dma("sp", dsk[:], W["s5_d"][j].rearrange("(c p) -> p c", p=128), writes=["dsk"], allow_slow_non_contiguous=True)
            with P.phase():
                lin = P.sb("lin", [32, 2, 128])
                P.dma("sp", lin[:, 0, :], W["s5_a_re"][j].rearrange("d (gp g2) p -> (d gp) (g2 p)", g2=2), writes=["lin"])
                P.dma("act", lin[:, 1, :], W["s5_a_im"][j].rearrange("d (gp g2) p -> (d gp) (g2 p)", g2=2), writes=["lin"])
                pst = P.ps("pst", [128, 512])
                aT = P.sb("aT", [128, 2, 32])
                for i in range(2):
                    P.op("pe", lambda e, i=i: e.transpose(pst[:, i * 32:(i + 1) * 32], lin[:, i, :], C.ident[0:32, 0:32]), reads=["lin", "ident"], writes=["pst"])
                P.op("dve", lambda e: e.tensor_copy(out=aT[:].rearrange("p a b -> p (a b)"), in_=pst[:, 0:64]), reads=["pst"], writes=["aT"])
                dt = P.sb("dt", [128, 32])
                for d in range(2):
                    for g2 in range(2):
                        src = W["s5_log_dt"][j, d:d + 1].rearrange("o (gp g2) -> o g2 gp", g2=2)[:, g2, :]
                        P.dma("sp", dt[g2 * 64:(g2 + 1) * 64, d * 16:(d + 1) * 16], src.partition_broadcast(64), writes=["dt"], allow_slow_non_contiguous=True)
                P.op("act", lambda e: e.activation(out=dt[:], in_=dt[:], func=AF.Exp), reads=["dt"], writes=["dt"])
                tA = [P.sb("tA%d" % i, [128, 32]) for i in range(8)]
                mag, ang, cs, sn, zr, zi, t6, t7 = tA
                ar, ai = aT[:, 0, :], aT[:, 1, :]
                TT = lambda out, a, b, op, rk, wk: P.op("dve", lambda e: e.tensor_tensor(out=out, in0=a, in1=b, op=op), reads=rk, writes=wk)
                TT(mag[:], ar, dt[:], ALU.mult, ["aT", "dt"], ["mag"])
                P.op("act", lambda e: e.activation(out=mag[:], in_=mag[:], func=AF.Exp), reads=["mag"], writes=["mag"])
                TT(ang[:], ai, dt[:], ALU.mult, ["aT", "dt"], ["ang"])
                ki = P.sb("ki", [128, 32], mybir.dt.int32)
                kf = P.sb("kf", [128, 32])
                for (dst, shift, key) in ((sn, 0.0, "sn"), (cs, 0.5 * PI, "cs")):
                    P.op("dve", lambda e, shift=shift: e.tensor_scalar(out=kf[:], in0=ang[:], scalar1=shift, scalar2=1.0 / (2 * PI), op0=ALU.add, op1=ALU.mult), reads=["ang", "kf"], writes=["kf"])
                    P.op("dve", lambda e: e.tensor_copy(out=ki[:], in_=kf[:]), reads=["kf", "ki"], writes=["ki"])
                    P.op("dve", lambda e: e.tensor_copy(out=kf[:], in_=ki[:]), reads=["ki"], writes=["kf"])
                    P.op("dve", lambda e, dst=dst, shift=shift: e.tensor_scalar(out=dst[:], in0=ang[:], scalar1=shift, scalar2=None, op0=ALU.add), reads=["ang"], writes=[key])
                    P.op("dve", lambda e, dst=dst: e.scalar_tensor_tensor(out=dst[:], in0=kf[:], scalar=-2 * PI, in1=dst[:], op0=ALU.mult, op1=ALU.add), reads=["kf", key], writes=[key])
                    P.op("dve", lambda e, dst=dst: e.tensor_scalar(out=kf[:], in0=dst[:], scalar1=PI, scalar2=-2 * PI, op0=ALU.is_gt, op1=ALU.mult), reads=[key, "kf"], writes=["kf"])
                    P.op("dve", lambda e, dst=dst: e.tensor_tensor(out=dst[:], in0=dst[:], in1=kf[:], op=ALU.add), reads=["kf", key], writes=[key])
                    P.op("dve", lambda e, dst=dst: e.tensor_scalar(out=kf[:], in0=dst[:], scalar1=-PI, scalar2=2 * PI, op0=ALU.is_lt, op1=ALU.mult), reads=[key, "kf"], writes=["kf"])
                    P.op("dve", lambda e, dst=dst: e.tensor_tensor(out=dst[:], in0=dst[:], in1=kf[:], op=ALU.add), reads=["kf", key], writes=[key])
                P.op("act", lambda e: e.activation(out=sn[:], in_=sn[:], func=AF.Sin), reads=["sn"], writes=["sn"])
                P.op("act", lambda e: e.activation(out=cs[:], in_=cs[:], func=AF.Sin), reads=["cs"], writes=["cs"])
                TT(pwr[:, :, 0], mag[:], cs[:], ALU.mult, ["mag", "cs"], ["pw"])
                TT(pwi[:, :, 0], mag[:], sn[:], ALU.mult, ["mag", "sn"], ["pw"])
                P.op("dve", lambda e: e.tensor_scalar(out=t6[:], in0=pwr[:, :, 0], scalar1=-1.0, scalar2=None, op0=ALU.add), reads=["pw"], writes=["t6"])
                TT(zr[:], t6[:], ar, ALU.mult, ["t6", "aT"], ["zr"])
                TT(t7[:], pwi[:, :, 0], ai, ALU.mult, ["pw", "aT"], ["t7"])
                TT(zr[:], zr[:], t7[:], ALU.add, ["zr", "t7"], ["zr"])
                TT(zi[:], pwi[:, :, 0], ar, ALU.mult, ["pw", "aT"], ["zi"])
                TT(t7[:], t6[:], ai, ALU.mult, ["t6", "aT", "t7"], ["t7"])
                TT(zi[:], zi[:], t7[:], ALU.subtract, ["zi", "t7"], ["zi"])
                TT(t6[:], ar, ar, ALU.mult, ["aT", "t6"], ["t6"])
                TT(t7[:], ai, ai, ALU.mult, ["aT", "t7"], ["t7"])
                TT(t6[:], t6[:], t7[:], ALU.add, ["t6", "t7"], ["t6"])
                P.op("dve", lambda e: e.reciprocal(out=t6[:], in_=t6[:]), reads=["t6"], writes=["t6"])
                TT(zr[:], zr[:], t6[:], ALU.mult, ["zr", "t6"], ["zr"])
                TT(zi[:], zi[:], t6[:], ALU.mult, ["zi", "t6"], ["zi"])
                nzr, nzi = mag, ang
                P.op("dve", lambda e: e.tensor_scalar(out=nzr[:], in0=zr[:], scalar1=-1.0, scalar2=None, op0=ALU.mult), reads=["zr", "mag"], writes=["nzr"])
                P.op("dve", lambda e: e.tensor_scalar(out=nzi[:], in0=zi[:], scalar1=-1.0, scalar2=None, op0=ALU.mult), reads=["zi", "ang"], writes=["nzi"])
                def cmul(oi, ai_, bi_):
                    TT(t6[:], pwr[:, :, ai_], pwr[:, :, bi_], ALU.mult, ["pw", "t6"], ["t6"])
                    TT(t7[:], pwi[:, :, ai_], pwi[:, :, bi_], ALU.mult, ["pw", "t7"], ["t7"])
                    TT(pwr[:, :, oi], t6[:], t7[:], ALU.subtract, ["t6", "t7"], ["pw"])
                    TT(t6[:], pwr[:, :, ai_], pwi[:, :, bi_], ALU.mult, ["pw", "t6"], ["t6"])
                    TT(t7[:], pwi[:, :, ai_], pwr[:, :, bi_], ALU.mult, ["pw", "t7"], ["t7"])
                    TT(pwi[:, :, oi], t6[:], t7[:], ALU.add, ["t6", "t7"], ["pw"])
                for k in range(1, 16):
                    cmul(k, k - 1, 0)
                for k in range(16, NPW):
                    cmul(k, k - 1, k - 1)
                P.op("dve", lambda e: e.tensor_scalar(out=pni[:], in0=pwi[:], scalar1=-1.0, scalar2=None, op0=ALU.mult), reads=["pw"], writes=["pni"])
                Bin = P.sb("Bin", [128, 16, 16])
                Bexp = P.sb("Bexp", [128, 16, 128], BF16)
                pbt = P.ps("pbt", [128, 4, 128], BF16)
                for d in range(2):
                    for ri in range(2):
                        src = W["s5_b_re" if ri == 0 else "s5_b_im"][j, d].rearrange("(gp g2) p h -> (g2 p) gp h", g2=2)
                        for hh in range(2):
                            P.dma("sp" if hh == 0 else "act", Bin[:, hh * 8:(hh + 1) * 8, :], src[:, hh * 8:(hh + 1) * 8, :], writes=["Bin"])
                        P.op("pool", lambda e: e.memset(Bexp[:], 0.0), writes=["Bexp"])
                        for g2 in range(2):
                            for q in range(4):
                                P.op("dve", lambda e, g2=g2, q=q: e.tensor_copy(
                                    out=Bexp[g2 * 64:(g2 + 1) * 64, q:16:4, q * 32 + g2 * 16:q * 32 + g2 * 16 + 16],
                                    in_=Bin[g2 * 64:(g2 + 1) * 64, q:16:4, :]), reads=["Bin", "Bexp"], writes=["Bexp"])
                        for c in range(4):
                            for q in range(4):
                                P.op("pe", lambda e, c=c, q=q: e.transpose(pbt[:, q, :], Bexp[:, 4 * c + q, :], C.identb[:]), reads=["Bexp", "identb"], writes=["pspbt"])
                            for q in range(4):
                                rr = slice((q // 2) * 64, (q // 2) * 64 + 64)
                                P.op("dve", lambda e, c=c, q=q, d=d, ri=ri, rr=rr: e.tensor_copy(out=BTp[rr, c, q % 2, d, ri, :], in_=pbt[rr, q, :]), reads=["pspbt"], writes=["BTp"])
                Cin = P.sb("Cin", [128, 2, 64])
                Craw = P.sb("Craw", [128, 16, 2, 2, 16])
                pct = P.ps("psct", [128, 512])
                for d in range(2):
                    for ri in range(2):
                        srcC = W["s5_c_re" if ri == 0 else "s5_c_im"][j, d].rearrange("g h p -> (g h) p")
                        for t in range(4):
                            for hh in range(2):
                                P.dma("sp" if hh == 0 else "act", Cin[:, hh, :], srcC[t * 128:(t + 1) * 128, :], writes=["Cin"])
                            P.op("pe", lambda e: e.transpose(pct[:, 0:128], Cin[:].rearrange("p a b -> p (a b)"), C.ident[:]), reads=["Cin", "ident"], writes=["psct"])
                            for g2 in range(2):
                                P.op("dve", lambda e, g2=g2, t=t, d=d, ri=ri: e.tensor_copy(
                                    out=Craw[g2 * 64:(g2 + 1) * 64, 4 * t:4 * t + 4, d, ri, :],
                                    in_=pct[g2 * 64:(g2 + 1) * 64, 0:128].rearrange("p (i g h) -> p i g h", g=2, h=16)[:, :, g2, :]),
                                    reads=["psct"], writes=["Craw"])
                P.op("pool", lambda e: e.memset(CTp[:], 0.0), writes=["CTp"])
                tc1 = P.sb("tc1", [128, 16])
                for d in range(2):
                    for gp in range(16):
                        col = d * 16 + gp
                        w = gp % 2
                        for ri, (s1, s2) in enumerate(((zr, nzi), (nzi, nzr))):
                            P.op("dve", lambda e, gp=gp, d=d, s1=s1, col=col: e.tensor_scalar(out=tc1[:], in0=Craw[:, gp, d, 0, :], scalar1=s1[:, col:col + 1], scalar2=None, op0=ALU.mult),
                                 reads=["Craw", "zr", "nzr", "nzi"], writes=["tc1"])
                            for g2 in range(2):
                                rr = slice(g2 * 64, (g2 + 1) * 64)
                                P.op("dve", lambda e, gp=gp, d=d, s2=s2, col=col, rr=rr, ri=ri, w=w, g2=g2: e.scalar_tensor_tensor(
                                    out=CTp[rr, gp, d, ri, w * 32 + g2 * 16:w * 32 + g2 * 16 + 16], in0=Craw[rr, gp, d, 1, :], scalar=s2[rr, col:col + 1], in1=tc1[rr, :], op0=ALU.mult, op1=ALU.add),
                                    reads=["Craw", "tc1", "zr", "nzr", "nzi", "CTp"], writes=["CTp"])
            if C.dbgout:
                P.dma("sp", C.dbgout["pwr"], pwr[:], reads=["pw"])
                P.dma("sp", C.dbgout["pwi"], pwi[:], reads=["pw"])
                P.dma("sp", C.dbgout["BTp"], BTp[:], reads=["BTp"])
                P.dma("sp", C.dbgout["CTp"], CTp[:], reads=["CTp"])
            wb = [P.sb("wbuf", [128, 8, 128], BF16) for _ in range(2)]
            uT = P.sb("uT", [128, L], BF16)
            Xr = P.sb("Xr", [128, 16, 128])
            Xi = P.sb("Xi", [128, 16, 128])
            Xbr = P.sb("Xbr", [128, L], BF16)
            Xbi = P.sb("Xbi", [128, L], BF16)
            Sb = [[P.sb("Sb", [128, 130]) for _ in range(2)] for _ in range(2)]
            pt_ = P.sb("s5post", [128, 512])
            psy = [P.ps("psy", [128, 512]) for _ in range(4)]
            psb = [P.ps("psb", [128, 512]) for _ in range(4)]
            Xbr3 = Xbr[:].rearrange("p (n a) -> p a n", a=16)
            Xbi3 = Xbi[:].rearrange("p (n a) -> p a n", a=16)
            for i in range(2):
                for k in range(2):
                    P.op("pool", lambda e, i=i, k=k: e.memset(Sb[i][k][:], 0.0), writes=[("Sb", i, k)])
            nb_ = 0
            for c in range(4):
                wt = wb[c % 2]
                wkey = ("wbuf", c % 2)
                P.dma("pool", wt[:], win[:, :, 768 + c * 128:768 + (c + 1) * 128], writes=[wkey])
                for tt in range(NTT):
                    pb = psb[nb_ % 4]
                    for kc in range(8):
                        P.op("pe", lambda e, kc=kc, tt=tt, pb=pb, wt=wt: e.matmul(pb[:], lhsT=wt[:, kc, :], rhs=C.xbf[:, kc, tt * 512:(tt + 1) * 512], start=(kc == 0), stop=(kc == 7)),
                             reads=[wkey, ("xbf", kc, tt)], writes=[("psb", nb_ % 4)])
                    P.op("act", lambda e, tt=tt, pb=pb: e.activation(out=uT[:, tt * 512:(tt + 1) * 512], in_=pb[:], func=AF.Copy), reads=[("psb", nb_ % 4)], writes=["uT"])
                    nb_ += 1
                for q in range(4):
                    gp = 4 * c + q
                    kr = slice((q // 2) * 64, (q // 2) * 64 + 64)
                    for d in range(2):
                        col = d * 16 + gp
                        LR = lambda k, col=col: pwr[:, col, k:k + 1]
                        LI = lambda k, col=col: pwi[:, col, k:k + 1]
                        NI = lambda k, col=col: pni[:, col, k:k + 1]
                        for tt in range(NTT):
                            for ri, X in enumerate((Xr, Xi)):
                                pb = psb[nb_ % 4]
                                P.op("pe", lambda e, tt=tt, pb=pb, ri=ri, kr=kr, q=q, d=d, c=c: e.matmul(pb[:], lhsT=BTp[kr, c, q % 2, d, ri, :], rhs=uT[kr, tt * 512:(tt + 1) * 512], start=True, stop=True),
                                     reads=["BTp", "uT"], writes=[("psb", nb_ % 4)])
                                if ri == 0:
                                    P.op("act", lambda e, tt=tt, pb=pb, X=X: e.activation(out=X[:, :, tt * 32:(tt + 1) * 32], in_=pb[:].rearrange("p (n a) -> p a n", a=16), func=AF.Copy),
                                         reads=[("psb", nb_ % 4)], writes=["X"])
                                else:
                                    P.op("dve", lambda e, tt=tt, pb=pb, X=X: e.tensor_copy(out=X[:, :, tt * 32:(tt + 1) * 32], in_=pb[:].rearrange("p (n a) -> p a n", a=16)),
                                         reads=[("psb", nb_ % 4)], writes=["X"])
                                nb_ += 1
                        steps = range(1, 16) if d == 0 else range(14, -1, -1)
                        for a in steps:
                            ap_ = a - 1 if d == 0 else a + 1
                            P.op("dve", lambda e, sc_=LR(0), a=a, ap_=ap_: e.scalar_tensor_tensor(out=Xr[:, a, :], in0=Xr[:, ap_, :], scalar=sc_, in1=Xr[:, a, :], op0=ALU.mult, op1=ALU.add), reads=["X", "pw"], writes=["X"])
                            P.op("dve", lambda e, sc_=LR(0), a=a, ap_=ap_: e.scalar_tensor_tensor(out=Xi[:, a, :], in0=Xi[:, ap_, :], scalar=sc_, in1=Xi[:, a, :], op0=ALU.mult, op1=ALU.add), reads=["X", "pw"], writes=["X"])
                            P.op("dve", lambda e, sc_=NI(0), a=a, ap_=ap_: e.scalar_tensor_tensor(out=Xr[:, a, :], in0=Xi[:, ap_, :], scalar=sc_, in1=Xr[:, a, :], op0=ALU.mult, op1=ALU.add), reads=["X", "pni"], writes=["X"])
                            P.op("dve", lambda e, sc_=LI(0), a=a, ap_=ap_: e.scalar_tensor_tensor(out=Xi[:, a, :], in0=Xr[:, ap_, :], scalar=sc_, in1=Xi[:, a, :], op0=ALU.mult, op1=ALU.add), reads=["X", "pw"], writes=["X"])
                        ae = 15 if d == 0 else 0
                        cur = 0
                        P.op("dve", lambda e, ae=ae: e.tensor_copy(out=Sb[0][0][:, 1:129], in_=Xr[:, ae, :]), reads=["X", ("Sb", 0, 0)], writes=[("Sb", 0, 0)])
                        P.op("dve", lambda e, ae=ae: e.tensor_copy(out=Sb[0][1][:, 1:129], in_=Xi[:, ae, :]), reads=["X", ("Sb", 0, 1)], writes=[("Sb", 0, 1)])
                        s = 1
                        lev = 0
                        while s < 128:
                            k = 15 + lev
                            A, B = Sb[cur], Sb[1 - cur]
                            ka = [("Sb", cur, 0), ("Sb", cur, 1)]
                            kb = [("Sb", 1 - cur, 0), ("Sb", 1 - cur, 1)]
                            if d == 0:
                                o_, i_, h_ = slice(1 + s, 129), slice(1, 129 - s), slice(1, 1 + s)
                            else:
                                o_, i_, h_ = slice(1, 129 - s), slice(1 + s, 129), slice(129 - s, 129)
                            P.op("dve", lambda e, sc_=LR(k), A=A, B=B, o_=o_, i_=i_, k=k: e.scalar_tensor_tensor(out=B[0][:, o_], in0=A[0][:, i_], scalar=sc_, in1=A[0][:, o_], op0=ALU.mult, op1=ALU.add), reads=ka + ["pw"], writes=[kb[0]])
                            P.op("dve", lambda e, sc_=LR(k), A=A, B=B, o_=o_, i_=i_, k=k: e.scalar_tensor_tensor(out=B[1][:, o_], in0=A[1][:, i_], scalar=sc_, in1=A[1][:, o_], op0=ALU.mult, op1=ALU.add), reads=ka + ["pw"], writes=[kb[1]])
                            P.op("dve", lambda e, sc_=NI(k), A=A, B=B, o_=o_, i_=i_, k=k: e.scalar_tensor_tensor(out=B[0][:, o_], in0=A[1][:, i_], scalar=sc_, in1=B[0][:, o_], op0=ALU.mult, op1=ALU.add), reads=ka + ["pni", kb[0]], writes=[kb[0]])
                            P.op("dve", lambda e, sc_=LI(k), A=A, B=B, o_=o_, i_=i_, k=k: e.scalar_tensor_tensor(out=B[1][:, o_], in0=A[0][:, i_], scalar=sc_, in1=B[1][:, o_], op0=ALU.mult, op1=ALU.add), reads=ka + ["pw", kb[1]], writes=[kb[1]])
                            P.op("pool", lambda e, A=A, B=B, h_=h_: e.tensor_copy(out=B[0][:, h_], in_=A[0][:, h_]), reads=[ka[0]], writes=[kb[0]])
                            P.op("pool", lambda e, A=A, B=B, h_=h_: e.tensor_copy(out=B[1][:, h_], in_=A[1][:, h_]), reads=[ka[1]], writes=[kb[1]])
                            cur = 1 - cur
                            s *= 2
                            lev += 1
                        S = Sb[cur]
                        ks = [("Sb", cur, 0), ("Sb", cur, 1)]
                        cin = slice(0, 128) if d == 0 else slice(2, 130)
                        for a in range(16):
                            k = a if d == 0 else 15 - a
                            P.op("dve", lambda e, sc_=LR(k), a=a, k=k, S=S, cin=cin: e.scalar_tensor_tensor(out=Xr[:, a, :], in0=S[0][:, cin], scalar=sc_, in1=Xr[:, a, :], op0=ALU.mult, op1=ALU.add), reads=["X", "pw"] + ks, writes=["X"])
                            P.op("dve", lambda e, sc_=LR(k), a=a, k=k, S=S, cin=cin: e.scalar_tensor_tensor(out=Xi[:, a, :], in0=S[1][:, cin], scalar=sc_, in1=Xi[:, a, :], op0=ALU.mult, op1=ALU.add), reads=["X", "pw"] + ks, writes=["X"])
                            P.op("dve", lambda e, sc_=NI(k), a=a, k=k, S=S, cin=cin: e.scalar_tensor_tensor(out=Xbr3[:, a, :], in0=S[1][:, cin], scalar=sc_, in1=Xr[:, a, :], op0=ALU.mult, op1=ALU.add), reads=["X", "pni"] + ks, writes=["Xb"])
                            P.op("dve", lambda e, sc_=LI(k), a=a, k=k, S=S, cin=cin: e.scalar_tensor_tensor(out=Xbi3[:, a, :], in0=S[0][:, cin], scalar=sc_, in1=Xi[:, a, :], op0=ALU.mult, op1=ALU.add), reads=["X", "pw"] + ks, writes=["Xb"])
                        if C.dbgout and gp == 0 and d == 0:
                            P.dma("sp", C.dbgout["Xbr"], Xbr[:], reads=["Xb"])
                            P.dma("sp", C.dbgout["Xbi"], Xbi[:], reads=["Xb"])
                        for tt in range(NTT):
                            for ri, Xb in enumerate((Xbr, Xbi)):
                                first = (q % 2 == 0 and d == 0 and ri == 0)
                                last = (q % 2 == 1 and d == 1 and ri == 1)
                                P.op("pe", lambda e, tt=tt, ri=ri, Xb=Xb, gp=gp, d=d, kr=kr, first=first, last=last: e.matmul(psy[tt][kr, :], lhsT=CTp[:, gp, d, ri, :], rhs=Xb[:, tt * 512:(tt + 1) * 512], start=first, stop=last),
                                     reads=["CTp", "Xb"], writes=[("psy", tt)])
                for tt in range(NTT):
                    ts = slice(tt * 512, (tt + 1) * 512)
                    P.op("dve", lambda e, tt=tt, ts=ts, c=c: e.scalar_tensor_tensor(out=pt_[:], in0=uT[:, ts], scalar=dsk[:, c:c + 1], in1=psy[tt][:], op0=ALU.mult, op1=ALU.add),
                         reads=["uT", "dsk", ("psy", tt)], writes=["s5post"])
                    P.op("act", lambda e, ts=ts, c=c: e.activation(out=hS[:, c, ts], in_=pt_[:], func=AF.Gelu_apprx_tanh), reads=["s5post"], writes=[("hS", c)])
            P.barrier()
            wglu = P.sb("wglu", [128, 4, 512], BF16) if False else None
        with P.phase():
            wglu = P.sb("wglu", [128, 4, 512], BF16)
            P.dma("pool", wglu[:], W["s5_w_glu"][j].rearrange("(k p) m -> p k m", p=128), writes=["wglu"])
            gs = [P.sb("gs", [128, 512], BF16) for _ in range(4)]
            psg = [P.ps("psgl", [128, 512]) for _ in range(2)]
            n = 0
            for tt in range(NTT):
                ts = slice(tt * 512, (tt + 1) * 512)
                for m in range(4):
                    pb = psg[n % 2]
                    for k in range(4):
                        P.op("pe", lambda e, k=k, m=m, ts=ts, pb=pb: e.matmul(pb[:], lhsT=wglu[:, k, m * 128:(m + 1) * 128], rhs=hS[:, k, ts], start=(k == 0), stop=(k == 3)),
                             reads=["wglu", ("hS", k, tt)], writes=[("psgl", n % 2)])
                    P.op("act", lambda e, m=m, pb=pb: e.activation(out=gs[m][:], in_=pb[:], func=AF.Sigmoid), reads=[("psgl", n % 2)], writes=[("gs", m)])
                    n += 1
                for m in range(4):
                    P.op("dve" if m % 2 == 0 else "pool", lambda e, m=m, ts=ts: e.tensor_tensor(out=hS[:, m, ts], in0=hS[:, m, ts], in1=gs[m][:], op=ALU.mult),
                         reads=[("gs", m)] + [("hS", k, tt) for k in range(4)], writes=[("hS", m, tt)])
        if C.dbgout:
            with P.phase():
                P.dma("sp", C.dbgout["yA"], yA[:], reads=[])
                P.dma("act", C.dbgout["hS"], hS[:], reads=[])
        with P.phase():
            T = ln_temps(P)
            wo = [P.sb("wo", [128, 8, 128], BF16) for _ in range(2)]
            pso = [P.ps("pso", [128, 512]) for _ in range(2)]
            wsrcA = W["ev_w_out"][j][0:512, :].rearrange("(hf c d) m -> hf d c m", hf=2, c=4)
            wsrcS = W["ev_w_out"][j][512:1024, :].rearrange("(k p) m -> p k m", p=128)
            for m in range(8):
                b = m % 2
                ms = slice(m * 128, (m + 1) * 128)
                for hf in range(2):
                    P.dma("pool", wo[b][hf * 64:(hf + 1) * 64, 0:4, :], wsrcA[hf][:, :, ms], writes=[("wo", b)])
                P.dma("pool", wo[b][:, 4:8, :], wsrcS[:, :, ms], writes=[("wo", b)])
                for tt in range(NTT):
                    ts = slice(tt * 512, (tt + 1) * 512)
                    pb = pso[tt % 2]
                    for k in range(8):
                        rhs = yA[:, k, ts] if k < 4 else hS[:, k - 4, ts]
                        P.op("pe", lambda e, k=k, b=b, pb=pb, rhs=rhs: e.matmul(pb[:], lhsT=wo[b][:, k, :], rhs=rhs, start=(k == 0), stop=(k == 7)),
                             reads=[("wo", b)], writes=[("pso", tt % 2)])
                    P.op("dve", lambda e, m=m, ts=ts, pb=pb: e.scalar_tensor_tensor(out=C.xres[:, m, ts], in0=C.xres[:, m, ts], scalar=ALPHA, in1=pb[:], op0=ALU.mult, op1=ALU.add),
                         reads=[("pso", tt % 2), ("xres", m, tt)], writes=[("xres", m, tt)])
            for tt in range(NTT):
                emit_ln(C, l, 0, tt, T)


def build(nseq, layers, in_mode, out_mode, dbg=()):
    nc = bass.Bass("TRN2", target_bir_lowering=False)
    C = Ctx()
    C.nc = nc
    C.W = {k: nc.dram_tensor(k, s, F32, kind="ExternalInput").ap() for k, s in WEIGHT_SHAPES.items()}
    C.consts = {k: nc.dram_tensor("c_" + k, s, F32, kind="ExternalInput").ap() for k, s in CONST_SHAPES.items()}
    if in_mode == "tm":
        xin = nc.dram_tensor("x", [nseq, L, D], F32, kind="ExternalInput").ap()
    else:
        xin = nc.dram_tensor("x", [nseq, D, L], F32, kind="ExternalInput").ap()
    if out_mode == "tm":
        yout = nc.dram_tensor("y", [nseq, L, D], F32, kind="ExternalOutput").ap()
    else:
        yout = nc.dram_tensor("y", [nseq, D, L], F32, kind="ExternalOutput").ap()
    C.dbgout = {}
    if dbg:
        C.dbgout = {"yA": nc.dram_tensor("dbg_yA", [128, 4, L], BF16, kind="ExternalOutput").ap(),
                    "hS": nc.dram_tensor("dbg_hS", [128, 4, L], BF16, kind="ExternalOutput").ap(),
                    "pwr": nc.dram_tensor("dbg_pwr", [128, 32, 22], F32, kind="ExternalOutput").ap(),
                    "pwi": nc.dram_tensor("dbg_pwi", [128, 32, 22], F32, kind="ExternalOutput").ap(),
                    "BTp": nc.dram_tensor("dbg_BTp", [128, 4, 2, 2, 2, 128], BF16, kind="ExternalOutput").ap(),
                    "CTp": nc.dram_tensor("dbg_CTp", [128, 16, 2, 2, 64], BF16, kind="ExternalOutput").ap(),
                    "Xbr": nc.dram_tensor("dbg_Xbr", [128, L], BF16, kind="ExternalOutput").ap(),
                    "Xbi": nc.dram_tensor("dbg_Xbi", [128, L], BF16, kind="ExternalOutput").ap()}
    P = Prog(nc)
    C.P = P
    C.xres = P.sb("xres", [128, 8, L])
    C.xbf = P.sb("xbf", [128, 8, L], BF16)
    emit_const_setup(C)
    P.barrier()
    P.flush()
    for s in range(nseq):
        (load_x_tm if in_mode == "tm" else load_x_fm)(C, xin[s])
        import os
        STG = os.environ.get("KSTAGE", "all")
        for l in layers:
            if STG in ("all", "mixer"):
                if l % 2 == 0:
                    emit_even(C, l)
                else:
                    emit_mlstm(C, l)
            if STG in ("all", "ffn"):
                emit_ffn(C, l)
        (store_x_tm if out_mode == "tm" else store_x_fm)(C, yout[s])
    P.finish()
    return nc


_CACHE = {}


def run_layers(xs, weights, layers, in_mode, out_mode, nseq):
    key = (tuple(layers), in_mode, out_mode, nseq)
    if key not in _CACHE:
        _CACHE[key] = build(nseq, layers, in_mode, out_mode)
    nc = _CACHE[key]
    consts = host_consts()
    in_maps = []
    for c in range(8):
        m = {k: np.ascontiguousarray(v, dtype=np.float32) for k, v in weights.items()}
        for k, v in consts.items():
            m["c_" + k] = v
        m["x"] = np.ascontiguousarray(xs[c])
        in_maps.append(m)
    res = run_bass_kernel_spmd(nc, in_maps, core_ids=list(range(8)))
    return [r["y"] for r in res.results]


def kernel(**inputs):
    x = np.asarray(inputs["x"], dtype=np.float32)
    weights = {k: np.asarray(v, dtype=np.float32) for k, v in inputs.items() if k != "x"}
    xs = [x[c * 4:(c + 1) * 4] for c in range(8)]
    ys = run_layers(xs, weights, [0, 1, 2, 3], "tm", "tm", 4)
    return np.concatenate(ys, axis=0).astype(np.float32)
```

```python
import math
from contextlib import ExitStack, contextmanager
import numpy as np
import ml_dtypes
import concourse.bass as bass
import concourse.mybir as mybir
from concourse.bass_utils import run_bass_kernel_spmd

F32 = mybir.dt.float32
BF16 = mybir.dt.bfloat16
AF = mybir.ActivationFunctionType
ALU = mybir.AluOpType

import os as _os
ENGS = ("pe", "act", "dve", "pool", "sp")
SAME_ENGINE_NOSYNC = tuple(_os.environ.get("KNOSYNC", "pe").split(","))
DQ = ("sp", "act", "pool")

D = 1024
L = 2048
NB = 16
NTT = 4
DEPTH = 4
DFF = 2816
NJ = 22
ALPHA = (2 * DEPTH) ** 0.25
EPS = 1e-5
NEG = -30000.0


class Prog:
    NSLOT = 8

    def __init__(self, nc):
        self.nc = nc
        self.es = ExitStack()
        self.ops = {e: [] for e in ENGS}
        self.sem = {"s_" + e: self.es.enter_context(nc.semaphore("s_" + e)) for e in ENGS}
        self.cnt = {e: 0 for e in ENGS}
        self.dcnt = {}
        for q in DQ:
            for i in range(self.NSLOT):
                n = "d_%s%d" % (q, i)
                self.sem[n] = self.es.enter_context(nc.semaphore(n))
                self.dcnt[n] = 0
        self.dnext = {q: 0 for q in DQ}
        self.seen = {e: {} for e in ENGS}
        self.lastw = {}
        self.readers = {}
        self.stack = [self.es]
        self.uid = 0

    def sb(self, name, shape, dt=F32):
        self.uid += 1
        return self.stack[-1].enter_context(self.nc.sbuf_tensor("%s_%d" % (name, self.uid), list(shape), dt))

    def ps(self, name, shape, dt=F32):
        self.uid += 1
        return self.stack[-1].enter_context(self.nc.psum_tensor("%s_%d" % (name, self.uid), list(shape), dt))

    @contextmanager
    def phase(self):
        es = ExitStack()
        self.stack.append(es)
        try:
            yield
        finally:
            self.barrier()
            self.flush()
            self.stack.pop()
            es.close()

    def _need(self, eng, tok, waits):
        if tok is None:
            return
        sname, val, teng = tok
        if teng == eng and eng in SAME_ENGINE_NOSYNC:
            return
        if self.seen[eng].get(sname, 0) >= val:
            return
        self.seen[eng][sname] = val
        waits.append((sname, val))

    def _deps(self, eng, reads, writes):
        waits = []
        for k in reads:
            self._need(eng, self.lastw.get(k), waits)
        for k in writes:
            self._need(eng, self.lastw.get(k), waits)
            for t in self.readers.get(k, ()):
                self._need(eng, t, waits)
        return waits

    def _commit(self, tok, reads, writes):
        for k in reads:
            self.readers.setdefault(k, []).append(tok)
        for k in writes:
            self.lastw[k] = tok
            self.readers[k] = []

    @staticmethod
    def _isps(k):
        k0 = k[0] if isinstance(k, tuple) else k
        return isinstance(k0, str) and k0.startswith("ps")

    def op(self, eng, fn, reads=(), writes=()):
        psr = [k for k in reads if self._isps(k)]
        if psr:
            reads = [k for k in reads if not self._isps(k)]
            writes = list(writes) + psr
        waits = self._deps(eng, reads, writes)
        self.cnt[eng] += 1
        tok = ("s_" + eng, self.cnt[eng], eng)
        self.ops[eng].append((waits, fn, ("s_" + eng, 1)))
        self._commit(tok, reads, writes)
        return tok

    def dma(self, q, out, in_, reads=(), writes=(), **kw):
        waits = self._deps(q, reads, writes)
        s = self.dnext[q]
        self.dnext[q] = (s + 1) % self.NSLOT
        sname = "d_%s%d" % (q, s)
        prev = self.dcnt[sname]
        if prev > 0 and self.seen[q].get(sname, 0) < prev:
            self.seen[q][sname] = prev
            waits.append((sname, prev))
        self.dcnt[sname] = prev + 16
        tok = (sname, prev + 16, "dma_" + q)
        self.ops[q].append((waits, lambda e: e.dma_start(out=out, in_=in_, **kw), (sname, 16)))
        self._commit(tok, reads, writes)
        return tok

    def barrier(self):
        for e in ENGS:
            waits = []
            for n, v in self.dcnt.items():
                if v > 0 and self.seen[e].get(n, 0) < v:
                    self.seen[e][n] = v
                    waits.append((n, v))
            for e2 in ENGS:
                n = "s_" + e2
                v = self.cnt[e2]
                if e2 != e and v > 0 and self.seen[e].get(n, 0) < v:
                    self.seen[e][n] = v
                    waits.append((n, v))
            if waits:
                self.ops[e].append((waits, None, None))
        self.lastw = {}
        self.readers = {}

    def flush(self):
        nc = self.nc
        handles = {"pe": "tensor", "act": "scalar", "dve": "vector", "pool": "gpsimd", "sp": "sync"}
        if not any(self.ops[e] for e in ENGS):
            return
        with nc.Block() as block:
            for e in ENGS:
                lst = self.ops[e]
                if not lst:
                    continue

                def body(eng, lst=lst):
                    for waits, fn, inc in lst:
                        for sname, val in waits:
                            eng.wait_ge(self.sem[sname], val)
                        if fn is not None:
                            fn(eng).then_inc(self.sem[inc[0]], inc[1])
                getattr(block, handles[e])(body)
        self.ops = {e: [] for e in ENGS}

    def finish(self):
        self.barrier()
        self.flush()
        self.es.close()


def host_consts():
    c = {}
    c["ident"] = np.eye(128, dtype=np.float32)
    s = np.arange(128)[:, None]
    t = np.arange(512)[None, :]
    mf = np.stack([(t >= jj * 128 + s) for jj in range(4)], 1).astype(np.float32)
    mb = np.stack([(t <= jj * 128 + s) for jj in range(4)], 1).astype(np.float32)
    s1 = np.arange(128)[:, None]
    t1 = np.arange(128)[None, :]
    c["tri"] = np.stack([(t1 >= s1), (t1 <= s1)], 1).astype(np.float32)
    slopes = 2.0 ** (-8.0 * np.arange(1, 9) / 8)
    kk = np.arange(128)[:, None]
    qq = np.arange(128)[None, :]
    ab = np.zeros((128, 8, 3, 128), np.float32)
    for h in range(8):
        for r in range(3):
            dist = np.abs(qq - (kk + (r - 1) * 128))
            ab[:, h, r, :] = np.where(dist <= 128, -slopes[h] * dist * 8.0, NEG)
    c["abias"] = ab
    sel = np.zeros((40, 8, 128), np.float32)
    for h in range(8):
        sel[h, h, :] = 1.0
        sel[32 + h, h, :] = 1.0
    c["sel"] = sel
    return c


CONST_SHAPES = {"ident": [128, 128], "tri": [128, 2, 128],
                "abias": [128, 8, 3, 128], "sel": [40, 8, 128]}

WEIGHT_SHAPES = {
    "ev_w_in": [2, 1024, 1280], "ev_sink": [2, 8], "s5_a_re": [2, 2, 32, 64], "s5_a_im": [2, 2, 32, 64],
    "s5_log_dt": [2, 2, 32], "s5_b_re": [2, 2, 32, 64, 16], "s5_b_im": [2, 2, 32, 64, 16],
    "s5_c_re": [2, 2, 32, 16, 64], "s5_c_im": [2, 2, 32, 16, 64], "s5_d": [2, 512],
    "s5_w_glu": [2, 512, 512], "ev_w_out": [2, 1024, 1024], "od_w_in": [2, 1024, 4128],
    "od_gate_b": [2, 32], "od_conv_w": [2, 3, 2048], "od_conv_b": [2, 2048], "od_norm_g": [2, 1024],
    "od_w_out": [2, 1024, 1024], "ffn_w_up": [4, 1024, 5632], "ffn_conv_w": [4, 3, 2816],
    "ffn_conv_b": [4, 2816], "ffn_w_down": [4, 2816, 1024], "ln_g": [4, 2, 1024], "ln_b": [4, 2, 1024],
}


class Ctx:
    pass


def kc_view(w2d):
    return w2d.rearrange("(kc k) m -> k kc m", k=128)


def emit_const_setup(C):
    P = C.P
    W = C.W
    C.ident = P.sb("ident", [128, 128])
    C.identb = P.sb("identb", [128, 128], BF16)
    C.onesf = P.sb("onesf", [128, 128])
    C.ones1k = P.sb("ones1k", [128, 128])
    C.onesb = P.sb("onesb", [128, 128], BF16)
    C.sel = P.sb("sel", [40, 8, 128])
    C.one8 = P.sb("one8", [128, 1])
    P.dma("sp", C.ident[:], C.consts["ident"], writes=["ident"])
    P.dma("sp", C.sel[:], C.consts["sel"], writes=["sel"])
    P.op("dve", lambda e: e.tensor_copy(out=C.identb[:], in_=C.ident[:]), reads=["ident"], writes=["identb"])
    P.op("pool", lambda e: e.memset(C.onesf[:], 1.0 / 128), writes=["onesf"])
    P.op("pool", lambda e: e.memset(C.ones1k[:], 1.0 / 1024), writes=["ones1k"])
    P.op("pool", lambda e: e.memset(C.onesb[:], 1.0), writes=["onesb"])
    P.op("pool", lambda e: e.memset(C.one8[:], 1.0), writes=["one8"])
    C.epsc = P.sb("epsc", [128, 1])
    P.op("pool", lambda e: e.memset(C.epsc[:], EPS), writes=["epsc"])
    C.lng = P.sb("lng", [128, 4, 2, 8])
    C.lnb = P.sb("lnb", [128, 4, 2, 8])
    import os
    for l in range(4 if os.environ.get("KNOLN") is None else 0):
        for j in range(2):
            P.dma("sp", C.lng[:, l, j, :], W["ln_g"][l, j].rearrange("(c p) -> p c", p=128), writes=["lng"],
                  allow_slow_non_contiguous=True)
            P.dma("act", C.lnb[:, l, j, :], W["ln_b"][l, j].rearrange("(c p) -> p c", p=128), writes=["lnb"],
                  allow_slow_non_contiguous=True)


def load_x_tm(C, x_seq):
    P = C.P
    with P.phase():
        stg = [P.sb("xstg", [128, 1024]) for _ in range(2)]
        pt = [P.ps("xps", [128, 4, 128]) for _ in range(2)]
        for nb in range(NB):
            s = stg[nb % 2]
            P.dma("sp" if nb % 2 == 0 else "act", s[:], x_seq[nb * 128:(nb + 1) * 128, :], writes=[("xstg", nb % 2)])
            for half in range(2):
                pp = pt[half]
                for i in range(4):
                    c = half * 4 + i
                    P.op("pe", lambda e, pp=pp, i=i, c=c, s=s: e.transpose(pp[:, i, :], s[:, c * 128:(c + 1) * 128], C.ident[:]),
                         reads=[("xstg", nb % 2), "ident"], writes=[("psxp", half)])
                P.op("act", lambda e, pp=pp, half=half, nb=nb: e.activation(
                    out=C.xres[:, half * 4:half * 4 + 4, nb * 128:(nb + 1) * 128], in_=pp[:], func=AF.Copy),
                    reads=[("psxp", half)], writes=[("xres", nb, half)])
                P.op("dve", lambda e, pp=pp, half=half, nb=nb: e.tensor_copy(
                    out=C.xbf[:, half * 4:half * 4 + 4, nb * 128:(nb + 1) * 128], in_=pp[:]),
                    reads=[("psxp", half)], writes=[("xbf", nb, half)])


def load_x_fm(C, xT_seq):
    P = C.P
    with P.phase():
        for c in range(8):
            P.dma("sp" if c % 2 == 0 else "act", C.xres[:, c, :], xT_seq[c * 128:(c + 1) * 128, :], writes=[("xres", c)])
            P.op("dve" if c % 2 == 0 else "pool", lambda e, c=c: e.tensor_copy(out=C.xbf[:, c, :], in_=C.xres[:, c, :]),
                 reads=[("xres", c)], writes=[("xbf", c)])


def store_x_fm(C, xT_seq):
    P = C.P
    with P.phase():
        for c in range(8):
            P.dma("sp" if c % 2 == 0 else "act", xT_seq[c * 128:(c + 1) * 128, :], C.xres[:, c, :], reads=[("xres", c)])


def store_x_tm(C, y_seq):
    P = C.P
    with P.phase():
        stg = [P.sb("ystg", [128, 1024]) for _ in range(2)]
        pt = [P.ps("yps", [128, 4, 128]) for _ in range(2)]
        for nb in range(NB):
            s = stg[nb % 2]
            for half in range(2):
                pp = pt[half]
                for i in range(4):
                    c = half * 4 + i
                    P.op("pe", lambda e, pp=pp, i=i, c=c, nb=nb: e.transpose(pp[:, i, :], C.xres[:, c, nb * 128:(nb + 1) * 128], C.ident[:]),
                         reads=["ident"], writes=[("psyp", half)])
                eng = "act" if half == 0 else "dve"
                if eng == "act":
                    P.op("act", lambda e, pp=pp, s=s, half=half: e.activation(out=s[:, half * 512:(half + 1) * 512], in_=pp[:].rearrange("p a b -> p (a b)"), func=AF.Copy),
                         reads=[("psyp", half)], writes=[("ystg", nb % 2, half)])
                else:
                    P.op("dve", lambda e, pp=pp, s=s, half=half: e.tensor_copy(out=s[:, half * 512:(half + 1) * 512], in_=pp[:].rearrange("p a b -> p (a b)")),
                         reads=[("psyp", half)], writes=[("ystg", nb % 2, half)])
            P.dma("sp" if nb % 2 == 0 else "act", y_seq[nb * 128:(nb + 1) * 128, :], s[:],
                  reads=[("ystg", nb % 2, 0), ("ystg", nb % 2, 1)])


def emit_ln(C, l, j, tt, T):
    P = C.P
    ts = slice(tt * 512, (tt + 1) * 512)
    rk = [("xres", c, tt) for c in range(8)]
    for c in range(8):
        P.op("pe", lambda e, c=c: e.matmul(T["psm"][:], lhsT=C.ones1k[:], rhs=C.xres[:, c, ts], start=(c == 0), stop=(c == 7)),
             reads=[rk[c], "ones1k"], writes=["psm"])
    for c in range(8):
        sq = T["sq"][c % 2]
        P.op("act", lambda e, c=c, sq=sq: e.activation(out=sq[:], in_=C.xres[:, c, ts], func=AF.Square),
             reads=[rk[c]], writes=[("sq", c % 2)])
        P.op("pe", lambda e, c=c, sq=sq: e.matmul(T["psv"][:], lhsT=C.ones1k[:], rhs=sq[:], start=(c == 0), stop=(c == 7)),
             reads=[("sq", c % 2), "ones1k"], writes=["psv"])
    mean, rstd = T["mean"], T["rstd"]
    P.op("act", lambda e: e.activation(out=mean[:], in_=T["psm"][:], func=AF.Copy), reads=["psm"], writes=["mean"])
    P.op("dve", lambda e: e.tensor_tensor(out=rstd[:], in0=mean[:], in1=mean[:], op=ALU.mult), reads=["mean"], writes=["rstd"])
    P.op("dve", lambda e: e.tensor_tensor(out=rstd[:], in0=T["psv"][:], in1=rstd[:], op=ALU.subtract), reads=["psv", "rstd"], writes=["rstd"])
    P.op("act", lambda e: e.activation(out=rstd[:], in_=rstd[:], func=AF.Sqrt, bias=C.epsc[:, 0:1], scale=1.0), reads=["rstd", "epsc"], writes=["rstd"])
    P.op("dve", lambda e: e.reciprocal(out=rstd[:], in_=rstd[:]), reads=["rstd"], writes=["rstd"])
    for c in range(8):
        tmp = T["tmp"][c % 2]
        P.op("pool", lambda e, c=c, tmp=tmp: e.tensor_tensor(out=tmp[:], in0=C.xres[:, c, ts], in1=mean[:], op=ALU.subtract),
             reads=[rk[c], "mean"], writes=[("lntmp", c % 2)])
        P.op("dve", lambda e, c=c, tmp=tmp: e.tensor_tensor(out=tmp[:], in0=tmp[:], in1=rstd[:], op=ALU.mult),
             reads=[("lntmp", c % 2), "rstd"], writes=[("lntmp", c % 2)])
        P.op("act", lambda e, c=c, tmp=tmp: e.activation(out=C.xres[:, c, ts], in_=tmp[:], func=AF.Identity,
                                                         bias=C.lnb[:, l, j, c:c + 1], scale=C.lng[:, l, j, c:c + 1]),
             reads=[("lntmp", c % 2), "lng", "lnb"], writes=[rk[c]])
        P.op("pool", lambda e, c=c: e.tensor_copy(out=C.xbf[:, c, ts], in_=C.xres[:, c, ts]),
             reads=[rk[c]], writes=[("xbf", c, tt)])


def ln_temps(P):
    return {"sq": [P.sb("lnsq", [128, 512]) for _ in range(2)], "tmp": [P.sb("lntmp", [128, 512]) for _ in range(2)],
            "mean": P.sb("lnmean", [128, 512]), "rstd": P.sb("lnrstd", [128, 512]),
            "psm": P.ps("psm", [128, 512]), "psv": P.ps("psv", [128, 512])}


def emit_conv3(C, P, asb, out, wv, bv, ncol, key_in, key_out):
    P.op("dve", lambda e: e.tensor_scalar(out=out, in0=asb[:, 1:ncol + 1], scalar1=wv[:, 1:2], scalar2=bv, op0=ALU.mult, op1=ALU.add),
         reads=[key_in], writes=[key_out])
    P.op("dve", lambda e: e.scalar_tensor_tensor(out=out, in0=asb[:, 0:ncol], scalar=wv[:, 0:1], in1=out, op0=ALU.mult, op1=ALU.add),
         reads=[key_in, key_out], writes=[key_out])
    P.op("dve", lambda e: e.scalar_tensor_tensor(out=out, in0=asb[:, 2:ncol + 2], scalar=wv[:, 2:3], in1=out, op0=ALU.mult, op1=ALU.add),
         reads=[key_in, key_out], writes=[key_out])


def emit_ffn(C, l):
    P = C.P
    W = C.W
    wup = kc_view(W["ffn_w_up"][l])
    wdn = W["ffn_w_down"][l].rearrange("(j p) m -> p j m", p=128)
    HL = 1024
    with P.phase():
        cw = P.sb("fcw", [128, NJ, 3])
        cb = P.sb("fcb", [128, NJ])
        for k in range(3):
            P.dma("sp", cw[:, :, k], W["ffn_conv_w"][l, k].rearrange("(j p) -> p j", p=128), writes=["fcw"], allow_slow_non_contiguous=True)
        P.dma("act", cb[:], W["ffn_conv_b"][l].rearrange("(j p) -> p j", p=128), writes=["fcw"], allow_slow_non_contiguous=True)
        hT = P.sb("hT", [128, NJ, HL], BF16)
        wa = [P.sb("wa", [128, 8, 128], BF16) for _ in range(2)]
        wg = [P.sb("wg", [128, 8, 128], BF16) for _ in range(2)]
        wd = [P.sb("wd", [128, NJ, 128], BF16) for _ in range(2)]
        asb = [P.sb("asb", [128, HL + 2]) for _ in range(2)]
        cv = [P.sb("cv", [128, HL]) for _ in range(2)]
        psa = [P.ps("psa", [128, 512]) for _ in range(2)]
        psg = [P.ps("psg", [128, 512]) for _ in range(2)]
        psh = P.ps("psh", [128, 512])
        psd = [P.ps("psd", [128, 512]) for _ in range(1)]
        T = {"sq": [cv[0][:, 0:512], cv[0][:, 512:1024]], "tmp": [cv[1][:, 0:512], cv[1][:, 512:1024]],
             "mean": asb[0][:, 0:512], "rstd": asb[0][:, 512:1024], "psm": psa[0], "psv": psa[1]}
        T = {k: (v if isinstance(v, list) else v) for k, v in T.items()}
        it = 0
        for half in range(2):
            t0 = half * HL
            for j in range(NJ):
                b = it % 2
                it += 1
                P.dma("pool", wa[b][:], wup[:, :, j * 128:(j + 1) * 128], writes=[("wa", b)])
                P.dma("pool", wg[b][:], wup[:, :, DFF + j * 128:DFF + (j + 1) * 128], writes=[("wg", b)])
                A = asb[b]
                hcols = []
                if half == 0:
                    P.op("pool", lambda e, A=A: e.memset(A[:, 0:1], 0.0), writes=[("asbh0", b)])
                    hcols.append((HL + 1, t0 + HL))
                else:
                    P.op("pool", lambda e, A=A: e.memset(A[:, HL + 1:HL + 2], 0.0), writes=[("asbh1", b)])
                    hcols.append((0, t0 - 1))
                for (dst, tok) in hcols:
                    for kc in range(8):
                        P.op("pe", lambda e, kc=kc, tok=tok, b=b: e.matmul(psh[:, 0:1], lhsT=wa[b][:, kc, :], rhs=C.xbf[:, kc, tok:tok + 1], start=(kc == 0), stop=(kc == 7)),
                             reads=[("wa", b), ("xbf", kc, tok // 512)], writes=["psh"])
                    P.op("act", lambda e, A=A, dst=dst: e.activation(out=A[:, dst:dst + 1], in_=psh[:, 0:1], func=AF.Copy),
                         reads=["psh"], writes=[("asbh%d" % (1 if dst > 0 else 0), b)])
                for q in range(2):
                    tt = half * 2 + q
                    for kc in range(8):
                        P.op("pe", lambda e, kc=kc, tt=tt, b=b, q=q: e.matmul(psa[q][:], lhsT=wa[b][:, kc, :], rhs=C.xbf[:, kc, tt * 512:(tt + 1) * 512], start=(kc == 0), stop=(kc == 7)),
                             reads=[("wa", b), ("xbf", kc, tt)], writes=[("psa", q)])
                    P.op("act", lambda e, A=A, q=q: e.activation(out=A[:, 1 + q * 512:1 + (q + 1) * 512], in_=psa[q][:], func=AF.Copy),
                         reads=[("psa", q)], writes=[("asb", b, q)])
                    for kc in range(8):
                        P.op("pe", lambda e, kc=kc, tt=tt, b=b, q=q: e.matmul(psg[q][:], lhsT=wg[b][:, kc, :], rhs=C.xbf[:, kc, tt * 512:(tt + 1) * 512], start=(kc == 0), stop=(kc == 7)),
                             reads=[("wg", b), ("xbf", kc, tt)], writes=[("psg", q)])
                kin = ("asball", b)
                P.op("dve", lambda e, b=b, j=j: e.tensor_scalar(out=cv[b][:], in0=asb[b][:, 1:HL + 1], scalar1=cw[:, j, 1:2], scalar2=cb[:, j:j + 1], op0=ALU.mult, op1=ALU.add),
                     reads=[("asb", b, 0), ("asb", b, 1), "fcw"], writes=[("cv", b)])
                P.op("dve", lambda e, b=b, j=j: e.scalar_tensor_tensor(out=cv[b][:], in0=asb[b][:, 0:HL], scalar=cw[:, j, 0:1], in1=cv[b][:], op0=ALU.mult, op1=ALU.add),
                     reads=[("asb", b, 0), ("asb", b, 1), ("asbh0", b), ("cv", b)], writes=[("cv", b)])
                P.op("dve", lambda e, b=b, j=j: e.scalar_tensor_tensor(out=cv[b][:], in0=asb[b][:, 2:HL + 2], scalar=cw[:, j, 2:3], in1=cv[b][:], op0=ALU.mult, op1=ALU.add),
                     reads=[("asb", b, 0), ("asb", b, 1), ("asbh1", b), ("cv", b)], writes=[("cv", b)])
                P.op("act", lambda e, b=b: e.activation(out=cv[b][:], in_=cv[b][:], func=AF.Gelu_apprx_tanh),
                     reads=[("cv", b)], writes=[("cv", b)])
                for q in range(2):
                    P.op("dve", lambda e, b=b, q=q, j=j: e.tensor_tensor(out=hT[:, j, q * 512:(q + 1) * 512], in0=cv[b][:, q * 512:(q + 1) * 512], in1=psg[q][:], op=ALU.mult),
                         reads=[("cv", b), ("psg", q)], writes=[("hT", j)])
            for m in range(8):
                b = m % 2
                P.dma("pool", wd[b][:], wdn[:, :, m * 128:(m + 1) * 128], writes=[("wd", b)])
                for q in range(2):
                    tt = half * 2 + q
                    for j in range(NJ):
                        P.op("pe", lambda e, j=j, b=b, q=q: e.matmul(psd[0][:], lhsT=wd[b][:, j, :], rhs=hT[:, j, q * 512:(q + 1) * 512], start=(j == 0), stop=(j == NJ - 1)),
                             reads=[("wd", b), ("hT", j)], writes=["psd"])
                    P.op("dve", lambda e, m=m, tt=tt: e.scalar_tensor_tensor(out=C.xres[:, m, tt * 512:(tt + 1) * 512], in0=C.xres[:, m, tt * 512:(tt + 1) * 512], scalar=ALPHA, in1=psd[0][:], op0=ALU.mult, op1=ALU.add),
                         reads=["psd", ("xres", m, tt)], writes=[("xres", m, tt)])
        P.barrier()
        for tt in range(NTT):
            emit_ln(C, l, 1, tt, T)


def emit_mixer_out(C, l, yT, wout_dram, nk, T):
    P = C.P
    wv = wout_dram.rearrange("(kc k) m -> k kc m", k=128)
    wo = [P.sb("wo", [128, nk, 128], BF16) for _ in range(2)]
    pso = [P.ps("pso", [128, 512]) for _ in range(2)]
    for m in range(8):
        b = m % 2
        P.dma("pool", wo[b][:], wv[:, :, m * 128:(m + 1) * 128], writes=[("wo", b)])
        for tt in range(NTT):
            pb = pso[tt % 2]
            for k in range(nk):
                P.op("pe", lambda e, k=k, b=b, tt=tt, pb=pb: e.matmul(pb[:], lhsT=wo[b][:, k, :], rhs=yT[:, k, tt * 512:(tt + 1) * 512], start=(k == 0), stop=(k == nk - 1)),
                     reads=[("wo", b), ("yT", k)], writes=[("pso", tt % 2)])
            P.op("dve", lambda e, m=m, tt=tt, pb=pb: e.scalar_tensor_tensor(out=C.xres[:, m, tt * 512:(tt + 1) * 512], in0=C.xres[:, m, tt * 512:(tt + 1) * 512], scalar=ALPHA, in1=pb[:], op0=ALU.mult, op1=ALU.add),
                 reads=[("pso", tt % 2), ("xres", m, tt)], writes=[("xres", m, tt)])
    for tt in range(NTT):
        emit_ln(C, l, 0, tt, T)


def emit_mlstm(C, l):
    P = C.P
    W = C.W
    j = l // 2
    win = kc_view(W["od_w_in"][j])
    SCALE = 128.0 ** -0.5
    RB = (0, 32)
    with P.phase():
        hTf = P.sb("hTfin", [128, 8, L], BF16)
        with P.phase():
            gb = P.sb("gb", [40, 2])
            for d in range(2):
                P.dma("sp", gb[RB[d]:RB[d] + 8, :], W["od_gate_b"][j][16 * d:16 * d + 16].rearrange("(q h) -> h q", h=8), writes=["gb"], allow_slow_non_contiguous=True)
            cwq = P.sb("mcw", [128, 16, 3])
            cbq = P.sb("mcb", [128, 16])
            for k in range(3):
                P.dma("sp", cwq[:, :, k], W["od_conv_w"][j, k].rearrange("(c p) -> p c", p=128), writes=["mcw"], allow_slow_non_contiguous=True)
            P.dma("act", cbq[:], W["od_conv_b"][j].rearrange("(c p) -> p c", p=128), writes=["mcw"], allow_slow_non_contiguous=True)
            ng = P.sb("ng", [128, 8])
            P.dma("sp", ng[:], W["od_norm_g"][j].rearrange("(c p) -> p c", p=128), writes=["ng"], allow_slow_non_contiguous=True)
            tri = P.sb("tri", [128, 2, 128], BF16)
            P.dma("pool", tri[:], C.consts["tri"], writes=["tri"])
            nmrel = P.sb("gnm", [40, L])
            negm = P.sb("gm", [40, L])
            colT = P.sb("colT", [128, 2, NB, 8])
            with P.phase():
                gw = P.sb("gw", [128, 8, 32], BF16)
                P.dma("pool", gw[:], win[:, :, 4096:4128], writes=["gw"])
                ci = P.sb("gci", [40, L])
                t1 = P.sb("gt1", [40, L])
                t2 = P.sb("gt2", [40, L])
                tot = P.sb("gtot", [40, 1])
                psg = [P.ps("psgate", [128, 512]) for _ in range(2)]
                pct = P.ps("pspct", [128, 2, NB, 16])
                n = 0
                for q in range(4):
                    d = q // 2
                    dst = ci if q % 2 == 0 else nmrel
                    for tt in range(NTT):
                        pb = psg[n % 2]
                        for kc in range(8):
                            P.op("pe", lambda e, kc=kc, q=q, tt=tt, pb=pb, d=d: e.matmul(pb[RB[d]:RB[d] + 8, :], lhsT=gw[:, kc, q * 8:(q + 1) * 8], rhs=C.xbf[:, kc, tt * 512:(tt + 1) * 512], start=(kc == 0), stop=(kc == 7)),
                                 reads=["gw", ("xbf", kc, tt)], writes=[("psgate", n % 2)])
                        P.op("act", lambda e, q=q, tt=tt, pb=pb, d=d, dst=dst: e.activation(out=dst[RB[d]:RB[d] + 8, tt * 512:(tt + 1) * 512], in_=pb[RB[d]:RB[d] + 8, :], func=AF.Identity, bias=gb[RB[d]:RB[d] + 8, (q % 2):(q % 2) + 1], scale=1.0),
                             reads=[("psgate", n % 2), "gb"], writes=[("gpre", q)])
                        n += 1
                for d in range(2):
                    r = slice(RB[d], RB[d] + 8)
                    kg, kf = ("gpre", 2 * d), ("gpre", 2 * d + 1)
                    K = lambda s, d=d: (s, d)
                    P.op("act", lambda e, r=r: e.activation(out=t1[r, :], in_=nmrel[r, :], func=AF.Exp, scale=-1.0), reads=[kf], writes=[K("t1")])
                    P.op("act", lambda e, r=r: e.activation(out=nmrel[r, :], in_=t1[r, :], func=AF.Ln, bias=1.0, scale=1.0), reads=[K("t1")], writes=[K("lf")])
                    P.op("dve", lambda e, r=r: e.tensor_tensor_scan(out=negm[r, :], data0=C.one8[r, 0:1].to_broadcast([8, L]), data1=nmrel[r, :], initial=0.0, op0=ALU.mult, op1=ALU.add),
                         reads=[K("lf"), "one8"], writes=[K("G")])
                    if d == 1:
                        P.op("dve", lambda e, r=r: e.tensor_copy(out=tot[r, :], in_=negm[r, L - 1:L]), reads=[K("G")], writes=[K("tot")])
                        P.op("dve", lambda e, r=r: e.tensor_tensor(out=t1[r, :], in0=nmrel[r, :], in1=negm[r, :], op=ALU.subtract), reads=[K("lf"), K("G"), K("t1")], writes=[K("t1")])
                        P.op("dve", lambda e, r=r: e.tensor_scalar(out=negm[r, :], in0=t1[r, :], scalar1=tot[r, 0:1], scalar2=None, op0=ALU.add), reads=[K("t1"), K("tot")], writes=[K("G")])
                    P.op("dve", lambda e, r=r: e.tensor_tensor(out=ci[r, :], in0=ci[r, :], in1=negm[r, :], op=ALU.add), reads=[kg, K("G")], writes=[K("c")])
                    if d == 0:
                        P.op("dve", lambda e, r=r: e.tensor_tensor_scan(out=t1[r, :], data0=C.one8[r, 0:1].to_broadcast([8, L]), data1=ci[r, :], initial=-1e30, op0=ALU.mult, op1=ALU.max),
                             reads=[K("c"), "one8", K("t1")], writes=[K("t1")])
                        cm, kcm = t1, K("t1")
                    else:
                        P.op("dve", lambda e, r=r: e.tensor_copy(out=t1[r, :], in_=ci[r, :]), reads=[K("c"), K("t1")], writes=[K("t1")])
                        src, dst, ks, kd = t1, t2, K("t1"), K("t2")
                        s = 1
                        while s < L:
                            P.op("dve", lambda e, src=src, dst=dst, s=s, r=r: e.tensor_tensor(out=dst[r, 0:L - s], in0=src[r, 0:L - s], in1=src[r, s:L], op=ALU.max), reads=[ks], writes=[kd])
                            P.op("dve", lambda e, src=src, dst=dst, s=s, r=r: e.tensor_copy(out=dst[r, L - s:L], in_=src[r, L - s:L]), reads=[ks], writes=[kd])
                            src, dst, ks, kd = dst, src, kd, ks
                            s *= 2
                        cm, kcm = src, ks
                    P.op("dve", lambda e, cm=cm, r=r: e.tensor_scalar(out=nmrel[r, :], in0=cm[r, :], scalar1=-1.0, scalar2=0.0, op0=ALU.mult, op1=ALU.min),
                         reads=[kcm, K("lf")], writes=[("gnm", d)])
                    P.op("dve", lambda e, r=r: e.tensor_tensor(out=negm[r, :], in0=negm[r, :], in1=nmrel[r, :], op=ALU.add), reads=[K("G"), ("gnm", d)], writes=[("gm", d)])
                    for nb in range(NB):
                        P.op("pe", lambda e, d=d, nb=nb, r=r: e.transpose(pct[:, d, nb, 0:8], ci[r, nb * 128:(nb + 1) * 128], C.ident[r, r]),
                             reads=[K("c"), "ident"], writes=["pspct"])
                P.op("dve", lambda e: e.tensor_copy(out=colT[:], in_=pct[:, :, :, 0:8]), reads=["pspct"], writes=["colT"])
            wb = [P.sb("wbuf", [128, 8, 128], BF16) for _ in range(2)]
            qT = P.sb("qT", [128, L], BF16)
            kT = P.sb("kT", [128, L], BF16)
            vt = P.sb("vt", [128, NB, 128], BF16)
            sgo = P.sb("sgo", [128, 512], BF16)
            bufA = P.sb("bufA", [128, L + 2])
            bufB = P.sb("bufB", [128, L])
            Dt = [P.sb("Dt", [128, 512], BF16) for _ in range(2)]
            Pt = [P.sb("Pt", [128, 512], BF16) for _ in range(2)]
            asb, cvt = bufA, bufB
            Rbc = [bufB[:, 0:512], bufB[:, 512:1024]]
            Mex = [bufB[:, 1024:1536], bufB[:, 1536:2048]]
            hsum, e1, e2 = bufA[:, 0:512], bufA[:, 512:1024], bufA[:, 1024:1536]
            pss = [P.ps("pss", [128, 512]) for _ in range(2)]
            psn = P.ps("psn", [128, 512])
            psdn = P.ps("psdn", [128, 512])
            psx = [P.ps("psx", [128, 512]) for _ in range(2)]
            nx = 0
            nw = 0
            for h in range(8):
                P.op("pool", lambda e: e.memset(asb[:, 0:1], 0.0), writes=["masbh"])
                P.op("pool", lambda e: e.memset(asb[:, L + 1:L + 2], 0.0), writes=["masbh"])
                for (col0, dstT, ci_, kd_) in ((h * 128, qT, h, "qT"), (1024 + h * 128, kT, 8 + h, "kT")):
                    wt = wb[nw % 2]
                    wkey = ("wbuf", nw % 2)
                    nw += 1
                    P.dma("pool", wt[:], win[:, :, col0:col0 + 128], writes=[wkey])
                    for tt in range(NTT):
                        pb = psx[nx % 2]
                        for kc in range(8):
                            P.op("pe", lambda e, kc=kc, tt=tt, pb=pb, wt=wt: e.matmul(pb[:], lhsT=wt[:, kc, :], rhs=C.xbf[:, kc, tt * 512:(tt + 1) * 512], start=(kc == 0), stop=(kc == 7)),
                                 reads=[wkey, ("xbf", kc, tt)], writes=[("psx", nx % 2)])
                        P.op("act", lambda e, tt=tt, pb=pb: e.activation(out=asb[:, 1 + tt * 512:1 + (tt + 1) * 512], in_=pb[:], func=AF.Copy),
                             reads=[("psx", nx % 2)], writes=["masb"])
                        nx += 1
                    P.op("dve", lambda e, ci_=ci_: e.tensor_scalar(out=cvt[:], in0=asb[:, 1:L + 1], scalar1=cwq[:, ci_, 1:2], scalar2=cbq[:, ci_:ci_ + 1], op0=ALU.mult, op1=ALU.add),
                         reads=["masb", "mcw"], writes=["mcv"])
                    P.op("dve", lambda e, ci_=ci_: e.scalar_tensor_tensor(out=cvt[:], in0=asb[:, 0:L], scalar=cwq[:, ci_, 0:1], in1=cvt[:], op0=ALU.mult, op1=ALU.add),
                         reads=["masb", "masbh", "mcv"], writes=["mcv"])
                    P.op("dve", lambda e, ci_=ci_: e.scalar_tensor_tensor(out=cvt[:], in0=asb[:, 2:L + 2], scalar=cwq[:, ci_, 2:3], in1=cvt[:], op0=ALU.mult, op1=ALU.add),
                         reads=["masb", "masbh", "mcv"], writes=["mcv"])
                    P.op("act", lambda e, dstT=dstT: e.activation(out=dstT[:], in_=cvt[:], func=AF.Silu), reads=["mcv"], writes=[kd_])
                wt = wb[nw % 2]
                wkey = ("wbuf", nw % 2)
                nw += 1
                P.dma("pool", wt[:], win[:, :, 2048 + h * 128:2048 + (h + 1) * 128], writes=[wkey])
                for nb in range(NB):
                    pb = psx[nx % 2]
                    for kc in range(8):
                        P.op("pe", lambda e, kc=kc, nb=nb, pb=pb, wt=wt: e.matmul(pb[:, 0:128], lhsT=C.xbf[:, kc, nb * 128:(nb + 1) * 128], rhs=wt[:, kc, :], start=(kc == 0), stop=(kc == 7)),
                             reads=[wkey, ("xbf", kc, nb // 4)], writes=[("psx", nx % 2)])
                    if nb % 2 == 0:
                        P.op("act", lambda e, nb=nb, pb=pb: e.activation(out=vt[:, nb, :], in_=pb[:, 0:128], func=AF.Copy), reads=[("psx", nx % 2)], writes=["vt"])
                    else:
                        P.op("dve", lambda e, nb=nb, pb=pb: e.tensor_copy(out=vt[:, nb, :], in_=pb[:, 0:128]), reads=[("psx", nx % 2)], writes=["vt"])
                    nx += 1
                wo = wb[nw % 2]
                wokey = ("wbuf", nw % 2)
                nw += 1
                P.dma("pool", wo[:], win[:, :, 3072 + h * 128:3072 + (h + 1) * 128], writes=[wokey])
                P.barrier()
                it = 0
                for tt in range(NTT):
                    ts = slice(tt * 512, (tt + 1) * 512)
                    for d in range(2):
                        r = slice(RB[d], RB[d] + 8)
                        pb = psx[nx % 2]
                        P.op("pe", lambda e, ts=ts, pb=pb, h=h, r=r: e.matmul(pb[:], lhsT=C.sel[r, h, :], rhs=nmrel[r, ts], start=True, stop=True),
                             reads=["sel", ("gnm", d)], writes=[("psx", nx % 2)])
                        P.op("dve", lambda e, d=d, pb=pb: e.tensor_copy(out=Rbc[d], in_=pb[:]), reads=[("psx", nx % 2)], writes=[("Rbc", d)])
                        nx += 1
                        pb = psx[nx % 2]
                        P.op("pe", lambda e, ts=ts, pb=pb, h=h, r=r: e.matmul(pb[:], lhsT=C.sel[r, h, :], rhs=negm[r, ts], start=True, stop=True),
                             reads=["sel", ("gm", d)], writes=[("psx", nx % 2)])
                        P.op("act", lambda e, d=d, pb=pb: e.activation(out=Mex[d], in_=pb[:], func=AF.Exp), reads=[("psx", nx % 2)], writes=[("Mex", d)])
                        nx += 1
                        if d == 0:
                            jl = list(range(0, 4 * tt + 4))
                        else:
                            jl = list(range(NB - 1, 4 * tt - 1, -1))
                        def front(ji, jb, b):
                            jj = jb - 4 * tt
                            if d == 0:
                                c0, c1 = (max(jj, 0) * 128, 512)
                                dc = c0 if 0 <= jj < 4 else None
                            else:
                                c0, c1 = (0, (min(jj, 3) + 1) * 128)
                                dc = c1 - 128 if 0 <= jj < 4 else None
                            cs = slice(c0, c1)
                            qs = slice(tt * 512 + c0, tt * 512 + c1)
                            P.op("pe", lambda e, jb=jb, b=b, cs=cs, qs=qs: e.matmul(pss[b][:, cs], lhsT=kT[:, jb * 128:(jb + 1) * 128], rhs=qT[:, qs], start=True, stop=True),
                                 reads=["qT", "kT"], writes=[("pss", b)])
                            P.op("act", lambda e, jb=jb, b=b, d=d, h=h, cs=cs: e.activation(out=Dt[b][:, cs], in_=Rbc[d][:, cs], func=AF.Exp, bias=colT[:, d, jb, h:h + 1], scale=1.0),
                                 reads=[("Rbc", d), "colT"], writes=[("Dt", b)])
                            if dc is not None:
                                P.op("pool", lambda e, b=b, d=d, dc=dc: e.tensor_tensor(out=Dt[b][:, dc:dc + 128], in0=Dt[b][:, dc:dc + 128], in1=tri[:, d, :], op=ALU.mult),
                                     reads=[("Dt", b), "tri"], writes=[("Dt", b)])
                            P.op("dve", lambda e, b=b, cs=cs: e.scalar_tensor_tensor(out=Pt[b][:, cs], in0=pss[b][:, cs], scalar=SCALE, in1=Dt[b][:, cs], op0=ALU.mult, op1=ALU.mult),
                                 reads=[("pss", b), ("Dt", b)], writes=[("Pt", b)])
                            return cs

                        def back(ji, jb, b, cs):
                            P.op("pe", lambda e, jb=jb, b=b, ji=ji, jl=jl, cs=cs: e.matmul(psn[:, cs], lhsT=vt[:, jb, :], rhs=Pt[b][:, cs], start=(ji == 0), stop=(ji == len(jl) - 1)),
                                 reads=["vt", ("Pt", b)], writes=["psn"])
                            P.op("pe", lambda e, jb=jb, b=b, ji=ji, jl=jl, cs=cs: e.matmul(psdn[:, cs], lhsT=C.onesb[:], rhs=Pt[b][:, cs], start=(ji == 0), stop=(ji == len(jl) - 1)),
                                 reads=["onesb", ("Pt", b)], writes=["psdn"])

                        bufs = [(it + ji) % 2 for ji in range(len(jl))]
                        it += len(jl)
                        csl = {0: front(0, jl[0], bufs[0])}
                        for ji, jb in enumerate(jl):
                            if ji + 1 < len(jl):
                                csl[ji + 1] = front(ji + 1, jl[ji + 1], bufs[ji + 1])
                            back(ji, jb, bufs[ji], csl[ji])
                        P.op("act", lambda e: e.activation(out=e1, in_=psdn[:], func=AF.Abs), reads=["psdn"], writes=["e1"])
                        P.op("dve", lambda e, d=d: e.tensor_tensor(out=e1, in0=e1, in1=Mex[d], op=ALU.max), reads=["e1", ("Mex", d)], writes=["e1"])
                        P.op("dve", lambda e: e.reciprocal(out=e1, in_=e1), reads=["e1"], writes=["e1"])
                        if d == 0:
                            P.op("dve", lambda e: e.tensor_tensor(out=hsum, in0=psn[:], in1=e1, op=ALU.mult), reads=["psn", "e1"], writes=["hsum"])
                        else:
                            P.op("dve", lambda e: e.tensor_tensor(out=e2, in0=psn[:], in1=e1, op=ALU.mult), reads=["psn", "e1"], writes=["e2"])
                            P.op("pool", lambda e: e.tensor_tensor(out=hsum, in0=hsum, in1=e2, op=ALU.add), reads=["e2", "hsum"], writes=["hsum"])
                    pb = psx[nx % 2]
                    for kc in range(8):
                        P.op("pe", lambda e, kc=kc, ts=ts, pb=pb, wo=wo: e.matmul(pb[:], lhsT=wo[:, kc, :], rhs=C.xbf[:, kc, ts], start=(kc == 0), stop=(kc == 7)),
                             reads=[wokey, ("xbf", kc, tt)], writes=[("psx", nx % 2)])
                    P.op("act", lambda e, pb=pb: e.activation(out=sgo[:], in_=pb[:], func=AF.Sigmoid), reads=[("psx", nx % 2)], writes=["sgo"])
                    nx += 1
                    P.op("pe", lambda e: e.matmul(psn[:], lhsT=C.onesf[:], rhs=hsum, start=True, stop=True), reads=["hsum", "onesf"], writes=["psn"])
                    P.op("act", lambda e: e.activation(out=e1, in_=hsum, func=AF.Square), reads=["hsum", "e1"], writes=["e1"])
                    P.op("pe", lambda e: e.matmul(psdn[:], lhsT=C.onesf[:], rhs=e1, start=True, stop=True), reads=["e1", "onesf"], writes=["psdn"])
                    P.op("act", lambda e: e.activation(out=e2, in_=psn[:], func=AF.Copy), reads=["psn", "e2"], writes=["e2"])
                    P.op("dve", lambda e: e.tensor_tensor(out=e1, in0=e2, in1=e2, op=ALU.mult), reads=["e2", "e1"], writes=["e1"])
                    P.op("dve", lambda e: e.tensor_tensor(out=e1, in0=psdn[:], in1=e1, op=ALU.subtract), reads=["psdn", "e1"], writes=["e1"])
                    P.op("act", lambda e: e.activation(out=e1, in_=e1, func=AF.Sqrt, bias=C.epsc[:, 0:1], scale=1.0), reads=["e1", "epsc"], writes=["e1"])
                    P.op("dve", lambda e: e.reciprocal(out=e1, in_=e1), reads=["e1"], writes=["e1"])
                    P.op("pool", lambda e: e.tensor_tensor(out=e2, in0=hsum, in1=e2, op=ALU.subtract), reads=["hsum", "e2"], writes=["e2"])
                    P.op("dve", lambda e: e.tensor_tensor(out=e2, in0=e2, in1=e1, op=ALU.mult), reads=["e1", "e2"], writes=["e2"])
                    P.op("dve", lambda e, ts=ts, h=h: e.scalar_tensor_tensor(out=hTf[:, h, ts], in0=e2, scalar=ng[:, h:h + 1], in1=sgo[:], op0=ALU.mult, op1=ALU.mult),
                         reads=["e2", "ng", "sgo"], writes=[("yT", h)])
                P.barrier()
        with P.phase():
            T = ln_temps(P)
            emit_mixer_out(C, l, hTf, W["od_w_out"][j], 8, T)


def emit_even(C, l):
    P = C.P
    W = C.W
    j = l // 2
    win = kc_view(W["ev_w_in"][j])
    PI = math.pi
    with P.phase():
        yA = P.sb("yA", [128, 4, L], BF16)
        hS = P.sb("hS", [128, 4, L], BF16)
        with P.phase():
            qT = P.sb("qT", [128, 4, L], BF16)
            kT = P.sb("kT", [128, L], BF16)
            vt = P.sb("vt", [128, NB, 128], BF16)
            ab8 = P.sb("ab8", [128, 8, 3, 128], BF16)
            P.dma("pool", ab8[:], C.consts["abias"], writes=["ab8"])
            esk = P.sb("esk", [128, 4])
            for hf in range(2):
                P.dma("sp", esk[hf * 64:(hf + 1) * 64, :], W["ev_sink"][j:j + 1, hf * 4:hf * 4 + 4].partition_broadcast(64), writes=["esk"])
            P.op("act", lambda e: e.activation(out=esk[:], in_=esk[:], func=AF.Exp), reads=["esk"], writes=["esk"])
            wb = [P.sb("wbuf", [128, 8, 128], BF16) for _ in range(2)]
            psx = [P.ps("psx", [128, 512]) for _ in range(2)]
            nx = 0
            nw = 0
            wq_view = win[:, :, 0:512].rearrange("k kc (hf c d) -> k kc c hf d", hf=2, c=4)
            for c in range(5):
                wt = wb[nw % 2]
                wkey = ("wbuf", nw % 2)
                nw += 1
                if c < 4:
                    for kc in range(8):
                        P.dma("pool", wt[:, kc, :].rearrange("k (hf d) -> k hf d", hf=2), wq_view[:, kc, c, :, :], writes=[wkey])
                    dst = qT[:, c, :]
                else:
                    P.dma("pool", wt[:], win[:, :, 512:640], writes=[wkey])
                    dst = kT[:, :]
                for tt in range(NTT):
                    pb = psx[nx % 2]
                    for kc in range(8):
                        P.op("pe", lambda e, kc=kc, tt=tt, pb=pb, wt=wt: e.matmul(pb[:], lhsT=wt[:, kc, :], rhs=C.xbf[:, kc, tt * 512:(tt + 1) * 512], start=(kc == 0), stop=(kc == 7)),
                             reads=[wkey, ("xbf", kc, tt)], writes=[("psx", nx % 2)])
                    if tt % 2 == 0:
                        P.op("act", lambda e, tt=tt, pb=pb, dst=dst: e.activation(out=dst[:, tt * 512:(tt + 1) * 512], in_=pb[:], func=AF.Copy), reads=[("psx", nx % 2)], writes=["qk"])
                    else:
                        P.op("dve", lambda e, tt=tt, pb=pb, dst=dst: e.tensor_copy(out=dst[:, tt * 512:(tt + 1) * 512], in_=pb[:]), reads=[("psx", nx % 2)], writes=["qk"])
                    nx += 1
            wt = wb[nw % 2]
            wkey = ("wbuf", nw % 2)
            nw += 1
            P.dma("pool", wt[:], win[:, :, 640:768], writes=[wkey])
            for nb in range(NB):
                pb = psx[nx % 2]
                for kc in range(8):
                    P.op("pe", lambda e, kc=kc, nb=nb, pb=pb, wt=wt: e.matmul(pb[:, 0:128], lhsT=C.xbf[:, kc, nb * 128:(nb + 1) * 128], rhs=wt[:, kc, :], start=(kc == 0), stop=(kc == 7)),
                         reads=[wkey, ("xbf", kc, nb // 4)], writes=[("psx", nx % 2)])
                if nb % 2 == 0:
                    P.op("act", lambda e, nb=nb, pb=pb: e.activation(out=vt[:, nb, :], in_=pb[:, 0:128], func=AF.Copy), reads=[("psx", nx % 2)], writes=["vt"])
                else:
                    P.op("dve", lambda e, nb=nb, pb=pb: e.tensor_copy(out=vt[:, nb, :], in_=pb[:, 0:128]), reads=[("psx", nx % 2)], writes=["vt"])
                nx += 1
            PT = [P.sb("PT", [128, 3, 128], BF16) for _ in range(2)]
            dn = P.sb("dn", [128, 128])
            pss = [P.ps("pss", [128, 512]) for _ in range(2)]
            psn = P.ps("psn", [128, 512])
            psdn = P.ps("psdn", [128, 512])
            it = 0
            for c in range(4):
                for qb in range(NB):
                    qs = slice(qb * 128, (qb + 1) * 128)
                    for hf in range(2):
                        h = hf * 4 + c
                        r = slice(hf * 64, (hf + 1) * 64)
                        b = it % 2
                        it += 1
                        rl = [r3 for r3 in range(3) if 0 <= qb + r3 - 1 < NB]
                        for r3 in rl:
                            kb = qb + r3 - 1
                            P.op("pe", lambda e, b=b, r3=r3, kb=kb, r=r, c=c, qs=qs: e.matmul(pss[b][:, r3 * 128:(r3 + 1) * 128], lhsT=kT[r, kb * 128:(kb + 1) * 128], rhs=qT[r, c, qs], start=True, stop=False),
                                 reads=["qk"], writes=[("pss", b)])
                            P.op("pe", lambda e, b=b, r3=r3, h=h: e.matmul(pss[b][:, r3 * 128:(r3 + 1) * 128], lhsT=C.identb[:], rhs=ab8[:, h, r3, :], start=False, stop=True),
                                 reads=["identb", "ab8"], writes=[("pss", b)])
                        c0, c1 = rl[0] * 128, (rl[-1] + 1) * 128
                        P.op("act", lambda e, b=b, c0=c0, c1=c1: e.activation(out=PT[b][:].rearrange("p a b -> p (a b)")[:, c0:c1], in_=pss[b][:, c0:c1], func=AF.Exp, scale=0.125),
                             reads=[("pss", b)], writes=[("PT", b)])
                        for i3, r3 in enumerate(rl):
                            kb = qb + r3 - 1
                            P.op("pe", lambda e, b=b, r3=r3, kb=kb, r=r, hf=hf, i3=i3, rl=rl: e.matmul(psn[r, 0:128], lhsT=vt[:, kb, hf * 64:(hf + 1) * 64], rhs=PT[b][:, r3, :], start=(i3 == 0), stop=(i3 == len(rl) - 1)),
                                 reads=["vt", ("PT", b)], writes=["psn"])
                        for i3, r3 in enumerate(rl):
                            P.op("pe", lambda e, b=b, r3=r3, r=r, i3=i3, rl=rl: e.matmul(psdn[r, 0:128], lhsT=C.onesb[:, 0:64], rhs=PT[b][:, r3, :], start=(i3 == 0), stop=(i3 == len(rl) - 1)),
                                 reads=["onesb", ("PT", b)], writes=["psdn"])
                    P.op("dve", lambda e, c=c: e.tensor_scalar(out=dn[:], in0=psdn[:, 0:128], scalar1=esk[:, c:c + 1], scalar2=None, op0=ALU.add), reads=["psdn", "esk"], writes=["dn"])
                    P.op("dve", lambda e: e.reciprocal(out=dn[:], in_=dn[:]), reads=["dn"], writes=["dn"])
                    P.op("dve", lambda e, c=c, qs=qs: e.tensor_tensor(out=yA[:, c, qs], in0=psn[:, 0:128], in1=dn[:], op=ALU.mult), reads=["psn", "dn"], writes=[("yA", c)])
        with P.phase():
            NPW = 22
            pw_exp = list(range(1, 17)) + [32, 64, 128, 256, 512, 1024]
            pwr = P.sb("pwr", [128, 32, NPW])
            pwi = P.sb("pwi", [128, 32, NPW])
            pni = P.sb("pni", [128, 32, NPW])
            BTp = P.sb("BTp", [128, 4, 2, 2, 2, 128], BF16)
            CTp = P.sb("CTp", [128, 16, 2, 2, 64], BF16)
            dsk = P.sb("dsk", [128, 4])
            P.dma("sp", dsk[:], W["s5_d"][j].rearrange("(c p) -> p c", p=128), writes=["dsk"], allow_slow_non_contiguous=True)
            with P.phase():
                lin = P.sb("lin", [32, 2, 128])
                P.dma("sp", lin[:, 0, :], W["s5_a_re"][j].rearrange("d (gp g2) p -> (d gp) (g2 p)", g2=2), writes=["lin"])
                P.dma("act", lin[:, 1, :], W["s5_a_im"][j].rearrange("d (gp g2) p -> (d gp) (g2 p)", g2=2), writes=["lin"])
                pst = P.ps("pst", [128, 512])
                aT = P.sb("aT", [128, 2, 32])
                for i in range(2):
                    P.op("pe", lambda e, i=i: e.transpose(pst[:, i * 32:(i + 1) * 32], lin[:, i, :], C.ident[0:32, 0:32]), reads=["lin", "ident"], writes=["pst"])
                P.op("dve", lambda e: e.tensor_copy(out=aT[:].rearrange("p a b -> p (a b)"), in_=pst[:, 0:64]), reads=["pst"], writes=["aT"])
                dt = P.sb("dt", [128, 32])
                for d in range(2):
                    for g2 in range(2):
                        src = W["s5_log_dt"][j, d:d + 1].rearrange("o (gp g2) -> o g2 gp", g2=2)[:, g2, :]
                        P.dma("sp", dt[g2 * 64:(g2 + 1) * 64, d * 16:(d + 1) * 16], src.partition_broadcast(64), writes=["dt"], allow_slow_non_contiguous=True)
                P.op("act", lambda e: e.activation(out=dt[:], in_=dt[:], func=AF.Exp), reads=["dt"], writes=["dt"])
                tA = [P.sb("tA%d" % i, [128, 32]) for i in range(8)]
                mag, ang, cs, sn, zr, zi, t6, t7 = tA
                ar, ai = aT[:, 0, :], aT[:, 1, :]
                TT = lambda out, a, b, op, rk, wk: P.op("dve", lambda e: e.tensor_tensor(out=out, in0=a, in1=b, op=op), reads=rk, writes=wk)
                TT(mag[:], ar, dt[:], ALU.mult, ["aT", "dt"], ["mag"])
                P.op("act", lambda e: e.activation(out=mag[:], in_=mag[:], func=AF.Exp), reads=["mag"], writes=["mag"])
                TT(ang[:], ai, dt[:], ALU.mult, ["aT", "dt"], ["ang"])
                ki = P.sb("ki", [128, 32], mybir.dt.int32)
                kf = P.sb("kf", [128, 32])
                for (dst, shift, key) in ((sn, 0.0, "sn"), (cs, 0.5 * PI, "cs")):
                    P.op("dve", lambda e, shift=shift: e.tensor_scalar(out=kf[:], in0=ang[:], scalar1=shift, scalar2=1.0 / (2 * PI), op0=ALU.add, op1=ALU.mult), reads=["ang", "kf"], writes=["kf"])
                    P.op("dve", lambda e: e.tensor_copy(out=ki[:], in_=kf[:]), reads=["kf", "ki"], writes=["ki"])
                    P.op("dve", lambda e: e.tensor_copy(out=kf[:], in_=ki[:]), reads=["ki"], writes=["kf"])
                    P.op("dve", lambda e, dst=dst, shift=shift: e.tensor_scalar(out=dst[:], in0=ang[:], scalar1=shift, scalar2=None, op0=ALU.add), reads=["ang"], writes=[key])
                    P.op("dve", lambda e, dst=dst: e.scalar_tensor_tensor(out=dst[:], in0=kf[:], scalar=-2 * PI, in1=dst[:], op0=ALU.mult, op1=ALU.add), reads=["kf", key], writes=[key])
                    P.op("dve", lambda e, dst=dst: e.tensor_scalar(out=kf[:], in0=dst[:], scalar1=PI, scalar2=-2 * PI, op0=ALU.is_gt, op1=ALU.mult), reads=[key, "kf"], writes=["kf"])
                    P.op("dve", lambda e, dst=dst: e.tensor_tensor(out=dst[:], in0=dst[:], in1=kf[:], op=ALU.add), reads=["kf", key], writes=[key])
                    P.op("dve", lambda e, dst=dst: e.tensor_scalar(out=kf[:], in0=dst[:], scalar1=-PI, scalar2=2 * PI, op0=ALU.is_lt, op1=ALU.mult), reads=[key, "kf"], writes=["kf"])
                    P.op("dve", lambda e, dst=dst: e.tensor_tensor(out=dst[:], in0=dst[:], in1=kf[:], op=ALU.add), reads=["kf", key], writes=[key])
                P.op("act", lambda e: e.activation(out=sn[:], in_=sn[:], func=AF.Sin), reads=["sn"], writes=["sn"])
                P.op("act", lambda e: e.activation(out=cs[:], in_=cs[:], func=AF.Sin), reads=["cs"], writes=["cs"])
                TT(pwr[:, :, 0], mag[:], cs[:], ALU.mult, ["mag", "cs"], ["pw"])
                TT(pwi[:, :, 0], mag[:], sn[:], ALU.mult, ["mag", "sn"], ["pw"])
                P.op("dve", lambda e: e.tensor_scalar(out=t6[:], in0=pwr[:, :, 0], scalar1=-1.0, scalar2=None, op0=ALU.add), reads=["pw"], writes=["t6"])
                TT(zr[:], t6[:], ar, ALU.mult, ["t6", "aT"], ["zr"])
                TT(t7[:], pwi[:, :, 0], ai, ALU.mult, ["pw", "aT"], ["t7"])
                TT(zr[:], zr[:], t7[:], ALU.add, ["zr", "t7"], ["zr"])
                TT(zi[:], pwi[:, :, 0], ar, ALU.mult, ["pw", "aT"], ["zi"])
                TT(t7[:], t6[:], ai, ALU.mult, ["t6", "aT", "t7"], ["t7"])
                TT(zi[:], zi[:], t7[:], ALU.subtract, ["zi", "t7"], ["zi"])
                TT(t6[:], ar, ar, ALU.mult, ["aT", "t6"], ["t6"])
                TT(t7[:], ai, ai, ALU.mult, ["aT", "t7"], ["t7"])
                TT(t6[:], t6[:], t7[:], ALU.add, ["t6", "t7"], ["t6"])
                P.op("dve", lambda e: e.reciprocal(out=t6[:], in_=t6[:]), reads=["t6"], writes=["t6"])
                TT(zr[:], zr[:], t6[:], ALU.mult, ["zr", "t6"], ["zr"])
                TT(zi[:], zi[:], t6[:], ALU.mult, ["zi", "t6"], ["zi"])
                nzr, nzi = mag, ang
                P.op("dve", lambda e: e.tensor_scalar(out=nzr[:], in0=zr[:], scalar1=-1.0, scalar2=None, op0=ALU.mult), reads=["zr", "mag"], writes=["nzr"])
                P.op("dve", lambda e: e.tensor_scalar(out=nzi[:], in0=zi[:], scalar1=-1.0, scalar2=None, op0=ALU.mult), reads=["zi", "ang"], writes=["nzi"])
                def cmul(oi, ai_, bi_):
                    TT(t6[:], pwr[:, :, ai_], pwr[:, :, bi_], ALU.mult, ["pw", "t6"], ["t6"])
                    TT(t7[:], pwi[:, :, ai_], pwi[:, :, bi_], ALU.mult, ["pw", "t7"], ["t7"])
                    TT(pwr[:, :, oi], t6[:], t7[:], ALU.subtract, ["t6", "t7"], ["pw"])
                    TT(t6[:], pwr[:, :, ai_], pwi[:, :, bi_], ALU.mult, ["pw", "t6"], ["t6"])
                    TT(t7[:], pwi[:, :, ai_], pwr[:, :, bi_], ALU.mult, ["pw", "t7"], ["t7"])
                    TT(pwi[:, :, oi], t6[:], t7[:], ALU.add, ["t6", "t7"], ["pw"])
                for k in range(1, 16):
                    cmul(k, k - 1, 0)
                for k in range(16, NPW):
                    cmul(k, k - 1, k - 1)
                P.op("dve", lambda e: e.tensor_scalar(out=pni[:], in0=pwi[:], scalar1=-1.0, scalar2=None, op0=ALU.mult), reads=["pw"], writes=["pni"])
                Bin = P.sb("Bin", [128, 16, 16])
                Bexp = P.sb("Bexp", [128, 16, 128], BF16)
                pbt = P.ps("pbt", [128, 4, 128], BF16)
                for d in range(2):
                    for ri in range(2):
                        src = W["s5_b_re" if ri == 0 else "s5_b_im"][j, d].rearrange("(gp g2) p h -> (g2 p) gp h", g2=2)
                        for hh in range(2):
                            P.dma("sp" if hh == 0 else "act", Bin[:, hh * 8:(hh + 1) * 8, :], src[:, hh * 8:(hh + 1) * 8, :], writes=["Bin"])
                        P.op("pool", lambda e: e.memset(Bexp[:], 0.0), writes=["Bexp"])
                        for g2 in range(2):
                            for q in range(4):
                                P.op("dve", lambda e, g2=g2, q=q: e.tensor_copy(
                                    out=Bexp[g2 * 64:(g2 + 1) * 64, q:16:4, q * 32 + g2 * 16:q * 32 + g2 * 16 + 16],
                                    in_=Bin[g2 * 64:(g2 + 1) * 64, q:16:4, :]), reads=["Bin", "Bexp"], writes=["Bexp"])
                        for c in range(4):
                            for q in range(4):
                                P.op("pe", lambda e, c=c, q=q: e.transpose(pbt[:, q, :], Bexp[:, 4 * c + q, :], C.identb[:]), reads=["Bexp", "identb"], writes=["pspbt"])
                            for q in range(4):
                                rr = slice((q // 2) * 64, (q // 2) * 64 + 64)
                                P.op("dve", lambda e, c=c, q=q, d=d, ri=ri, rr=rr: e.tensor_copy(out=BTp[rr, c, q % 2, d, ri, :], in_=pbt[rr, q, :]), reads=["pspbt"], writes=["BTp"])
                Cin = P.sb("Cin", [128, 2, 64])
                Craw = P.sb("Craw", [128, 16, 2, 2, 16])
                pct = P.ps("psct", [128, 512])
                for d in range(2):
                    for ri in range(2):
                        srcC = W["s5_c_re" if ri == 0 else "s5_c_im"][j, d].rearrange("g h p -> (g h) p")
                        for t in range(4):
                            for hh in range(2):
                                P.dma("sp" if hh == 0 else "act", Cin[:, hh, :], srcC[t * 128:(t + 1) * 128, :], writes=["Cin"])
                            P.op("pe", lambda e: e.transpose(pct[:, 0:128], Cin[:].rearrange("p a b -> p (a b)"), C.ident[:]), reads=["Cin", "ident"], writes=["psct"])
                            for g2 in range(2):
                                P.op("dve", lambda e, g2=g2, t=t, d=d, ri=ri: e.tensor_copy(
                                    out=Craw[g2 * 64:(g2 + 1) * 64, 4 * t:4 * t + 4, d, ri, :],
                                    in_=pct[g2 * 64:(g2 + 1) * 64, 0:128].rearrange("p (i g h) -> p i g h", g=2, h=16)[:, :, g2, :]),
                                    reads=["psct"], writes=["Craw"])
                P.op("pool", lambda e: e.memset(CTp[:], 0.0), writes=["CTp"])
                tc1 = P.sb("tc1", [128, 16])
                for d in range(2):
                    for gp in range(16):
                        col = d * 16 + gp
                        w = gp % 2
                        for ri, (s1, s2) in enumerate(((zr, nzi), (nzi, nzr))):
                            P.op("dve", lambda e, gp=gp, d=d, s1=s1, col=col: e.tensor_scalar(out=tc1[:], in0=Craw[:, gp, d, 0, :], scalar1=s1[:, col:col + 1], scalar2=None, op0=ALU.mult),
                                 reads=["Craw", "zr", "nzr", "nzi"], writes=["tc1"])
                            for g2 in range(2):
                                rr = slice(g2 * 64, (g2 + 1) * 64)
                                P.op("dve", lambda e, gp=gp, d=d, s2=s2, col=col, rr=rr, ri=ri, w=w, g2=g2: e.scalar_tensor_tensor(
                                    out=CTp[rr, gp, d, ri, w * 32 + g2 * 16:w * 32 + g2 * 16 + 16], in0=Craw[rr, gp, d, 1, :], scalar=s2[rr, col:col + 1], in1=tc1[rr, :], op0=ALU.mult, op1=ALU.add),
                                    reads=["Craw", "tc1", "zr", "nzr", "nzi", "CTp"], writes=["CTp"])
            if C.dbgout:
                P.dma("sp", C.dbgout["pwr"], pwr[:], reads=["pw"])
                P.dma("sp", C.dbgout["pwi"], pwi[:], reads=["pw"])
                P.dma("sp", C.dbgout["BTp"], BTp[:], reads=["BTp"])
                P.dma("sp", C.dbgout["CTp"], CTp[:], reads=["CTp"])
            wb = [P.sb("wbuf", [128, 8, 128], BF16) for _ in range(2)]
            uT = P.sb("uT", [128, L], BF16)
            Xr = P.sb("Xr", [128, 16, 128])
            Xi = P.sb("Xi", [128, 16, 128])
            Xbr = P.sb("Xbr", [128, L], BF16)
            Xbi = P.sb("Xbi", [128, L], BF16)
            Sb = [[P.sb("Sb", [128, 130]) for _ in range(2)] for _ in range(2)]
            pt_ = P.sb("s5post", [128, 512])
            psy = [P.ps("psy", [128, 512]) for _ in range(4)]
            psb = [P.ps("psb", [128, 512]) for _ in range(4)]
            Xbr3 = Xbr[:].rearrange("p (n a) -> p a n", a=16)
            Xbi3 = Xbi[:].rearrange("p (n a) -> p a n", a=16)
            for i in range(2):
                for k in range(2):
                    P.op("pool", lambda e, i=i, k=k: e.memset(Sb[i][k][:], 0.0), writes=[("Sb", i, k)])
            nb_ = 0
            for c in range(4):
                wt = wb[c % 2]
                wkey = ("wbuf", c % 2)
                P.dma("pool", wt[:], win[:, :, 768 + c * 128:768 + (c + 1) * 128], writes=[wkey])
                for tt in range(NTT):
                    pb = psb[nb_ % 4]
                    for kc in range(8):
                        P.op("pe", lambda e, kc=kc, tt=tt, pb=pb, wt=wt: e.matmul(pb[:], lhsT=wt[:, kc, :], rhs=C.xbf[:, kc, tt * 512:(tt + 1) * 512], start=(kc == 0), stop=(kc == 7)),
                             reads=[wkey, ("xbf", kc, tt)], writes=[("psb", nb_ % 4)])
                    P.op("act", lambda e, tt=tt, pb=pb: e.activation(out=uT[:, tt * 512:(tt + 1) * 512], in_=pb[:], func=AF.Copy), reads=[("psb", nb_ % 4)], writes=["uT"])
                    nb_ += 1
                for q in range(4):
                    gp = 4 * c + q
                    kr = slice((q // 2) * 64, (q // 2) * 64 + 64)
                    for d in range(2):
                        col = d * 16 + gp
                        LR = lambda k, col=col: pwr[:, col, k:k + 1]
                        LI = lambda k, col=col: pwi[:, col, k:k + 1]
                        NI = lambda k, col=col: pni[:, col, k:k + 1]
                        for tt in range(NTT):
                            for ri, X in enumerate((Xr, Xi)):
                                pb = psb[nb_ % 4]
                                P.op("pe", lambda e, tt=tt, pb=pb, ri=ri, kr=kr, q=q, d=d, c=c: e.matmul(pb[:], lhsT=BTp[kr, c, q % 2, d, ri, :], rhs=uT[kr, tt * 512:(tt + 1) * 512], start=True, stop=True),
                                     reads=["BTp", "uT"], writes=[("psb", nb_ % 4)])
                                if ri == 0:
                                    P.op("act", lambda e, tt=tt, pb=pb, X=X: e.activation(out=X[:, :, tt * 32:(tt + 1) * 32], in_=pb[:].rearrange("p (n a) -> p a n", a=16), func=AF.Copy),
                                         reads=[("psb", nb_ % 4)], writes=["X"])
                                else:
                                    P.op("dve", lambda e, tt=tt, pb=pb, X=X: e.tensor_copy(out=X[:, :, tt * 32:(tt + 1) * 32], in_=pb[:].rearrange("p (n a) -> p a n", a=16)),
                                         reads=[("psb", nb_ % 4)], writes=["X"])
                                nb_ += 1
                        steps = range(1, 16) if d == 0 else range(14, -1, -1)
                        for a in steps:
                            ap_ = a - 1 if d == 0 else a + 1
                            P.op("dve", lambda e, sc_=LR(0), a=a, ap_=ap_: e.scalar_tensor_tensor(out=Xr[:, a, :], in0=Xr[:, ap_, :], scalar=sc_, in1=Xr[:, a, :], op0=ALU.mult, op1=ALU.add), reads=["X", "pw"], writes=["X"])
                            P.op("dve", lambda e, sc_=LR(0), a=a, ap_=ap_: e.scalar_tensor_tensor(out=Xi[:, a, :], in0=Xi[:, ap_, :], scalar=sc_, in1=Xi[:, a, :], op0=ALU.mult, op1=ALU.add), reads=["X", "pw"], writes=["X"])
                            P.op("dve", lambda e, sc_=NI(0), a=a, ap_=ap_: e.scalar_tensor_tensor(out=Xr[:, a, :], in0=Xi[:, ap_, :], scalar=sc_, in1=Xr[:, a, :], op0=ALU.mult, op1=ALU.add), reads=["X", "pni"], writes=["X"])
                            P.op("dve", lambda e, sc_=LI(0), a=a, ap_=ap_: e.scalar_tensor_tensor(out=Xi[:, a, :], in0=Xr[:, ap_, :], scalar=sc_, in1=Xi[:, a, :], op0=ALU.mult, op1=ALU.add), reads=["X", "pw"], writes=["X"])
                        ae = 15 if d == 0 else 0
                        cur = 0
                        P.op("dve", lambda e, ae=ae: e.tensor_copy(out=Sb[0][0][:, 1:129], in_=Xr[:, ae, :]), reads=["X", ("Sb", 0, 0)], writes=[("Sb", 0, 0)])
                        P.op("dve", lambda e, ae=ae: e.tensor_copy(out=Sb[0][1][:, 1:129], in_=Xi[:, ae, :]), reads=["X", ("Sb", 0, 1)], writes=[("Sb", 0, 1)])
                        s = 1
                        lev = 0
                        while s < 128:
                            k = 15 + lev
                            A, B = Sb[cur], Sb[1 - cur]
                            ka = [("Sb", cur, 0), ("Sb", cur, 1)]
                            kb = [("Sb", 1 - cur, 0), ("Sb", 1 - cur, 1)]
                            if d == 0:
                                o_, i_, h_ = slice(1 + s, 129), slice(1, 129 - s), slice(1, 1 + s)
                            else:
                                o_, i_, h_ = slice(1, 129 - s), slice(1 + s, 129), slice(129 - s, 129)
                            P.op("dve", lambda e, sc_=LR(k), A=A, B=B, o_=o_, i_=i_, k=k: e.scalar_tensor_tensor(out=B[0][:, o_], in0=A[0][:, i_], scalar=sc_, in1=A[0][:, o_], op0=ALU.mult, op1=ALU.add), reads=ka + ["pw"], writes=[kb[0]])
                            P.op("dve", lambda e, sc_=LR(k), A=A, B=B, o_=o_, i_=i_, k=k: e.scalar_tensor_tensor(out=B[1][:, o_], in0=A[1][:, i_], scalar=sc_, in1=A[1][:, o_], op0=ALU.mult, op1=ALU.add), reads=ka + ["pw"], writes=[kb[1]])
                            P.op("dve", lambda e, sc_=NI(k), A=A, B=B, o_=o_, i_=i_, k=k: e.scalar_tensor_tensor(out=B[0][:, o_], in0=A[1][:, i_], scalar=sc_, in1=B[0][:, o_], op0=ALU.mult, op1=ALU.add), reads=ka + ["pni", kb[0]], writes=[kb[0]])
                            P.op("dve", lambda e, sc_=LI(k), A=A, B=B, o_=o_, i_=i_, k=k: e.scalar_tensor_tensor(out=B[1][:, o_], in0=A[0][:, i_], scalar=sc_, in1=B[1][:, o_], op0=ALU.mult, op1=ALU.add), reads=ka + ["pw", kb[1]], writes=[kb[1]])
                            P.op("pool", lambda e, A=A, B=B, h_=h_: e.tensor_copy(out=B[0][:, h_], in_=A[0][:, h_]), reads=[ka[0]], writes=[kb[0]])
                            P.op("pool", lambda e, A=A, B=B, h_=h_: e.tensor_copy(out=B[1][:, h_], in_=A[1][:, h_]), reads=[ka[1]], writes=[kb[1]])
                            cur = 1 - cur
                            s *= 2
                            lev += 1
                        S = Sb[cur]
                        ks = [("Sb", cur, 0), ("Sb", cur, 1)]
                        cin = slice(0, 128) if d == 0 else slice(2, 130)
                        for a in range(16):
                            k = a if d == 0 else 15 - a
                            P.op("dve", lambda e, sc_=LR(k), a=a, k=k, S=S, cin=cin: e.scalar_tensor_tensor(out=Xr[:, a, :], in0=S[0][:, cin], scalar=sc_, in1=Xr[:, a, :], op0=ALU.mult, op1=ALU.add), reads=["X", "pw"] + ks, writes=["X"])
                            P.op("dve", lambda e, sc_=LR(k), a=a, k=k, S=S, cin=cin: e.scalar_tensor_tensor(out=Xi[:, a, :], in0=S[1][:, cin], scalar=sc_, in1=Xi[:, a, :], op0=ALU.mult, op1=ALU.add), reads=["X", "pw"] + ks, writes=["X"])
                            P.op("dve", lambda e, sc_=NI(k), a=a, k=k, S=S, cin=cin: e.scalar_tensor_tensor(out=Xbr3[:, a, :], in0=S[1][:, cin], scalar=sc_, in1=Xr[:, a, :], op0=ALU.mult, op1=ALU.add), reads=["X", "pni"] + ks, writes=["Xb"])
                            P.op("dve", lambda e, sc_=LI(k), a=a, k=k, S=S, cin=cin: e.scalar_tensor_tensor(out=Xbi3[:, a, :], in0=S[0][:, cin], scalar=sc_, in1=Xi[:, a, :], op0=ALU.mult, op1=ALU.add), reads=["X", "pw"] + ks, writes=["Xb"])
                        if C.dbgout and gp == 0 and d == 0:
                            P.dma("sp", C.dbgout["Xbr"], Xbr[:], reads=["Xb"])
                            P.dma("sp", C.dbgout["Xbi"], Xbi[:], reads=["Xb"])
                        for tt in range(NTT):
                            for ri, Xb in enumerate((Xbr, Xbi)):
                                first = (q % 2 == 0 and d == 0 and ri == 0)
                                last = (q % 2 == 1 and d == 1 and ri == 1)
                                P.op("pe", lambda e, tt=tt, ri=ri, Xb=Xb, gp=gp, d=d, kr=kr, first=first, last=last: e.matmul(psy[tt][kr, :], lhsT=CTp[:, gp, d, ri, :], rhs=Xb[:, tt * 512:(tt + 1) * 512], start=first, stop=last),
                                     reads=["CTp", "Xb"], writes=[("psy", tt)])
                for tt in range(NTT):
                    ts = slice(tt * 512, (tt + 1) * 512)
                    P.op("dve", lambda e, tt=tt, ts=ts, c=c: e.scalar_tensor_tensor(out=pt_[:], in0=uT[:, ts], scalar=dsk[:, c:c + 1], in1=psy[tt][:], op0=ALU.mult, op1=ALU.add),
                         reads=["uT", "dsk", ("psy", tt)], writes=["s5post"])
                    P.op("act", lambda e, ts=ts, c=c: e.activation(out=hS[:, c, ts], in_=pt_[:], func=AF.Gelu_apprx_tanh), reads=["s5post"], writes=[("hS", c)])
            P.barrier()
            wglu = P.sb("wglu", [128, 4, 512], BF16) if False else None
        with P.phase():
            wglu = P.sb("wglu", [128, 4, 512], BF16)
            P.dma("pool", wglu[:], W["s5_w_glu"][j].rearrange("(k p) m -> p k m", p=128), writes=["wglu"])
            gs = [P.sb("gs", [128, 512], BF16) for _ in range(4)]
            psg = [P.ps("psgl", [128, 512]) for _ in range(2)]
            n = 0
            for tt in range(NTT):
                ts = slice(tt * 512, (tt + 1) * 512)
                for m in range(4):
                    pb = psg[n % 2]
                    for k in range(4):
                        P.op("pe", lambda e, k=k, m=m, ts=ts, pb=pb: e.matmul(pb[:], lhsT=wglu[:, k, m * 128:(m + 1) * 128], rhs=hS[:, k, ts], start=(k == 0), stop=(k == 3)),
                             reads=["wglu", ("hS", k, tt)], writes=[("psgl", n % 2)])
                    P.op("act", lambda e, m=m, pb=pb: e.activation(out=gs[m][:], in_=pb[:], func=AF.Sigmoid), reads=[("psgl", n % 2)], writes=[("gs", m)])
                    n += 1
                for m in range(4):
                    P.op("dve" if m % 2 == 0 else "pool", lambda e, m=m, ts=ts: e.tensor_tensor(out=hS[:, m, ts], in0=hS[:, m, ts], in1=gs[m][:], op=ALU.mult),
                         reads=[("gs", m)] + [("hS", k, tt) for k in range(4)], writes=[("hS", m, tt)])
        if C.dbgout:
            with P.phase():
                P.dma("sp", C.dbgout["yA"], yA[:], reads=[])
                P.dma("act", C.dbgout["hS"], hS[:], reads=[])
        with P.phase():
            T = ln_temps(P)
            wo = [P.sb("wo", [128, 8, 128], BF16) for _ in range(2)]
            pso = [P.ps("pso", [128, 512]) for _ in range(2)]
            wsrcA = W["ev_w_out"][j][0:512, :].rearrange("(hf c d) m -> hf d c m", hf=2, c=4)
            wsrcS = W["ev_w_out"][j][512:1024, :].rearrange("(k p) m -> p k m", p=128)
            for m in range(8):
                b = m % 2
                ms = slice(m * 128, (m + 1) * 128)
                for hf in range(2):
                    P.dma("pool", wo[b][hf * 64:(hf + 1) * 64, 0:4, :], wsrcA[hf][:, :, ms], writes=[("wo", b)])
                P.dma("pool", wo[b][:, 4:8, :], wsrcS[:, :, ms], writes=[("wo", b)])
                for tt in range(NTT):
                    ts = slice(tt * 512, (tt + 1) * 512)
                    pb = pso[tt % 2]
                    for k in range(8):
                        rhs = yA[:, k, ts] if k < 4 else hS[:, k - 4, ts]
                        P.op("pe", lambda e, k=k, b=b, pb=pb, rhs=rhs: e.matmul(pb[:], lhsT=wo[b][:, k, :], rhs=rhs, start=(k == 0), stop=(k == 7)),
                             reads=[("wo", b)], writes=[("pso", tt % 2)])
                    P.op("dve", lambda e, m=m, ts=ts, pb=pb: e.scalar_tensor_tensor(out=C.xres[:, m, ts], in0=C.xres[:, m, ts], scalar=ALPHA, in1=pb[:], op0=ALU.mult, op1=ALU.add),
                         reads=[("pso", tt % 2), ("xres", m, tt)], writes=[("xres", m, tt)])
            for tt in range(NTT):
                emit_ln(C, l, 0, tt, T)


def build(nseq, layers, in_mode, out_mode, dbg=()):
    nc = bass.Bass("TRN2", target_bir_lowering=False)
    C = Ctx()
    C.nc = nc
    C.W = {k: nc.dram_tensor(k, s, F32, kind="ExternalInput").ap() for k, s in WEIGHT_SHAPES.items()}
    C.consts = {k: nc.dram_tensor("c_" + k, s, F32, kind="ExternalInput").ap() for k, s in CONST_SHAPES.items()}
    if in_mode == "tm":
        xin = nc.dram_tensor("x", [nseq, L, D], F32, kind="ExternalInput").ap()
    else:
        xin = nc.dram_tensor("x", [nseq, D, L], F32, kind="ExternalInput").ap()
    if out_mode == "tm":
        yout = nc.dram_tensor("y", [nseq, L, D], F32, kind="ExternalOutput").ap()
    else:
        yout = nc.dram_tensor("y", [nseq, D, L], F32, kind="ExternalOutput").ap()
    C.dbgout = {}
    if dbg:
        C.dbgout = {"yA": nc.dram_tensor("dbg_yA", [128, 4, L], BF16, kind="ExternalOutput").ap(),
                    "hS": nc.dram_tensor("dbg_hS", [128, 4, L], BF16, kind="ExternalOutput").ap(),
                    "pwr": nc.dram_tensor("dbg_pwr", [128, 32, 22], F32, kind="ExternalOutput").ap(),
                    "pwi": nc.dram_tensor("dbg_pwi", [128, 32, 22], F32, kind="ExternalOutput").ap(),
                    "BTp": nc.dram_tensor("dbg_BTp", [128, 4, 2, 2, 2, 128], BF16, kind="ExternalOutput").ap(),
                    "CTp": nc.dram_tensor("dbg_CTp", [128, 16, 2, 2, 64], BF16, kind="ExternalOutput").ap(),
                    "Xbr": nc.dram_tensor("dbg_Xbr", [128, L], BF16, kind="ExternalOutput").ap(),
                    "Xbi": nc.dram_tensor("dbg_Xbi", [128, L], BF16, kind="ExternalOutput").ap()}
    P = Prog(nc)
    C.P = P
    C.xres = P.sb("xres", [128, 8, L])
    C.xbf = P.sb("xbf", [128, 8, L], BF16)
    emit_const_setup(C)
    P.barrier()
    P.flush()
    for s in range(nseq):
        (load_x_tm if in_mode == "tm" else load_x_fm)(C, xin[s])
        import os
        STG = os.environ.get("KSTAGE", "all")
        for l in layers:
            if STG in ("all", "mixer"):
                if l % 2 == 0:
                    emit_even(C, l)
                else:
                    emit_mlstm(C, l)
            if STG in ("all", "ffn"):
                emit_ffn(C, l)
        (store_x_tm if out_mode == "tm" else store_x_fm)(C, yout[s])
    P.finish()
    return nc


_CACHE = {}


def run_layers(xs, weights, layers, in_mode, out_mode, nseq):
    key = (tuple(layers), in_mode, out_mode, nseq)
    if key not in _CACHE:
        _CACHE[key] = build(nseq, layers, in_mode, out_mode)
    nc = _CACHE[key]
    consts = host_consts()
    in_maps = []
    for c in range(8):
        m = {k: np.ascontiguousarray(v, dtype=np.float32) for k, v in weights.items()}
        for k, v in consts.items():
            m["c_" + k] = v
        m["x"] = np.ascontiguousarray(xs[c])
        in_maps.append(m)
    res = run_bass_kernel_spmd(nc, in_maps, core_ids=list(range(8)))
    return [r["y"] for r in res.results]


def kernel(**inputs):
    x = np.asarray(inputs["x"], dtype=np.float32)
    weights = {k: np.asarray(v, dtype=np.float32) for k, v in inputs.items() if k != "x"}
    xs = [x[c * 4:(c + 1) * 4] for c in range(8)]
    ys = run_layers(xs, weights, [0, 1, 2, 3], "tm", "tm", 4)
    return np.concatenate(ys, axis=0).astype(np.float32)
```

```python
import math
from contextlib import ExitStack, contextmanager
import numpy as np
import ml_dtypes
import concourse.bass as bass
import concourse.mybir as mybir
from concourse.bass_utils import run_bass_kernel_spmd

F32 = mybir.dt.float32
BF16 = mybir.dt.bfloat16
AF = mybir.ActivationFunctionType
ALU = mybir.AluOpType

import os as _os
ENGS = ("pe", "act", "dve", "pool", "sp")
SAME_ENGINE_NOSYNC = tuple(_os.environ.get("KNOSYNC", "pe").split(","))
DQ = ("sp", "act", "pool")

D = 1024
L = 2048
NB = 16
NTT = 4
DEPTH = 4
DFF = 2816
NJ = 22
ALPHA = (2 * DEPTH) ** 0.25
EPS = 1e-5
NEG = -30000.0


class Prog:
    NSLOT = 8

    def __init__(self, nc):
        self.nc = nc
        self.es = ExitStack()
        self.ops = {e: [] for e in ENGS}
        self.sem = {"s_" + e: self.es.enter_context(nc.semaphore("s_" + e)) for e in ENGS}
        self.cnt = {e: 0 for e in ENGS}
        self.dcnt = {}
        for q in DQ:
            for i in range(self.NSLOT):
                n = "d_%s%d" % (q, i)
                self.sem[n] = self.es.enter_context(nc.semaphore(n))
                self.dcnt[n] = 0
        self.dnext = {q: 0 for q in DQ}
        self.seen = {e: {} for e in ENGS}
        self.lastw = {}
        self.readers = {}
        self.stack = [self.es]
        self.uid = 0

    def sb(self, name, shape, dt=F32):
        self.uid += 1
        return self.stack[-1].enter_context(self.nc.sbuf_tensor("%s_%d" % (name, self.uid), list(shape), dt))

    def ps(self, name, shape, dt=F32):
        self.uid += 1
        return self.stack[-1].enter_context(self.nc.psum_tensor("%s_%d" % (name, self.uid), list(shape), dt))

    @contextmanager
    def phase(self):
        es = ExitStack()
        self.stack.append(es)
        try:
            yield
        finally:
            self.barrier()
            self.flush()
            self.stack.pop()
            es.close()

    def _need(self, eng, tok, waits):
        if tok is None:
            return
        sname, val, teng = tok
        if teng == eng and eng in SAME_ENGINE_NOSYNC:
            return
        if self.seen[eng].get(sname, 0) >= val:
            return
        self.seen[eng][sname] = val
        waits.append((sname, val))

    def _deps(self, eng, reads, writes):
        waits = []
        for k in reads:
            self._need(eng, self.lastw.get(k), waits)
        for k in writes:
            self._need(eng, self.lastw.get(k), waits)
            for t in self.readers.get(k, ()):
                self._need(eng, t, waits)
        return waits

    def _commit(self, tok, reads, writes):
        for k in reads:
            self.readers.setdefault(k, []).append(tok)
        for k in writes:
            self.lastw[k] = tok
            self.readers[k] = []

    @staticmethod
    def _isps(k):
        k0 = k[0] if isinstance(k, tuple) else k
        return isinstance(k0, str) and k0.startswith("ps")

    def op(self, eng, fn, reads=(), writes=()):
        psr = [k for k in reads if self._isps(k)]
        if psr:
            reads = [k for k in reads if not self._isps(k)]
            writes = list(writes) + psr
        waits = self._deps(eng, reads, writes)
        self.cnt[eng] += 1
        tok = ("s_" + eng, self.cnt[eng], eng)
        self.ops[eng].append((waits, fn, ("s_" + eng, 1)))
        self._commit(tok, reads, writes)
        return tok

    def dma(self, q, out, in_, reads=(), writes=(), **kw):
        waits = self._deps(q, reads, writes)
        s = self.dnext[q]
        self.dnext[q] = (s + 1) % self.NSLOT
        sname = "d_%s%d" % (q, s)
        prev = self.dcnt[sname]
        if prev > 0 and self.seen[q].get(sname, 0) < prev:
            self.seen[q][sname] = prev
            waits.append((sname, prev))
        self.dcnt[sname] = prev + 16
        tok = (sname, prev + 16, "dma_" + q)
        self.ops[q].append((waits, lambda e: e.dma_start(out=out, in_=in_, **kw), (sname, 16)))
        self._commit(tok, reads, writes)
        return tok

    def barrier(self):
        for e in ENGS:
            waits = []
            for n, v in self.dcnt.items():
                if v > 0 and self.seen[e].get(n, 0) < v:
                    self.seen[e][n] = v
                    waits.append((n, v))
            for e2 in ENGS:
                n = "s_" + e2
                v = self.cnt[e2]
                if e2 != e and v > 0 and self.seen[e].get(n, 0) < v:
                    self.seen[e][n] = v
                    waits.append((n, v))
            if waits:
                self.ops[e].append((waits, None, None))
        self.lastw = {}
        self.readers = {}

    def flush(self):
        nc = self.nc
        handles = {"pe": "tensor", "act": "scalar", "dve": "vector", "pool": "gpsimd", "sp": "sync"}
        if not any(self.ops[e] for e in ENGS):
            return
        with nc.Block() as block:
            for e in ENGS:
                lst = self.ops[e]
                if not lst:
                    continue

                def body(eng, lst=lst):
                    for waits, fn, inc in lst:
                        for sname, val in waits:
                            eng.wait_ge(self.sem[sname], val)
                        if fn is not None:
                            fn(eng).then_inc(self.sem[inc[0]], inc[1])
                getattr(block, handles[e])(body)
        self.ops = {e: [] for e in ENGS}

    def finish(self):
        self.barrier()
        self.flush()
        self.es.close()


def host_consts():
    c = {}
    c["ident"] = np.eye(128, dtype=np.float32)
    s = np.arange(128)[:, None]
    t = np.arange(512)[None, :]
    mf = np.stack([(t >= jj * 128 + s) for jj in range(4)], 1).astype(np.float32)
    mb = np.stack([(t <= jj * 128 + s) for jj in range(4)], 1).astype(np.float32)
    s1 = np.arange(128)[:, None]
    t1 = np.arange(128)[None, :]
    c["tri"] = np.stack([(t1 >= s1), (t1 <= s1)], 1).astype(np.float32)
    slopes = 2.0 ** (-8.0 * np.arange(1, 9) / 8)
    kk = np.arange(128)[:, None]
    qq = np.arange(128)[None, :]
    ab = np.zeros((128, 8, 3, 128), np.float32)
    for h in range(8):
        for r in range(3):
            dist = np.abs(qq - (kk + (r - 1) * 128))
            ab[:, h, r, :] = np.where(dist <= 128, -slopes[h] * dist * 8.0, NEG)
    c["abias"] = ab
    sel = np.zeros((40, 8, 128), np.float32)
    for h in range(8):
        sel[h, h, :] = 1.0
        sel[32 + h, h, :] = 1.0
    c["sel"] = sel
    return c


CONST_SHAPES = {"ident": [128, 128], "tri": [128, 2, 128],
                "abias": [128, 8, 3, 128], "sel": [40, 8, 128]}

WEIGHT_SHAPES = {
    "ev_w_in": [2, 1024, 1280], "ev_sink": [2, 8], "s5_a_re": [2, 2, 32, 64], "s5_a_im": [2, 2, 32, 64],
    "s5_log_dt": [2, 2, 32], "s5_b_re": [2, 2, 32, 64, 16], "s5_b_im": [2, 2, 32, 64, 16],
    "s5_c_re": [2, 2, 32, 16, 64], "s5_c_im": [2, 2, 32, 16, 64], "s5_d": [2, 512],
    "s5_w_glu": [2, 512, 512], "ev_w_out": [2, 1024, 1024], "od_w_in": [2, 1024, 4128],
    "od_gate_b": [2, 32], "od_conv_w": [2, 3, 2048], "od_conv_b": [2, 2048], "od_norm_g": [2, 1024],
    "od_w_out": [2, 1024, 1024], "ffn_w_up": [4, 1024, 5632], "ffn_conv_w": [4, 3, 2816],
    "ffn_conv_b": [4, 2816], "ffn_w_down": [4, 2816, 1024], "ln_g": [4, 2, 1024], "ln_b": [4, 2, 1024],
}


class Ctx:
    pass


def kc_view(w2d):
    return w2d.rearrange("(kc k) m -> k kc m", k=128)


def emit_const_setup(C):
    P = C.P
    W = C.W
    C.ident = P.sb("ident", [128, 128])
    C.identb = P.sb("identb", [128, 128], BF16)
    C.onesf = P.sb("onesf", [128, 128])
    C.ones1k = P.sb("ones1k", [128, 128])
    C.onesb = P.sb("onesb", [128, 128], BF16)
    C.sel = P.sb("sel", [40, 8, 128])
    C.one8 = P.sb("one8", [128, 1])
    P.dma("sp", C.ident[:], C.consts["ident"], writes=["ident"])
    P.dma("sp", C.sel[:], C.consts["sel"], writes=["sel"])
    P.op("dve", lambda e: e.tensor_copy(out=C.identb[:], in_=C.ident[:]), reads=["ident"], writes=["identb"])
    P.op("pool", lambda e: e.memset(C.onesf[:], 1.0 / 128), writes=["onesf"])
    P.op("pool", lambda e: e.memset(C.ones1k[:], 1.0 / 1024), writes=["ones1k"])
    P.op("pool", lambda e: e.memset(C.onesb[:], 1.0), writes=["onesb"])
    P.op("pool", lambda e: e.memset(C.one8[:], 1.0), writes=["one8"])
    C.epsc = P.sb("epsc", [128, 1])
    P.op("pool", lambda e: e.memset(C.epsc[:], EPS), writes=["epsc"])
    C.lng = P.sb("lng", [128, 4, 2, 8])
    C.lnb = P.sb("lnb", [128, 4, 2, 8])
    import os
    for l in range(4 if os.environ.get("KNOLN") is None else 0):
        for j in range(2):
            P.dma("sp", C.lng[:, l, j, :], W["ln_g"][l, j].rearrange("(c p) -> p c", p=128), writes=["lng"],
                  allow_slow_non_contiguous=True)
            P.dma("act", C.lnb[:, l, j, :], W["ln_b"][l, j].rearrange("(c p) -> p c", p=128), writes=["lnb"],
                  allow_slow_non_contiguous=True)


def load_x_tm(C, x_seq):
    P = C.P
    with P.phase():
        stg = [P.sb("xstg", [128, 1024]) for _ in range(2)]
        pt = [P.ps("xps", [128, 4, 128]) for _ in range(2)]
        for nb in range(NB):
            s = stg[nb % 2]
            P.dma("sp" if nb % 2 == 0 else "act", s[:], x_seq[nb * 128:(nb + 1) * 128, :], writes=[("xstg", nb % 2)])
            for half in range(2):
                pp = pt[half]
                for i in range(4):
                    c = half * 4 + i
                    P.op("pe", lambda e, pp=pp, i=i, c=c, s=s: e.transpose(pp[:, i, :], s[:, c * 128:(c + 1) * 128], C.ident[:]),
                         reads=[("xstg", nb % 2), "ident"], writes=[("psxp", half)])
                P.op("act", lambda e, pp=pp, half=half, nb=nb: e.activation(
                    out=C.xres[:, half * 4:half * 4 + 4, nb * 128:(nb + 1) * 128], in_=pp[:], func=AF.Copy),
                    reads=[("psxp", half)], writes=[("xres", nb, half)])
                P.op("dve", lambda e, pp=pp, half=half, nb=nb: e.tensor_copy(
                    out=C.xbf[:, half * 4:half * 4 + 4, nb * 128:(nb + 1) * 128], in_=pp[:]),
                    reads=[("psxp", half)], writes=[("xbf", nb, half)])


def load_x_fm(C, xT_seq):
    P = C.P
    with P.phase():
        for c in range(8):
            P.dma("sp" if c % 2 == 0 else "act", C.xres[:, c, :], xT_seq[c * 128:(c + 1) * 128, :], writes=[("xres", c)])
            P.op("dve" if c % 2 == 0 else "pool", lambda e, c=c: e.tensor_copy(out=C.xbf[:, c, :], in_=C.xres[:, c, :]),
                 reads=[("xres", c)], writes=[("xbf", c)])


def store_x_fm(C, xT_seq):
    P = C.P
    with P.phase():
        for c in range(8):
            P.dma("sp" if c % 2 == 0 else "act", xT_seq[c * 128:(c + 1) * 128, :], C.xres[:, c, :], reads=[("xres", c)])


def store_x_tm(C, y_seq):
    P = C.P
    with P.phase():
        stg = [P.sb("ystg", [128, 1024]) for _ in range(2)]
        pt = [P.ps("yps", [128, 4, 128]) for _ in range(2)]
        for nb in range(NB):
            s = stg[nb % 2]
            for half in range(2):
                pp = pt[half]
                for i in range(4):
                    c = half * 4 + i
                    P.op("pe", lambda e, pp=pp, i=i, c=c, nb=nb: e.transpose(pp[:, i, :], C.xres[:, c, nb * 128:(nb + 1) * 128], C.ident[:]),
                         reads=["ident"], writes=[("psyp", half)])
                eng = "act" if half == 0 else "dve"
                if eng == "act":
                    P.op("act", lambda e, pp=pp, s=s, half=half: e.activation(out=s[:, half * 512:(half + 1) * 512], in_=pp[:].rearrange("p a b -> p (a b)"), func=AF.Copy),
                         reads=[("psyp", half)], writes=[("ystg", nb % 2, half)])
                else:
                    P.op("dve", lambda e, pp=pp, s=s, half=half: e.tensor_copy(out=s[:, half * 512:(half + 1) * 512], in_=pp[:].rearrange("p a b -> p (a b)")),
                         reads=[("psyp", half)], writes=[("ystg", nb % 2, half)])
            P.dma("sp" if nb % 2 == 0 else "act", y_seq[nb * 128:(nb + 1) * 128, :], s[:],
                  reads=[("ystg", nb % 2, 0), ("ystg", nb % 2, 1)])


def emit_ln(C, l, j, tt, T):
    P = C.P
    ts = slice(tt * 512, (tt + 1) * 512)
    rk = [("xres", c, tt) for c in range(8)]
    for c in range(8):
        P.op("pe", lambda e, c=c: e.matmul(T["psm"][:], lhsT=C.ones1k[:], rhs=C.xres[:, c, ts], start=(c == 0), stop=(c == 7)),
             reads=[rk[c], "ones1k"], writes=["psm"])
    for c in range(8):
        sq = T["sq"][c % 2]
        P.op("act", lambda e, c=c, sq=sq: e.activation(out=sq[:], in_=C.xres[:, c, ts], func=AF.Square),
             reads=[rk[c]], writes=[("sq", c % 2)])
        P.op("pe", lambda e, c=c, sq=sq: e.matmul(T["psv"][:], lhsT=C.ones1k[:], rhs=sq[:], start=(c == 0), stop=(c == 7)),
             reads=[("sq", c % 2), "ones1k"], writes=["psv"])
    mean, rstd = T["mean"], T["rstd"]
    P.op("act", lambda e: e.activation(out=mean[:], in_=T["psm"][:], func=AF.Copy), reads=["psm"], writes=["mean"])
    P.op("dve", lambda e: e.tensor_tensor(out=rstd[:], in0=mean[:], in1=mean[:], op=ALU.mult), reads=["mean"], writes=["rstd"])
    P.op("dve", lambda e: e.tensor_tensor(out=rstd[:], in0=T["psv"][:], in1=rstd[:], op=ALU.subtract), reads=["psv", "rstd"], writes=["rstd"])
    P.op("act", lambda e: e.activation(out=rstd[:], in_=rstd[:], func=AF.Sqrt, bias=C.epsc[:, 0:1], scale=1.0), reads=["rstd", "epsc"], writes=["rstd"])
    P.op("dve", lambda e: e.reciprocal(out=rstd[:], in_=rstd[:]), reads=["rstd"], writes=["rstd"])
    for c in range(8):
        tmp = T["tmp"][c % 2]
        P.op("pool", lambda e, c=c, tmp=tmp: e.tensor_tensor(out=tmp[:], in0=C.xres[:, c, ts], in1=mean[:], op=ALU.subtract),
             reads=[rk[c], "mean"], writes=[("lntmp", c % 2)])
        P.op("dve", lambda e, c=c, tmp=tmp: e.tensor_tensor(out=tmp[:], in0=tmp[:], in1=rstd[:], op=ALU.mult),
             reads=[("lntmp", c % 2), "rstd"], writes=[("lntmp", c % 2)])
        P.op("act", lambda e, c=c, tmp=tmp: e.activation(out=C.xres[:, c, ts], in_=tmp[:], func=AF.Identity,
                                                         bias=C.lnb[:, l, j, c:c + 1], scale=C.lng[:, l, j, c:c + 1]),
             reads=[("lntmp", c % 2), "lng", "lnb"], writes=[rk[c]])
        P.op("pool", lambda e, c=c: e.tensor_copy(out=C.xbf[:, c, ts], in_=C.xres[:, c, ts]),
             reads=[rk[c]], writes=[("xbf", c, tt)])


def ln_temps(P):
    return {"sq": [P.sb("lnsq", [128, 512]) for _ in range(2)], "tmp": [P.sb("lntmp", [128, 512]) for _ in range(2)],
            "mean": P.sb("lnmean", [128, 512]), "rstd": P.sb("lnrstd", [128, 512]),
            "psm": P.ps("psm", [128, 512]), "psv": P.ps("psv", [128, 512])}


def emit_conv3(C, P, asb, out, wv, bv, ncol, key_in, key_out):
    P.op("dve", lambda e: e.tensor_scalar(out=out, in0=asb[:, 1:ncol + 1], scalar1=wv[:, 1:2], scalar2=bv, op0=ALU.mult, op1=ALU.add),
         reads=[key_in], writes=[key_out])
    P.op("dve", lambda e: e.scalar_tensor_tensor(out=out, in0=asb[:, 0:ncol], scalar=wv[:, 0:1], in1=out, op0=ALU.mult, op1=ALU.add),
         reads=[key_in, key_out], writes=[key_out])
    P.op("dve", lambda e: e.scalar_tensor_tensor(out=out, in0=asb[:, 2:ncol + 2], scalar=wv[:, 2:3], in1=out, op0=ALU.mult, op1=ALU.add),
         reads=[key_in, key_out], writes=[key_out])


def emit_ffn(C, l):
    P = C.P
    W = C.W
    wup = kc_view(W["ffn_w_up"][l])
    wdn = W["ffn_w_down"][l].rearrange("(j p) m -> p j m", p=128)
    HL = 1024
    with P.phase():
        cw = P.sb("fcw", [128, NJ, 3])
        cb = P.sb("fcb", [128, NJ])
        for k in range(3):
            P.dma("sp", cw[:, :, k], W["ffn_conv_w"][l, k].rearrange("(j p) -> p j", p=128), writes=["fcw"], allow_slow_non_contiguous=True)
        P.dma("act", cb[:], W["ffn_conv_b"][l].rearrange("(j p) -> p j", p=128), writes=["fcw"], allow_slow_non_contiguous=True)
        hT = P.sb("hT", [128, NJ, HL], BF16)
        wa = [P.sb("wa", [128, 8, 128], BF16) for _ in range(2)]
        wg = [P.sb("wg", [128, 8, 128], BF16) for _ in range(2)]
        wd = [P.sb("wd", [128, NJ, 128], BF16) for _ in range(2)]
        asb = [P.sb("asb", [128, HL + 2]) for _ in range(2)]
        cv = [P.sb("cv", [128, HL]) for _ in range(2)]
        psa = [P.ps("psa", [128, 512]) for _ in range(2)]
        psg = [P.ps("psg", [128, 512]) for _ in range(2)]
        psh = P.ps("psh", [128, 512])
        psd = [P.ps("psd", [128, 512]) for _ in range(1)]
        T = {"sq": [cv[0][:, 0:512], cv[0][:, 512:1024]], "tmp": [cv[1][:, 0:512], cv[1][:, 512:1024]],
             "mean": asb[0][:, 0:512], "rstd": asb[0][:, 512:1024], "psm": psa[0], "psv": psa[1]}
        T = {k: (v if isinstance(v, list) else v) for k, v in T.items()}
        it = 0
        for half in range(2):
            t0 = half * HL
            for j in range(NJ):
                b = it % 2
                it += 1
                P.dma("pool", wa[b][:], wup[:, :, j * 128:(j + 1) * 128], writes=[("wa", b)])
                P.dma("pool", wg[b][:], wup[:, :, DFF + j * 128:DFF + (j + 1) * 128], writes=[("wg", b)])
                A = asb[b]
                hcols = []
                if half == 0:
                    P.op("pool", lambda e, A=A: e.memset(A[:, 0:1], 0.0), writes=[("asbh0", b)])
                    hcols.append((HL + 1, t0 + HL))
                else:
                    P.op("pool", lambda e, A=A: e.memset(A[:, HL + 1:HL + 2], 0.0), writes=[("asbh1", b)])
                    hcols.append((0, t0 - 1))
                for (dst, tok) in hcols:
                    for kc in range(8):
                        P.op("pe", lambda e, kc=kc, tok=tok, b=b: e.matmul(psh[:, 0:1], lhsT=wa[b][:, kc, :], rhs=C.xbf[:, kc, tok:tok + 1], start=(kc == 0), stop=(kc == 7)),
                             reads=[("wa", b), ("xbf", kc, tok // 512)], writes=["psh"])
                    P.op("act", lambda e, A=A, dst=dst: e.activation(out=A[:, dst:dst + 1], in_=psh[:, 0:1], func=AF.Copy),
                         reads=["psh"], writes=[("asbh%d" % (1 if dst > 0 else 0), b)])
                for q in range(2):
                    tt = half * 2 + q
                    for kc in range(8):
                        P.op("pe", lambda e, kc=kc, tt=tt, b=b, q=q: e.matmul(psa[q][:], lhsT=wa[b][:, kc, :], rhs=C.xbf[:, kc, tt * 512:(tt + 1) * 512], start=(kc == 0), stop=(kc == 7)),
                             reads=[("wa", b), ("xbf", kc, tt)], writes=[("psa", q)])
                    P.op("act", lambda e, A=A, q=q: e.activation(out=A[:, 1 + q * 512:1 + (q + 1) * 512], in_=psa[q][:], func=AF.Copy),
                         reads=[("psa", q)], writes=[("asb", b, q)])
                    for kc in range(8):
                        P.op("pe", lambda e, kc=kc, tt=tt, b=b, q=q: e.matmul(psg[q][:], lhsT=wg[b][:, kc, :], rhs=C.xbf[:, kc, tt * 512:(tt + 1) * 512], start=(kc == 0), stop=(kc == 7)),
                             reads=[("wg", b), ("xbf", kc, tt)], writes=[("psg", q)])
                kin = ("asball", b)
                P.op("dve", lambda e, b=b, j=j: e.tensor_scalar(out=cv[b][:], in0=asb[b][:, 1:HL + 1], scalar1=cw[:, j, 1:2], scalar2=cb[:, j:j + 1], op0=ALU.mult, op1=ALU.add),
                     reads=[("asb", b, 0), ("asb", b, 1), "fcw"], writes=[("cv", b)])
                P.op("dve", lambda e, b=b, j=j: e.scalar_tensor_tensor(out=cv[b][:], in0=asb[b][:, 0:HL], scalar=cw[:, j, 0:1], in1=cv[b][:], op0=ALU.mult, op1=ALU.add),
                     reads=[("asb", b, 0), ("asb", b, 1), ("asbh0", b), ("cv", b)], writes=[("cv", b)])
                P.op("dve", lambda e, b=b, j=j: e.scalar_tensor_tensor(out=cv[b][:], in0=asb[b][:, 2:HL + 2], scalar=cw[:, j, 2:3], in1=cv[b][:], op0=ALU.mult, op1=ALU.add),
                     reads=[("asb", b, 0), ("asb", b, 1), ("asbh1", b), ("cv", b)], writes=[("cv", b)])
                P.op("act", lambda e, b=b: e.activation(out=cv[b][:], in_=cv[b][:], func=AF.Gelu_apprx_tanh),
                     reads=[("cv", b)], writes=[("cv", b)])
                for q in range(2):
                    P.op("dve", lambda e, b=b, q=q, j=j: e.tensor_tensor(out=hT[:, j, q * 512:(q + 1) * 512], in0=cv[b][:, q * 512:(q + 1) * 512], in1=psg[q][:], op=ALU.mult),
                         reads=[("cv", b), ("psg", q)], writes=[("hT", j)])
            for m in range(8):
                b = m % 2
                P.dma("pool", wd[b][:], wdn[:, :, m * 128:(m + 1) * 128], writes=[("wd", b)])
                for q in range(2):
                    tt = half * 2 + q
                    for j in range(NJ):
                        P.op("pe", lambda e, j=j, b=b, q=q: e.matmul(psd[0][:], lhsT=wd[b][:, j, :], rhs=hT[:, j, q * 512:(q + 1) * 512], start=(j == 0), stop=(j == NJ - 1)),
                             reads=[("wd", b), ("hT", j)], writes=["psd"])
                    P.op("dve", lambda e, m=m, tt=tt: e.scalar_tensor_tensor(out=C.xres[:, m, tt * 512:(tt + 1) * 512], in0=C.xres[:, m, tt * 512:(tt + 1) * 512], scalar=ALPHA, in1=psd[0][:], op0=ALU.mult, op1=ALU.add),
                         reads=["psd", ("xres", m, tt)], writes=[("xres", m, tt)])
        P.barrier()
        for tt in range(NTT):
            emit_ln(C, l, 1, tt, T)


def emit_mixer_out(C, l, yT, wout_dram, nk, T):
    P = C.P
    wv = wout_dram.rearrange("(kc k) m -> k kc m", k=128)
    wo = [P.sb("wo", [128, nk, 128], BF16) for _ in range(2)]
    pso = [P.ps("pso", [128, 512]) for _ in range(2)]
    for m in range(8):
        b = m % 2
        P.dma("pool", wo[b][:], wv[:, :, m * 128:(m + 1) * 128], writes=[("wo", b)])
        for tt in range(NTT):
            pb = pso[tt % 2]
            for k in range(nk):
                P.op("pe", lambda e, k=k, b=b, tt=tt, pb=pb: e.matmul(pb[:], lhsT=wo[b][:, k, :], rhs=yT[:, k, tt * 512:(tt + 1) * 512], start=(k == 0), stop=(k == nk - 1)),
                     reads=[("wo", b), ("yT", k)], writes=[("pso", tt % 2)])
            P.op("dve", lambda e, m=m, tt=tt, pb=pb: e.scalar_tensor_tensor(out=C.xres[:, m, tt * 512:(tt + 1) * 512], in0=C.xres[:, m, tt * 512:(tt + 1) * 512], scalar=ALPHA, in1=pb[:], op0=ALU.mult, op1=ALU.add),
                 reads=[("pso", tt % 2), ("xres", m, tt)], writes=[("xres", m, tt)])
    for tt in range(NTT):
        emit_ln(C, l, 0, tt, T)


def emit_mlstm(C, l):
    P = C.P
    W = C.W
    j = l // 2
    win = kc_view(W["od_w_in"][j])
    SCALE = 128.0 ** -0.5
    RB = (0, 32)
    with P.phase():
        hTf = P.sb("hTfin", [128, 8, L], BF16)
        with P.phase():
            gb = P.sb("gb", [40, 2])
            for d in range(2):
                P.dma("sp", gb[RB[d]:RB[d] + 8, :], W["od_gate_b"][j][16 * d:16 * d + 16].rearrange("(q h) -> h q", h=8), writes=["gb"], allow_slow_non_contiguous=True)
            cwq = P.sb("mcw", [128, 16, 3])
            cbq = P.sb("mcb", [128, 16])
            for k in range(3):
                P.dma("sp", cwq[:, :, k], W["od_conv_w"][j, k].rearrange("(c p) -> p c", p=128), writes=["mcw"], allow_slow_non_contiguous=True)
            P.dma("act", cbq[:], W["od_conv_b"][j].rearrange("(c p) -> p c", p=128), writes=["mcw"], allow_slow_non_contiguous=True)
            ng = P.sb("ng", [128, 8])
            P.dma("sp", ng[:], W["od_norm_g"][j].rearrange("(c p) -> p c", p=128), writes=["ng"], allow_slow_non_contiguous=True)
            tri = P.sb("tri", [128, 2, 128], BF16)
            P.dma("pool", tri[:], C.consts["tri"], writes=["tri"])
            nmrel = P.sb("gnm", [40, L])
            negm = P.sb("gm", [40, L])
            colT = P.sb("colT", [128, 2, NB, 8])
            with P.phase():
                gw = P.sb("gw", [128, 8, 32], BF16)
                P.dma("pool", gw[:], win[:, :, 4096:4128], writes=["gw"])
                ci = P.sb("gci", [40, L])
                t1 = P.sb("gt1", [40, L])
                t2 = P.sb("gt2", [40, L])
                tot = P.sb("gtot", [40, 1])
                psg = [P.ps("psgate", [128, 512]) for _ in range(2)]
                pct = P.ps("pspct", [128, 2, NB, 16])
                n = 0
                for q in range(4):
                    d = q // 2
                    dst = ci if q % 2 == 0 else nmrel
                    for tt in range(NTT):
                        pb = psg[n % 2]
                        for kc in range(8):
                            P.op("pe", lambda e, kc=kc, q=q, tt=tt, pb=pb, d=d: e.matmul(pb[RB[d]:RB[d] + 8, :], lhsT=gw[:, kc, q * 8:(q + 1) * 8], rhs=C.xbf[:, kc, tt * 512:(tt + 1) * 512], start=(kc == 0), stop=(kc == 7)),
                                 reads=["gw", ("xbf", kc, tt)], writes=[("psgate", n % 2)])
                        P.op("act", lambda e, q=q, tt=tt, pb=pb, d=d, dst=dst: e.activation(out=dst[RB[d]:RB[d] + 8, tt * 512:(tt + 1) * 512], in_=pb[RB[d]:RB[d] + 8, :], func=AF.Identity, bias=gb[RB[d]:RB[d] + 8, (q % 2):(q % 2) + 1], scale=1.0),
                             reads=[("psgate", n % 2), "gb"], writes=[("gpre", q)])
                        n += 1
                for d in range(2):
                    r = slice(RB[d], RB[d] + 8)
                    kg, kf = ("gpre", 2 * d), ("gpre", 2 * d + 1)
                    K = lambda s, d=d: (s, d)
                    P.op("act", lambda e, r=r: e.activation(out=t1[r, :], in_=nmrel[r, :], func=AF.Exp, scale=-1.0), reads=[kf], writes=[K("t1")])
                    P.op("act", lambda e, r=r: e.activation(out=nmrel[r, :], in_=t1[r, :], func=AF.Ln, bias=1.0, scale=1.0), reads=[K("t1")], writes=[K("lf")])
                    P.op("dve", lambda e, r=r: e.tensor_tensor_scan(out=negm[r, :], data0=C.one8[r, 0:1].to_broadcast([8, L]), data1=nmrel[r, :], initial=0.0, op0=ALU.mult, op1=ALU.add),
                         reads=[K("lf"), "one8"], writes=[K("G")])
                    if d == 1:
                        P.op("dve", lambda e, r=r: e.tensor_copy(out=tot[r, :], in_=negm[r, L - 1:L]), reads=[K("G")], writes=[K("tot")])
                        P.op("dve", lambda e, r=r: e.tensor_tensor(out=t1[r, :], in0=nmrel[r, :], in1=negm[r, :], op=ALU.subtract), reads=[K("lf"), K("G"), K("t1")], writes=[K("t1")])
                        P.op("dve", lambda e, r=r: e.tensor_scalar(out=negm[r, :], in0=t1[r, :], scalar1=tot[r, 0:1], scalar2=None, op0=ALU.add), reads=[K("t1"), K("tot")], writes=[K("G")])
                    P.op("dve", lambda e, r=r: e.tensor_tensor(out=ci[r, :], in0=ci[r, :], in1=negm[r, :], op=ALU.add), reads=[kg, K("G")], writes=[K("c")])
                    if d == 0:
                        P.op("dve", lambda e, r=r: e.tensor_tensor_scan(out=t1[r, :], data0=C.one8[r, 0:1].to_broadcast([8, L]), data1=ci[r, :], initial=-1e30, op0=ALU.mult, op1=ALU.max),
                             reads=[K("c"), "one8", K("t1")], writes=[K("t1")])
                        cm, kcm = t1, K("t1")
                    else:
                        P.op("dve", lambda e, r=r: e.tensor_copy(out=t1[r, :], in_=ci[r, :]), reads=[K("c"), K("t1")], writes=[K("t1")])
                        src, dst, ks, kd = t1, t2, K("t1"), K("t2")
                        s = 1
                        while s < L:
                            P.op("dve", lambda e, src=src, dst=dst, s=s, r=r: e.tensor_tensor(out=dst[r, 0:L - s], in0=src[r, 0:L - s], in1=src[r, s:L], op=ALU.max), reads=[ks], writes=[kd])
                            P.op("dve", lambda e, src=src, dst=dst, s=s, r=r: e.tensor_copy(out=dst[r, L - s:L], in_=src[r, L - s:L]), reads=[ks], writes=[kd])
                            src, dst, ks, kd = dst, src, kd, ks
                            s *= 2
                        cm, kcm = src, ks
                    P.op("dve", lambda e, cm=cm, r=r: e.tensor_scalar(out=nmrel[r, :], in0=cm[r, :], scalar1=-1.0, scalar2=0.0, op0=ALU.mult, op1=ALU.min),
                         reads=[kcm, K("lf")], writes=[("gnm", d)])
                    P.op("dve", lambda e, r=r: e.tensor_tensor(out=negm[r, :], in0=negm[r, :], in1=nmrel[r, :], op=ALU.add), reads=[K("G"), ("gnm", d)], writes=[("gm", d)])
                    for nb in range(NB):
                        P.op("pe", lambda e, d=d, nb=nb, r=r: e.transpose(pct[:, d, nb, 0:8], ci[r, nb * 128:(nb + 1) * 128], C.ident[r, r]),
                             reads=[K("c"), "ident"], writes=["pspct"])
                P.op("dve", lambda e: e.tensor_copy(out=colT[:], in_=pct[:, :, :, 0:8]), reads=["pspct"], writes=["colT"])
            wb = [P.sb("wbuf", [128, 8, 128], BF16) for _ in range(2)]
            qT = P.sb("qT", [128, L], BF16)
            kT = P.sb("kT", [128, L], BF16)
            vt = P.sb("vt", [128, NB, 128], BF16)
            sgo = P.sb("sgo", [128, 512], BF16)
            bufA = P.sb("bufA", [128, L + 2])
            bufB = P.sb("bufB", [128, L])
            Dt = [P.sb("Dt", [128, 512], BF16) for _ in range(2)]
            Pt = [P.sb("Pt", [128, 512], BF16) for _ in range(2)]
            asb, cvt = bufA, bufB
            Rbc = [bufB[:, 0:512], bufB[:, 512:1024]]
            Mex = [bufB[:, 1024:1536], bufB[:, 1536:2048]]
            hsum, e1, e2 = bufA[:, 0:512], bufA[:, 512:1024], bufA[:, 1024:1536]
            pss = [P.ps("pss", [128, 512]) for _ in range(2)]
            psn = P.ps("psn", [128, 512])
            psdn = P.ps("psdn", [128, 512])
            psx = [P.ps("psx", [128, 512]) for _ in range(2)]
            nx = 0
            nw = 0
            for h in range(8):
                P.op("pool", lambda e: e.memset(asb[:, 0:1], 0.0), writes=["masbh"])
                P.op("pool", lambda e: e.memset(asb[:, L + 1:L + 2], 0.0), writes=["masbh"])
                for (col0, dstT, ci_, kd_) in ((h * 128, qT, h, "qT"), (1024 + h * 128, kT, 8 + h, "kT")):
                    wt = wb[nw % 2]
                    wkey = ("wbuf", nw % 2)
                    nw += 1
                    P.dma("pool", wt[:], win[:, :, col0:col0 + 128], writes=[wkey])
                    for tt in range(NTT):
                        pb = psx[nx % 2]
                        for kc in range(8):
                            P.op("pe", lambda e, kc=kc, tt=tt, pb=pb, wt=wt: e.matmul(pb[:], lhsT=wt[:, kc, :], rhs=C.xbf[:, kc, tt * 512:(tt + 1) * 512], start=(kc == 0), stop=(kc == 7)),
                                 reads=[wkey, ("xbf", kc, tt)], writes=[("psx", nx % 2)])
                        P.op("act", lambda e, tt=tt, pb=pb: e.activation(out=asb[:, 1 + tt * 512:1 + (tt + 1) * 512], in_=pb[:], func=AF.Copy),
                             reads=[("psx", nx % 2)], writes=["masb"])
                        nx += 1
                    P.op("dve", lambda e, ci_=ci_: e.tensor_scalar(out=cvt[:], in0=asb[:, 1:L + 1], scalar1=cwq[:, ci_, 1:2], scalar2=cbq[:, ci_:ci_ + 1], op0=ALU.mult, op1=ALU.add),
                         reads=["masb", "mcw"], writes=["mcv"])
                    P.op("dve", lambda e, ci_=ci_: e.scalar_tensor_tensor(out=cvt[:], in0=asb[:, 0:L], scalar=cwq[:, ci_, 0:1], in1=cvt[:], op0=ALU.mult, op1=ALU.add),
                         reads=["masb", "masbh", "mcv"], writes=["mcv"])
                    P.op("dve", lambda e, ci_=ci_: e.scalar_tensor_tensor(out=cvt[:], in0=asb[:, 2:L + 2], scalar=cwq[:, ci_, 2:3], in1=cvt[:], op0=ALU.mult, op1=ALU.add),
                         reads=["masb", "masbh", "mcv"], writes=["mcv"])
                    P.op("act", lambda e, dstT=dstT: e.activation(out=dstT[:], in_=cvt[:], func=AF.Silu), reads=["mcv"], writes=[kd_])
                wt = wb[nw % 2]
                wkey = ("wbuf", nw % 2)
                nw += 1
                P.dma("pool", wt[:], win[:, :, 2048 + h * 128:2048 + (h + 1) * 128], writes=[wkey])
                for nb in range(NB):
                    pb = psx[nx % 2]
                    for kc in range(8):
                        P.op("pe", lambda e, kc=kc, nb=nb, pb=pb, wt=wt: e.matmul(pb[:, 0:128], lhsT=C.xbf[:, kc, nb * 128:(nb + 1) * 128], rhs=wt[:, kc, :], start=(kc == 0), stop=(kc == 7)),
                             reads=[wkey, ("xbf", kc, nb // 4)], writes=[("psx", nx % 2)])
                    if nb % 2 == 0:
                        P.op("act", lambda e, nb=nb, pb=pb: e.activation(out=vt[:, nb, :], in_=pb[:, 0:128], func=AF.Copy), reads=[("psx", nx % 2)], writes=["vt"])
                    else:
                        P.op("dve", lambda e, nb=nb, pb=pb: e.tensor_copy(out=vt[:, nb, :], in_=pb[:, 0:128]), reads=[("psx", nx % 2)], writes=["vt"])
                    nx += 1
                wo = wb[nw % 2]
                wokey = ("wbuf", nw % 2)
                nw += 1
                P.dma("pool", wo[:], win[:, :, 3072 + h * 128:3072 + (h + 1) * 128], writes=[wokey])
                P.barrier()
                it = 0
                for tt in range(NTT):
                    ts = slice(tt * 512, (tt + 1) * 512)
                    for d in range(2):
                        r = slice(RB[d], RB[d] + 8)
                        pb = psx[nx % 2]
                        P.op("pe", lambda e, ts=ts, pb=pb, h=h, r=r: e.matmul(pb[:], lhsT=C.sel[r, h, :], rhs=nmrel[r, ts], start=True, stop=True),
                             reads=["sel", ("gnm", d)], writes=[("psx", nx % 2)])
                        P.op("dve", lambda e, d=d, pb=pb: e.tensor_copy(out=Rbc[d], in_=pb[:]), reads=[("psx", nx % 2)], writes=[("Rbc", d)])
                        nx += 1
                        pb = psx[nx % 2]
                        P.op("pe", lambda e, ts=ts, pb=pb, h=h, r=r: e.matmul(pb[:], lhsT=C.sel[r, h, :], rhs=negm[r, ts], start=True, stop=True),
                             reads=["sel", ("gm", d)], writes=[("psx", nx % 2)])
                        P.op("act", lambda e, d=d, pb=pb: e.activation(out=Mex[d], in_=pb[:], func=AF.Exp), reads=[("psx", nx % 2)], writes=[("Mex", d)])
                        nx += 1
                        if d == 0:
                            jl = list(range(0, 4 * tt + 4))
                        else:
                            jl = list(range(NB - 1, 4 * tt - 1, -1))
                        def front(ji, jb, b):
                            jj = jb - 4 * tt
                            if d == 0:
                                c0, c1 = (max(jj, 0) * 128, 512)
                                dc = c0 if 0 <= jj < 4 else None
                            else:
                                c0, c1 = (0, (min(jj, 3) + 1) * 128)
                                dc = c1 - 128 if 0 <= jj < 4 else None
                            cs = slice(c0, c1)
                            qs = slice(tt * 512 + c0, tt * 512 + c1)
                            P.op("pe", lambda e, jb=jb, b=b, cs=cs, qs=qs: e.matmul(pss[b][:, cs], lhsT=kT[:, jb * 128:(jb + 1) * 128], rhs=qT[:, qs], start=True, stop=True),
                                 reads=["qT", "kT"], writes=[("pss", b)])
                            P.op("act", lambda e, jb=jb, b=b, d=d, h=h, cs=cs: e.activation(out=Dt[b][:, cs], in_=Rbc[d][:, cs], func=AF.Exp, bias=colT[:, d, jb, h:h + 1], scale=1.0),
                                 reads=[("Rbc", d), "colT"], writes=[("Dt", b)])
                            if dc is not None:
                                P.op("pool", lambda e, b=b, d=d, dc=dc: e.tensor_tensor(out=Dt[b][:, dc:dc + 128], in0=Dt[b][:, dc:dc + 128], in1=tri[:, d, :], op=ALU.mult),
                                     reads=[("Dt", b), "tri"], writes=[("Dt", b)])
                            P.op("dve", lambda e, b=b, cs=cs: e.scalar_tensor_tensor(out=Pt[b][:, cs], in0=pss[b][:, cs], scalar=SCALE, in1=Dt[b][:, cs], op0=ALU.mult, op1=ALU.mult),
                                 reads=[("pss", b), ("Dt", b)], writes=[("Pt", b)])
                            return cs

                        def back(ji, jb, b, cs):
                            P.op("pe", lambda e, jb=jb, b=b, ji=ji, jl=jl, cs=cs: e.matmul(psn[:, cs], lhsT=vt[:, jb, :], rhs=Pt[b][:, cs], start=(ji == 0), stop=(ji == len(jl) - 1)),
                                 reads=["vt", ("Pt", b)], writes=["psn"])
                            P.op("pe", lambda e, jb=jb, b=b, ji=ji, jl=jl, cs=cs: e.matmul(psdn[:, cs], lhsT=C.onesb[:], rhs=Pt[b][:, cs], start=(ji == 0), stop=(ji == len(jl) - 1)),
                                 reads=["onesb", ("Pt", b)], writes=["psdn"])

                        bufs = [(it + ji) % 2 for ji in range(len(jl))]
                        it += len(jl)
                        csl = {0: front(0, jl[0], bufs[0])}
                        for ji, jb in enumerate(jl):
                            if ji + 1 < len(jl):
                                csl[ji + 1] = front(ji + 1, jl[ji + 1], bufs[ji + 1])
                            back(ji, jb, bufs[ji], csl[ji])
                        P.op("act", lambda e: e.activation(out=e1, in_=psdn[:], func=AF.Abs), reads=["psdn"], writes=["e1"])
                        P.op("dve", lambda e, d=d: e.tensor_tensor(out=e1, in0=e1, in1=Mex[d], op=ALU.max), reads=["e1", ("Mex", d)], writes=["e1"])
                        P.op("dve", lambda e: e.reciprocal(out=e1, in_=e1), reads=["e1"], writes=["e1"])
                        if d == 0:
                            P.op("dve", lambda e: e.tensor_tensor(out=hsum, in0=psn[:], in1=e1, op=ALU.mult), reads=["psn", "e1"], writes=["hsum"])
                        else:
                            P.op("dve", lambda e: e.tensor_tensor(out=e2, in0=psn[:], in1=e1, op=ALU.mult), reads=["psn", "e1"], writes=["e2"])
                            P.op("pool", lambda e: e.tensor_tensor(out=hsum, in0=hsum, in1=e2, op=ALU.add), reads=["e2", "hsum"], writes=["hsum"])
                    pb = psx[nx % 2]
                    for kc in range(8):
                        P.op("pe", lambda e, kc=kc, ts=ts, pb=pb, wo=wo: e.matmul(pb[:], lhsT=wo[:, kc, :], rhs=C.xbf[:, kc, ts], start=(kc == 0), stop=(kc == 7)),
                             reads=[wokey, ("xbf", kc, tt)], writes=[("psx", nx % 2)])
                    P.op("act", lambda e, pb=pb: e.activation(out=sgo[:], in_=pb[:], func=AF.Sigmoid), reads=[("psx", nx % 2)], writes=["sgo"])
                    nx += 1
                    P.op("pe", lambda e: e.matmul(psn[:], lhsT=C.onesf[:], rhs=hsum, start=True, stop=True), reads=["hsum", "onesf"], writes=["psn"])
                    P.op("act", lambda e: e.activation(out=e1, in_=hsum, func=AF.Square), reads=["hsum", "e1"], writes=["e1"])
                    P.op("pe", lambda e: e.matmul(psdn[:], lhsT=C.onesf[:], rhs=e1, start=True, stop=True), reads=["e1", "onesf"], writes=["psdn"])
                    P.op("act", lambda e: e.activation(out=e2, in_=psn[:], func=AF.Copy), reads=["psn", "e2"], writes=["e2"])
                    P.op("dve", lambda e: e.tensor_tensor(out=e1, in0=e2, in1=e2, op=ALU.mult), reads=["e2", "e1"], writes=["e1"])
                    P.op("dve", lambda e: e.tensor_tensor(out=e1, in0=psdn[:], in1=e1, op=ALU.subtract), reads=["psdn", "e1"], writes=["e1"])
                    P.op("act", lambda e: e.activation(out=e1, in_=e1, func=AF.Sqrt, bias=C.epsc[:, 0:1], scale=1.0), reads=["e1", "epsc"], writes=["e1"])
                    P.op("dve", lambda e: e.reciprocal(out=e1, in_=e1), reads=["e1"], writes=["e1"])
                    P.op("pool", lambda e: e.tensor_tensor(out=e2, in0=hsum, in1=e2, op=ALU.subtract), reads=["hsum", "e2"], writes=["e2"])
                    P.op("dve", lambda e: e.tensor_tensor(out=e2, in0=e2, in1=e1, op=ALU.mult), reads=["e1", "e2"], writes=["e2"])
                    P.op("dve", lambda e, ts=ts, h=h: e.scalar_tensor_tensor(out=hTf[:, h, ts], in0=e2, scalar=ng[:, h:h + 1], in1=sgo[:], op0=ALU.mult, op1=ALU.mult),
                         reads=["e2", "ng", "sgo"], writes=[("yT", h)])
                P.barrier()
        with P.phase():
            T = ln_temps(P)
            emit_mixer_out(C, l, hTf, W["od_w_out"][j], 8, T)


def emit_even(C, l):
    P = C.P
    W = C.W
    j = l // 2
    win = kc_view(W["ev_w_in"][j])
    PI = math.pi
    with P.phase():
        yA = P.sb("yA", [128, 4, L], BF16)
        hS = P.sb("hS", [128, 4, L], BF16)
        with P.phase():
            qT = P.sb("qT", [128, 4, L], BF16)
            kT = P.sb("kT", [128, L], BF16)
            vt = P.sb("vt", [128, NB, 128], BF16)
            ab8 = P.sb("ab8", [128, 8, 3, 128], BF16)
            P.dma("pool", ab8[:], C.consts["abias"], writes=["ab8"])
            esk = P.sb("esk", [128, 4])
            for hf in range(2):
                P.dma("sp", esk[hf * 64:(hf + 1) * 64, :], W["ev_sink"][j:j + 1, hf * 4:hf * 4 + 4].partition_broadcast(64), writes=["esk"])
            P.op("act", lambda e: e.activation(out=esk[:], in_=esk[:], func=AF.Exp), reads=["esk"], writes=["esk"])
            wb = [P.sb("wbuf", [128, 8, 128], BF16) for _ in range(2)]
            psx = [P.ps("psx", [128, 512]) for _ in range(2)]
            nx = 0
            nw = 0
            wq_view = win[:, :, 0:512].rearrange("k kc (hf c d) -> k kc c hf d", hf=2, c=4)
            for c in range(5):
                wt = wb[nw % 2]
                wkey = ("wbuf", nw % 2)
                nw += 1
                if c < 4:
                    for kc in range(8):
                        P.dma("pool", wt[:, kc, :].rearrange("k (hf d) -> k hf d", hf=2), wq_view[:, kc, c, :, :], writes=[wkey])
                    dst = qT[:, c, :]
                else:
                    P.dma("pool", wt[:], win[:, :, 512:640], writes=[wkey])
                    dst = kT[:, :]
                for tt in range(NTT):
                    pb = psx[nx % 2]
                    for kc in range(8):
                        P.op("pe", lambda e, kc=kc, tt=tt, pb=pb, wt=wt: e.matmul(pb[:], lhsT=wt[:, kc, :], rhs=C.xbf[:, kc, tt * 512:(tt + 1) * 512], start=(kc == 0), stop=(kc == 7)),
                             reads=[wkey, ("xbf", kc, tt)], writes=[("psx", nx % 2)])
                    if tt % 2 == 0:
                        P.op("act", lambda e, tt=tt, pb=pb, dst=dst: e.activation(out=dst[:, tt * 512:(tt + 1) * 512], in_=pb[:], func=AF.Copy), reads=[("psx", nx % 2)], writes=["qk"])
                    else:
                        P.op("dve", lambda e, tt=tt, pb=pb, dst=dst: e.tensor_copy(out=dst[:, tt * 512:(tt + 1) * 512], in_=pb[:]), reads=[("psx", nx % 2)], writes=["qk"])
                    nx += 1
            wt = wb[nw % 2]
            wkey = ("wbuf", nw % 2)
            nw += 1
            P.dma("pool", wt[:], win[:, :, 640:768], writes=[wkey])
            for nb in range(NB):
                pb = psx[nx % 2]
                for kc in range(8):
                    P.op("pe", lambda e, kc=kc, nb=nb, pb=pb, wt=wt: e.matmul(pb[:, 0:128], lhsT=C.xbf[:, kc, nb * 128:(nb + 1) * 128], rhs=wt[:, kc, :], start=(kc == 0), stop=(kc == 7)),
                         reads=[wkey, ("xbf", kc, nb // 4)], writes=[("psx", nx % 2)])
                if nb % 2 == 0:
                    P.op("act", lambda e, nb=nb, pb=pb: e.activation(out=vt[:, nb, :], in_=pb[:, 0:128], func=AF.Copy), reads=[("psx", nx % 2)], writes=["vt"])
                else:
                    P.op("dve", lambda e, nb=nb, pb=pb: e.tensor_copy(out=vt[:, nb, :], in_=pb[:, 0:128]), reads=[("psx", nx % 2)], writes=["vt"])
                nx += 1
            PT = [P.sb("PT", [128, 3, 128], BF16) for _ in range(2)]
            dn = P.sb("dn", [128, 128])
            pss = [P.ps("pss", [128, 512]) for _ in range(2)]
            psn = P.ps("psn", [128, 512])
            psdn = P.ps("psdn", [128, 512])
            units = [(c, qb, hf) for c in range(4) for qb in range(NB) for hf in range(2)]

            def afront(u, b):
                c, qb, hf = u
                qs = slice(qb * 128, (qb + 1) * 128)
                h = hf * 4 + c
                r = slice(hf * 64, (hf + 1) * 64)
                rl = [r3 for r3 in range(3) if 0 <= qb + r3 - 1 < NB]
                for r3 in rl:
                    kb = qb + r3 - 1
                    P.op("pe", lambda e, b=b, r3=r3, kb=kb, r=r, c=c, qs=qs: e.matmul(pss[b][:, r3 * 128:(r3 + 1) * 128], lhsT=kT[r, kb * 128:(kb + 1) * 128], rhs=qT[r, c, qs], start=True, stop=False),
                         reads=["qk"], writes=[("pss", b)])
                    P.op("pe", lambda e, b=b, r3=r3, h=h: e.matmul(pss[b][:, r3 * 128:(r3 + 1) * 128], lhsT=C.identb[:], rhs=ab8[:, h, r3, :], start=False, stop=True),
                         reads=["identb", "ab8"], writes=[("pss", b)])
                c0, c1 = rl[0] * 128, (rl[-1] + 1) * 128
                P.op("act", lambda e, b=b, c0=c0, c1=c1: e.activation(out=PT[b][:].rearrange("p a b -> p (a b)")[:, c0:c1], in_=pss[b][:, c0:c1], func=AF.Exp, scale=0.125),
                     reads=[("pss", b)], writes=[("PT", b)])

            def aback(u, b):
                c, qb, hf = u
                qs = slice(qb * 128, (qb + 1) * 128)
                r = slice(hf * 64, (hf + 1) * 64)
                rl = [r3 for r3 in range(3) if 0 <= qb + r3 - 1 < NB]
                for i3, r3 in enumerate(rl):
                    kb = qb + r3 - 1
                    P.op("pe", lambda e, b=b, r3=r3, kb=kb, r=r, hf=hf, i3=i3, rl=rl: e.matmul(psn[r, 0:128], lhsT=vt[:, kb, hf * 64:(hf + 1) * 64], rhs=PT[b][:, r3, :], start=(i3 == 0), stop=(i3 == len(rl) - 1)),
                         reads=["vt", ("PT", b)], writes=["psn"])
                for i3, r3 in enumerate(rl):
                    P.op("pe", lambda e, b=b, r3=r3, r=r, i3=i3, rl=rl: e.matmul(psdn[r, 0:128], lhsT=C.onesb[:, 0:64], rhs=PT[b][:, r3, :], start=(i3 == 0), stop=(i3 == len(rl) - 1)),
                         reads=["onesb", ("PT", b)], writes=["psdn"])
                if hf == 1:
                    P.op("dve", lambda e, c=c: e.tensor_scalar(out=dn[:], in0=psdn[:, 0:128], scalar1=esk[:, c:c + 1], scalar2=None, op0=ALU.add), reads=["psdn", "esk"], writes=["dn"])
                    P.op("dve", lambda e: e.reciprocal(out=dn[:], in_=dn[:]), reads=["dn"], writes=["dn"])
                    P.op("dve", lambda e, c=c, qs=qs: e.tensor_tensor(out=yA[:, c, qs], in0=psn[:, 0:128], in1=dn[:], op=ALU.mult), reads=["psn", "dn"], writes=[("yA", c)])

            afront(units[0], 0)
            for i, u in enumerate(units):
                if i + 1 < len(units):
                    afront(units[i + 1], (i + 1) % 2)
                aback(u, i % 2)
        with P.phase():
            NPW = 22
            pw_exp = list(range(1, 17)) + [32, 64, 128, 256, 512, 1024]
            pwr = P.sb("pwr", [128, 32, NPW])
            pwi = P.sb("pwi", [128, 32, NPW])
            pni = P.sb("pni", [128, 32, NPW])
            BTp = P.sb("BTp", [128, 4, 2, 2, 2, 128], BF16)
            CTp = P.sb("CTp", [128, 16, 2, 2, 64], BF16)
            dsk = P.sb("dsk", [128, 4])
            P.dma("sp", dsk[:], W["s5_d"][j].rearrange("(c p) -> p c", p=128), writes=["dsk"], allow_slow_non_contiguous=True)
            with P.phase():
                lin = P.sb("lin", [32, 2, 128])
                P.dma("sp", lin[:, 0, :], W["s5_a_re"][j].rearrange("d (gp g2) p -> (d gp) (g2 p)", g2=2), writes=["lin"])
                P.dma("act", lin[:, 1, :], W["s5_a_im"][j].rearrange("d (gp g2) p -> (d gp) (g2 p)", g2=2), writes=["lin"])
                pst = P.ps("pst", [128, 512])
                aT = P.sb("aT", [128, 2, 32])
                for i in range(2):
                    P.op("pe", lambda e, i=i: e.transpose(pst[:, i * 32:(i + 1) * 32], lin[:, i, :], C.ident[0:32, 0:32]), reads=["lin", "ident"], writes=["pst"])
                P.op("dve", lambda e: e.tensor_copy(out=aT[:].rearrange("p a b -> p (a b)"), in_=pst[:, 0:64]), reads=["pst"], writes=["aT"])
                dt = P.sb("dt", [128, 32])
                for d in range(2):
                    for g2 in range(2):
                        src = W["s5_log_dt"][j, d:d + 1].rearrange("o (gp g2) -> o g2 gp", g2=2)[:, g2, :]
                        P.dma("sp", dt[g2 * 64:(g2 + 1) * 64, d * 16:(d + 1) * 16], src.partition_broadcast(64), writes=["dt"], allow_slow_non_contiguous=True)
                P.op("act", lambda e: e.activation(out=dt[:], in_=dt[:], func=AF.Exp), reads=["dt"], writes=["dt"])
                tA = [P.sb("tA%d" % i, [128, 32]) for i in range(8)]
                mag, ang, cs, sn, zr, zi, t6, t7 = tA
                ar, ai = aT[:, 0, :], aT[:, 1, :]
                TT = lambda out, a, b, op, rk, wk: P.op("dve", lambda e: e.tensor_tensor(out=out, in0=a, in1=b, op=op), reads=rk, writes=wk)
                TT(mag[:], ar, dt[:], ALU.mult, ["aT", "dt"], ["mag"])
                P.op("act", lambda e: e.activation(out=mag[:], in_=mag[:], func=AF.Exp), reads=["mag"], writes=["mag"])
                TT(ang[:], ai, dt[:], ALU.mult, ["aT", "dt"], ["ang"])
                ki = P.sb("ki", [128, 32], mybir.dt.int32)
                kf = P.sb("kf", [128, 32])
                for (dst, shift, key) in ((sn, 0.0, "sn"), (cs, 0.5 * PI, "cs")):
                    P.op("dve", lambda e, shift=shift: e.tensor_scalar(out=kf[:], in0=ang[:], scalar1=shift, scalar2=1.0 / (2 * PI), op0=ALU.add, op1=ALU.mult), reads=["ang", "kf"], writes=["kf"])
                    P.op("dve", lambda e: e.tensor_copy(out=ki[:], in_=kf[:]), reads=["kf", "ki"], writes=["ki"])
                    P.op("dve", lambda e: e.tensor_copy(out=kf[:], in_=ki[:]), reads=["ki"], writes=["kf"])
                    P.op("dve", lambda e, dst=dst, shift=shift: e.tensor_scalar(out=dst[:], in0=ang[:], scalar1=shift, scalar2=None, op0=ALU.add), reads=["ang"], writes=[key])
                    P.op("dve", lambda e, dst=dst: e.scalar_tensor_tensor(out=dst[:], in0=kf[:], scalar=-2 * PI, in1=dst[:], op0=ALU.mult, op1=ALU.add), reads=["kf", key], writes=[key])
                    P.op("dve", lambda e, dst=dst: e.tensor_scalar(out=kf[:], in0=dst[:], scalar1=PI, scalar2=-2 * PI, op0=ALU.is_gt, op1=ALU.mult), reads=[key, "kf"], writes=["kf"])
                    P.op("dve", lambda e, dst=dst: e.tensor_tensor(out=dst[:], in0=dst[:], in1=kf[:], op=ALU.add), reads=["kf", key], writes=[key])
                    P.op("dve", lambda e, dst=dst: e.tensor_scalar(out=kf[:], in0=dst[:], scalar1=-PI, scalar2=2 * PI, op0=ALU.is_lt, op1=ALU.mult), reads=[key, "kf"], writes=["kf"])
                    P.op("dve", lambda e, dst=dst: e.tensor_tensor(out=dst[:], in0=dst[:], in1=kf[:], op=ALU.add), reads=["kf", key], writes=[key])
                P.op("act", lambda e: e.activation(out=sn[:], in_=sn[:], func=AF.Sin), reads=["sn"], writes=["sn"])
                P.op("act", lambda e: e.activation(out=cs[:], in_=cs[:], func=AF.Sin), reads=["cs"], writes=["cs"])
                TT(pwr[:, :, 0], mag[:], cs[:], ALU.mult, ["mag", "cs"], ["pw"])
                TT(pwi[:, :, 0], mag[:], sn[:], ALU.mult, ["mag", "sn"], ["pw"])
                P.op("dve", lambda e: e.tensor_scalar(out=t6[:], in0=pwr[:, :, 0], scalar1=-1.0, scalar2=None, op0=ALU.add), reads=["pw"], writes=["t6"])
                TT(zr[:], t6[:], ar, ALU.mult, ["t6", "aT"], ["zr"])
                TT(t7[:], pwi[:, :, 0], ai, ALU.mult, ["pw", "aT"], ["t7"])
                TT(zr[:], zr[:], t7[:], ALU.add, ["zr", "t7"], ["zr"])
                TT(zi[:], pwi[:, :, 0], ar, ALU.mult, ["pw", "aT"], ["zi"])
                TT(t7[:], t6[:], ai, ALU.mult, ["t6", "aT", "t7"], ["t7"])
                TT(zi[:], zi[:], t7[:], ALU.subtract, ["zi", "t7"], ["zi"])
                TT(t6[:], ar, ar, ALU.mult, ["aT", "t6"], ["t6"])
                TT(t7[:], ai, ai, ALU.mult, ["aT", "t7"], ["t7"])
                TT(t6[:], t6[:], t7[:], ALU.add, ["t6", "t7"], ["t6"])
                P.op("dve", lambda e: e.reciprocal(out=t6[:], in_=t6[:]), reads=["t6"], writes=["t6"])
                TT(zr[:], zr[:], t6[:], ALU.mult, ["zr", "t6"], ["zr"])
                TT(zi[:], zi[:], t6[:], ALU.mult, ["zi", "t6"], ["zi"])
                nzr, nzi = mag, ang
                P.op("dve", lambda e: e.tensor_scalar(out=nzr[:], in0=zr[:], scalar1=-1.0, scalar2=None, op0=ALU.mult), reads=["zr", "mag"], writes=["nzr"])
                P.op("dve", lambda e: e.tensor_scalar(out=nzi[:], in0=zi[:], scalar1=-1.0, scalar2=None, op0=ALU.mult), reads=["zi", "ang"], writes=["nzi"])
                def cmul(oi, ai_, bi_):
                    TT(t6[:], pwr[:, :, ai_], pwr[:, :, bi_], ALU.mult, ["pw", "t6"], ["t6"])
                    TT(t7[:], pwi[:, :, ai_], pwi[:, :, bi_], ALU.mult, ["pw", "t7"], ["t7"])
                    TT(pwr[:, :, oi], t6[:], t7[:], ALU.subtract, ["t6", "t7"], ["pw"])
                    TT(t6[:], pwr[:, :, ai_], pwi[:, :, bi_], ALU.mult, ["pw", "t6"], ["t6"])
                    TT(t7[:], pwi[:, :, ai_], pwr[:, :, bi_], ALU.mult, ["pw", "t7"], ["t7"])
                    TT(pwi[:, :, oi], t6[:], t7[:], ALU.add, ["t6", "t7"], ["pw"])
                for k in range(1, 16):
                    cmul(k, k - 1, 0)
                for k in range(16, NPW):
                    cmul(k, k - 1, k - 1)
                P.op("dve", lambda e: e.tensor_scalar(out=pni[:], in0=pwi[:], scalar1=-1.0, scalar2=None, op0=ALU.mult), reads=["pw"], writes=["pni"])
                Bin = P.sb("Bin", [128, 16, 16])
                Bexp = P.sb("Bexp", [128, 16, 128], BF16)
                pbt = P.ps("pbt", [128, 4, 128], BF16)
                for d in range(2):
                    for ri in range(2):
                        src = W["s5_b_re" if ri == 0 else "s5_b_im"][j, d].rearrange("(gp g2) p h -> (g2 p) gp h", g2=2)
                        for hh in range(2):
                            P.dma("sp" if hh == 0 else "act", Bin[:, hh * 8:(hh + 1) * 8, :], src[:, hh * 8:(hh + 1) * 8, :], writes=["Bin"])
                        P.op("pool", lambda e: e.memset(Bexp[:], 0.0), writes=["Bexp"])
                        for g2 in range(2):
                            for q in range(4):
                                P.op("dve", lambda e, g2=g2, q=q: e.tensor_copy(
                                    out=Bexp[g2 * 64:(g2 + 1) * 64, q:16:4, q * 32 + g2 * 16:q * 32 + g2 * 16 + 16],
                                    in_=Bin[g2 * 64:(g2 + 1) * 64, q:16:4, :]), reads=["Bin", "Bexp"], writes=["Bexp"])
                        for c in range(4):
                            for q in range(4):
                                P.op("pe", lambda e, c=c, q=q: e.transpose(pbt[:, q, :], Bexp[:, 4 * c + q, :], C.identb[:]), reads=["Bexp", "identb"], writes=["pspbt"])
                            for q in range(4):
                                rr = slice((q // 2) * 64, (q // 2) * 64 + 64)
                                P.op("dve", lambda e, c=c, q=q, d=d, ri=ri, rr=rr: e.tensor_copy(out=BTp[rr, c, q % 2, d, ri, :], in_=pbt[rr, q, :]), reads=["pspbt"], writes=["BTp"])
                Cin = P.sb("Cin", [128, 2, 64])
                Craw = P.sb("Craw", [128, 16, 2, 2, 16])
                pct = P.ps("psct", [128, 512])
                for d in range(2):
                    for ri in range(2):
                        srcC = W["s5_c_re" if ri == 0 else "s5_c_im"][j, d].rearrange("g h p -> (g h) p")
                        for t in range(4):
                            for hh in range(2):
                                P.dma("sp" if hh == 0 else "act", Cin[:, hh, :], srcC[t * 128:(t + 1) * 128, :], writes=["Cin"])
                            P.op("pe", lambda e: e.transpose(pct[:, 0:128], Cin[:].rearrange("p a b -> p (a b)"), C.ident[:]), reads=["Cin", "ident"], writes=["psct"])
                            for g2 in range(2):
                                P.op("dve", lambda e, g2=g2, t=t, d=d, ri=ri: e.tensor_copy(
                                    out=Craw[g2 * 64:(g2 + 1) * 64, 4 * t:4 * t + 4, d, ri, :],
                                    in_=pct[g2 * 64:(g2 + 1) * 64, 0:128].rearrange("p (i g h) -> p i g h", g=2, h=16)[:, :, g2, :]),
                                    reads=["psct"], writes=["Craw"])
                P.op("pool", lambda e: e.memset(CTp[:], 0.0), writes=["CTp"])
                tc1 = P.sb("tc1", [128, 16])
                for d in range(2):
                    for gp in range(16):
                        col = d * 16 + gp
                        w = gp % 2
                        for ri, (s1, s2) in enumerate(((zr, nzi), (nzi, nzr))):
                            P.op("dve", lambda e, gp=gp, d=d, s1=s1, col=col: e.tensor_scalar(out=tc1[:], in0=Craw[:, gp, d, 0, :], scalar1=s1[:, col:col + 1], scalar2=None, op0=ALU.mult),
                                 reads=["Craw", "zr", "nzr", "nzi"], writes=["tc1"])
                            for g2 in range(2):
                                rr = slice(g2 * 64, (g2 + 1) * 64)
                                P.op("dve", lambda e, gp=gp, d=d, s2=s2, col=col, rr=rr, ri=ri, w=w, g2=g2: e.scalar_tensor_tensor(
                                    out=CTp[rr, gp, d, ri, w * 32 + g2 * 16:w * 32 + g2 * 16 + 16], in0=Craw[rr, gp, d, 1, :], scalar=s2[rr, col:col + 1], in1=tc1[rr, :], op0=ALU.mult, op1=ALU.add),
                                    reads=["Craw", "tc1", "zr", "nzr", "nzi", "CTp"], writes=["CTp"])
            if C.dbgout:
                P.dma("sp", C.dbgout["pwr"], pwr[:], reads=["pw"])
                P.dma("sp", C.dbgout["pwi"], pwi[:], reads=["pw"])
                P.dma("sp", C.dbgout["BTp"], BTp[:], reads=["BTp"])
                P.dma("sp", C.dbgout["CTp"], CTp[:], reads=["CTp"])
            wb = [P.sb("wbuf", [128, 8, 128], BF16) for _ in range(2)]
            uT = P.sb("uT", [128, L], BF16)
            Xr = P.sb("Xr", [128, 16, 128])
            Xi = P.sb("Xi", [128, 16, 128])
            Xbr = P.sb("Xbr", [128, L], BF16)
            Xbi = P.sb("Xbi", [128, L], BF16)
            Sb = [[P.sb("Sb", [128, 130]) for _ in range(2)] for _ in range(2)]
            pt_ = P.sb("s5post", [128, 512])
            psy = [P.ps("psy", [128, 512]) for _ in range(4)]
            psb = [P.ps("psb", [128, 512]) for _ in range(4)]
            Xbr3 = Xbr[:].rearrange("p (n a) -> p a n", a=16)
            Xbi3 = Xbi[:].rearrange("p (n a) -> p a n", a=16)
            for i in range(2):
                for k in range(2):
                    P.op("pool", lambda e, i=i, k=k: e.memset(Sb[i][k][:], 0.0), writes=[("Sb", i, k)])
            nb_ = 0
            for c in range(4):
                wt = wb[c % 2]
                wkey = ("wbuf", c % 2)
                P.dma("pool", wt[:], win[:, :, 768 + c * 128:768 + (c + 1) * 128], writes=[wkey])
                for tt in range(NTT):
                    pb = psb[nb_ % 4]
                    for kc in range(8):
                        P.op("pe", lambda e, kc=kc, tt=tt, pb=pb, wt=wt: e.matmul(pb[:], lhsT=wt[:, kc, :], rhs=C.xbf[:, kc, tt * 512:(tt + 1) * 512], start=(kc == 0), stop=(kc == 7)),
                             reads=[wkey, ("xbf", kc, tt)], writes=[("psb", nb_ % 4)])
                    P.op("act", lambda e, tt=tt, pb=pb: e.activation(out=uT[:, tt * 512:(tt + 1) * 512], in_=pb[:], func=AF.Copy), reads=[("psb", nb_ % 4)], writes=["uT"])
                    nb_ += 1
                for q in range(4):
                    gp = 4 * c + q
                    kr = slice((q // 2) * 64, (q // 2) * 64 + 64)
                    for d in range(2):
                        col = d * 16 + gp
                        LR = lambda k, col=col: pwr[:, col, k:k + 1]
                        LI = lambda k, col=col: pwi[:, col, k:k + 1]
                        NI = lambda k, col=col: pni[:, col, k:k + 1]
                        for tt in range(NTT):
                            for ri, X in enumerate((Xr, Xi)):
                                pb = psb[nb_ % 4]
                                P.op("pe", lambda e, tt=tt, pb=pb, ri=ri, kr=kr, q=q, d=d, c=c: e.matmul(pb[:], lhsT=BTp[kr, c, q % 2, d, ri, :], rhs=uT[kr, tt * 512:(tt + 1) * 512], start=True, stop=True),
                                     reads=["BTp", "uT"], writes=[("psb", nb_ % 4)])
                                if ri == 0:
                                    P.op("act", lambda e, tt=tt, pb=pb, X=X: e.activation(out=X[:, :, tt * 32:(tt + 1) * 32], in_=pb[:].rearrange("p (n a) -> p a n", a=16), func=AF.Copy),
                                         reads=[("psb", nb_ % 4)], writes=["X"])
                                else:
                                    P.op("act", lambda e, tt=tt, pb=pb, X=X: e.activation(out=X[:, :, tt * 32:(tt + 1) * 32], in_=pb[:].rearrange("p (n a) -> p a n", a=16), func=AF.Copy),
                                         reads=[("psb", nb_ % 4)], writes=["X"])
                                nb_ += 1
                        steps = range(1, 16) if d == 0 else range(14, -1, -1)
                        for a in steps:
                            ap_ = a - 1 if d == 0 else a + 1
                            P.op("dve", lambda e, sc_=LR(0), a=a, ap_=ap_: e.scalar_tensor_tensor(out=Xr[:, a, :], in0=Xr[:, ap_, :], scalar=sc_, in1=Xr[:, a, :], op0=ALU.mult, op1=ALU.add), reads=["X", "pw"], writes=["X"])
                            P.op("dve", lambda e, sc_=LR(0), a=a, ap_=ap_: e.scalar_tensor_tensor(out=Xi[:, a, :], in0=Xi[:, ap_, :], scalar=sc_, in1=Xi[:, a, :], op0=ALU.mult, op1=ALU.add), reads=["X", "pw"], writes=["X"])
                            P.op("dve", lambda e, sc_=NI(0), a=a, ap_=ap_: e.scalar_tensor_tensor(out=Xr[:, a, :], in0=Xi[:, ap_, :], scalar=sc_, in1=Xr[:, a, :], op0=ALU.mult, op1=ALU.add), reads=["X", "pni"], writes=["X"])
                            P.op("dve", lambda e, sc_=LI(0), a=a, ap_=ap_: e.scalar_tensor_tensor(out=Xi[:, a, :], in0=Xr[:, ap_, :], scalar=sc_, in1=Xi[:, a, :], op0=ALU.mult, op1=ALU.add), reads=["X", "pw"], writes=["X"])
                        ae = 15 if d == 0 else 0
                        cur = 0
                        P.op("dve", lambda e, ae=ae: e.tensor_copy(out=Sb[0][0][:, 1:129], in_=Xr[:, ae, :]), reads=["X", ("Sb", 0, 0)], writes=[("Sb", 0, 0)])
                        P.op("dve", lambda e, ae=ae: e.tensor_copy(out=Sb[0][1][:, 1:129], in_=Xi[:, ae, :]), reads=["X", ("Sb", 0, 1)], writes=[("Sb", 0, 1)])
                        s = 1
                        lev = 0
                        while s < 128:
                            k = 15 + lev
                            A, B = Sb[cur], Sb[1 - cur]
                            ka = [("Sb", cur, 0), ("Sb", cur, 1)]
                            kb = [("Sb", 1 - cur, 0), ("Sb", 1 - cur, 1)]
                            if d == 0:
                                o_, i_, h_ = slice(1 + s, 129), slice(1, 129 - s), slice(1, 1 + s)
                            else:
                                o_, i_, h_ = slice(1, 129 - s), slice(1 + s, 129), slice(129 - s, 129)
                            P.op("dve", lambda e, sc_=LR(k), A=A, B=B, o_=o_, i_=i_, k=k: e.scalar_tensor_tensor(out=B[0][:, o_], in0=A[0][:, i_], scalar=sc_, in1=A[0][:, o_], op0=ALU.mult, op1=ALU.add), reads=ka + ["pw"], writes=[kb[0]])
                            P.op("dve", lambda e, sc_=LR(k), A=A, B=B, o_=o_, i_=i_, k=k: e.scalar_tensor_tensor(out=B[1][:, o_], in0=A[1][:, i_], scalar=sc_, in1=A[1][:, o_], op0=ALU.mult, op1=ALU.add), reads=ka + ["pw"], writes=[kb[1]])
                            P.op("dve", lambda e, sc_=NI(k), A=A, B=B, o_=o_, i_=i_, k=k: e.scalar_tensor_tensor(out=B[0][:, o_], in0=A[1][:, i_], scalar=sc_, in1=B[0][:, o_], op0=ALU.mult, op1=ALU.add), reads=ka + ["pni", kb[0]], writes=[kb[0]])
                            P.op("dve", lambda e, sc_=LI(k), A=A, B=B, o_=o_, i_=i_, k=k: e.scalar_tensor_tensor(out=B[1][:, o_], in0=A[0][:, i_], scalar=sc_, in1=B[1][:, o_], op0=ALU.mult, op1=ALU.add), reads=ka + ["pw", kb[1]], writes=[kb[1]])
                            P.op("pool", lambda e, A=A, B=B, h_=h_: e.tensor_copy(out=B[0][:, h_], in_=A[0][:, h_]), reads=[ka[0]], writes=[kb[0]])
                            P.op("pool", lambda e, A=A, B=B, h_=h_: e.tensor_copy(out=B[1][:, h_], in_=A[1][:, h_]), reads=[ka[1]], writes=[kb[1]])
                            cur = 1 - cur
                            s *= 2
                            lev += 1
                        S = Sb[cur]
                        ks = [("Sb", cur, 0), ("Sb", cur, 1)]
                        cin = slice(0, 128) if d == 0 else slice(2, 130)
                        for a in range(16):
                            k = a if d == 0 else 15 - a
                            P.op("dve", lambda e, sc_=LR(k), a=a, k=k, S=S, cin=cin: e.scalar_tensor_tensor(out=Xr[:, a, :], in0=S[0][:, cin], scalar=sc_, in1=Xr[:, a, :], op0=ALU.mult, op1=ALU.add), reads=["X", "pw"] + ks, writes=["X"])
                            P.op("dve", lambda e, sc_=LR(k), a=a, k=k, S=S, cin=cin: e.scalar_tensor_tensor(out=Xi[:, a, :], in0=S[1][:, cin], scalar=sc_, in1=Xi[:, a, :], op0=ALU.mult, op1=ALU.add), reads=["X", "pw"] + ks, writes=["X"])
                            P.op("dve", lambda e, sc_=NI(k), a=a, k=k, S=S, cin=cin: e.scalar_tensor_tensor(out=Xbr3[:, a, :], in0=S[1][:, cin], scalar=sc_, in1=Xr[:, a, :], op0=ALU.mult, op1=ALU.add), reads=["X", "pni"] + ks, writes=["Xb"])
                            P.op("dve", lambda e, sc_=LI(k), a=a, k=k, S=S, cin=cin: e.scalar_tensor_tensor(out=Xbi3[:, a, :], in0=S[0][:, cin], scalar=sc_, in1=Xi[:, a, :], op0=ALU.mult, op1=ALU.add), reads=["X", "pw"] + ks, writes=["Xb"])
                        if C.dbgout and gp == 0 and d == 0:
                            P.dma("sp", C.dbgout["Xbr"], Xbr[:], reads=["Xb"])
                            P.dma("sp", C.dbgout["Xbi"], Xbi[:], reads=["Xb"])
                        for tt in range(NTT):
                            for ri, Xb in enumerate((Xbr, Xbi)):
                                first = (q % 2 == 0 and d == 0 and ri == 0)
                                last = (q % 2 == 1 and d == 1 and ri == 1)
                                P.op("pe", lambda e, tt=tt, ri=ri, Xb=Xb, gp=gp, d=d, kr=kr, first=first, last=last: e.matmul(psy[tt][kr, :], lhsT=CTp[:, gp, d, ri, :], rhs=Xb[:, tt * 512:(tt + 1) * 512], start=first, stop=last),
                                     reads=["CTp", "Xb"], writes=[("psy", tt)])
                for tt in range(NTT):
                    ts = slice(tt * 512, (tt + 1) * 512)
                    P.op("dve", lambda e, tt=tt, ts=ts, c=c: e.scalar_tensor_tensor(out=pt_[:], in0=uT[:, ts], scalar=dsk[:, c:c + 1], in1=psy[tt][:], op0=ALU.mult, op1=ALU.add),
                         reads=["uT", "dsk", ("psy", tt)], writes=["s5post"])
                    P.op("act", lambda e, ts=ts, c=c: e.activation(out=hS[:, c, ts], in_=pt_[:], func=AF.Gelu_apprx_tanh), reads=["s5post"], writes=[("hS", c)])
            P.barrier()
            wglu = P.sb("wglu", [128, 4, 512], BF16) if False else None
        with P.phase():
            wglu = P.sb("wglu", [128, 4, 512], BF16)
            P.dma("pool", wglu[:], W["s5_w_glu"][j].rearrange("(k p) m -> p k m", p=128), writes=["wglu"])
            gs = [P.sb("gs", [128, 512], BF16) for _ in range(4)]
            psg = [P.ps("psgl", [128, 512]) for _ in range(2)]
            n = 0
            for tt in range(NTT):
                ts = slice(tt * 512, (tt + 1) * 512)
                for m in range(4):
                    pb = psg[n % 2]
                    for k in range(4):
                        P.op("pe", lambda e, k=k, m=m, ts=ts, pb=pb: e.matmul(pb[:], lhsT=wglu[:, k, m * 128:(m + 1) * 128], rhs=hS[:, k, ts], start=(k == 0), stop=(k == 3)),
                             reads=["wglu", ("hS", k, tt)], writes=[("psgl", n % 2)])
                    P.op("act", lambda e, m=m, pb=pb: e.activation(out=gs[m][:], in_=pb[:], func=AF.Sigmoid), reads=[("psgl", n % 2)], writes=[("gs", m)])
                    n += 1
                for m in range(4):
                    P.op("dve" if m % 2 == 0 else "pool", lambda e, m=m, ts=ts: e.tensor_tensor(out=hS[:, m, ts], in0=hS[:, m, ts], in1=gs[m][:], op=ALU.mult),
                         reads=[("gs", m)] + [("hS", k, tt) for k in range(4)], writes=[("hS", m, tt)])
        if C.dbgout:
            with P.phase():
                P.dma("sp", C.dbgout["yA"], yA[:], reads=[])
                P.dma("act", C.dbgout["hS"], hS[:], reads=[])
        with P.phase():
            T = ln_temps(P)
            wo = [P.sb("wo", [128, 8, 128], BF16) for _ in range(2)]
            pso = [P.ps("pso", [128, 512]) for _ in range(2)]
            wsrcA = W["ev_w_out"][j][0:512, :].rearrange("(hf c d) m -> hf d c m", hf=2, c=4)
            wsrcS = W["ev_w_out"][j][512:1024, :].rearrange("(k p) m -> p k m", p=128)
            for m in range(8):
                b = m % 2
                ms = slice(m * 128, (m + 1) * 128)
                for hf in range(2):
                    P.dma("pool", wo[b][hf * 64:(hf + 1) * 64, 0:4, :], wsrcA[hf][:, :, ms], writes=[("wo", b)])
                P.dma("pool", wo[b][:, 4:8, :], wsrcS[:, :, ms], writes=[("wo", b)])
                for tt in range(NTT):
                    ts = slice(tt * 512, (tt + 1) * 512)
                    pb = pso[tt % 2]
                    for k in range(8):
                        rhs = yA[:, k, ts] if k < 4 else hS[:, k - 4, ts]
                        P.op("pe", lambda e, k=k, b=b, pb=pb, rhs=rhs: e.matmul(pb[:], lhsT=wo[b][:, k, :], rhs=rhs, start=(k == 0), stop=(k == 7)),
                             reads=[("wo", b)], writes=[("pso", tt % 2)])
                    P.op("dve", lambda e, m=m, ts=ts, pb=pb: e.scalar_tensor_tensor(out=C.xres[:, m, ts], in0=C.xres[:, m, ts], scalar=ALPHA, in1=pb[:], op0=ALU.mult, op1=ALU.add),
                         reads=[("pso", tt % 2), ("xres", m, tt)], writes=[("xres", m, tt)])
            for tt in range(NTT):
                emit_ln(C, l, 0, tt, T)


def build(nseq, layers, in_mode, out_mode, dbg=()):
    nc = bass.Bass("TRN2", target_bir_lowering=False)
    C = Ctx()
    C.nc = nc
    C.W = {k: nc.dram_tensor(k, s, F32, kind="ExternalInput").ap() for k, s in WEIGHT_SHAPES.items()}
    C.consts = {k: nc.dram_tensor("c_" + k, s, F32, kind="ExternalInput").ap() for k, s in CONST_SHAPES.items()}
    if in_mode == "tm":
        xin = nc.dram_tensor("x", [nseq, L, D], F32, kind="ExternalInput").ap()
    else:
        xin = nc.dram_tensor("x", [nseq, D, L], F32, kind="ExternalInput").ap()
    if out_mode == "tm":
        yout = nc.dram_tensor("y", [nseq, L, D], F32, kind="ExternalOutput").ap()
    else:
        yout = nc.dram_tensor("y", [nseq, D, L], F32, kind="ExternalOutput").ap()
    C.dbgout = {}
    if dbg:
        C.dbgout = {"yA": nc.dram_tensor("dbg_yA", [128, 4, L], BF16, kind="ExternalOutput").ap(),
                    "hS": nc.dram_tensor("dbg_hS", [128, 4, L], BF16, kind="ExternalOutput").ap(),
                    "pwr": nc.dram_tensor("dbg_pwr", [128, 32, 22], F32, kind="ExternalOutput").ap(),
                    "pwi": nc.dram_tensor("dbg_pwi", [128, 32, 22], F32, kind="ExternalOutput").ap(),
                    "BTp": nc.dram_tensor("dbg_BTp", [128, 4, 2, 2, 2, 128], BF16, kind="ExternalOutput").ap(),
                    "CTp": nc.dram_tensor("dbg_CTp", [128, 16, 2, 2, 64], BF16, kind="ExternalOutput").ap(),
                    "Xbr": nc.dram_tensor("dbg_Xbr", [128, L], BF16, kind="ExternalOutput").ap(),
                    "Xbi": nc.dram_tensor("dbg_Xbi", [128, L], BF16, kind="ExternalOutput").ap()}
    P = Prog(nc)
    C.P = P
    C.xres = P.sb("xres", [128, 8, L])
    C.xbf = P.sb("xbf", [128, 8, L], BF16)
    emit_const_setup(C)
    P.barrier()
    P.flush()
    for s in range(nseq):
        (load_x_tm if in_mode == "tm" else load_x_fm)(C, xin[s])
        import os
        STG = os.environ.get("KSTAGE", "all")
        for l in layers:
            if STG in ("all", "mixer"):
                if l % 2 == 0:
                    emit_even(C, l)
                else:
                    emit_mlstm(C, l)
            if STG in ("all", "ffn"):
                emit_ffn(C, l)
        (store_x_tm if out_mode == "tm" else store_x_fm)(C, yout[s])
    P.finish()
    return nc


_CACHE = {}


def run_layers(xs, weights, layers, in_mode, out_mode, nseq):
    key = (tuple(layers), in_mode, out_mode, nseq)
    if key not in _CACHE:
        _CACHE[key] = build(nseq, layers, in_mode, out_mode)
    nc = _CACHE[key]
    consts = host_consts()
    in_maps = []
    for c in range(8):
        m = {k: np.ascontiguousarray(v, dtype=np.float32) for k, v in weights.items()}
        for k, v in consts.items():
            m["c_" + k] = v
        m["x"] = np.ascontiguousarray(xs[c])
        in_maps.append(m)
    res = run_bass_kernel_spmd(nc, in_maps, core_ids=list(range(8)))
    return [r["y"] for r in res.results]


def kernel(**inputs):
    x = np.asarray(inputs["x"], dtype=np.float32)
    weights = {k: np.asarray(v, dtype=np.float32) for k, v in inputs.items() if k != "x"}
    xs = [x[c * 4:(c + 1) * 4] for c in range(8)]
    ys = run_layers(xs, weights, [0, 1, 2, 3], "tm", "tm", 4)
    return np.concatenate(ys, axis=0).astype(np.float32)
```

```python
import math
from contextlib import ExitStack, contextmanager
import numpy as np
import ml_dtypes
import concourse.bass as bass
import concourse.mybir as mybir
from concourse.bass_utils import run_bass_kernel_spmd

F32 = mybir.dt.float32
BF16 = mybir.dt.bfloat16
AF = mybir.ActivationFunctionType
ALU = mybir.AluOpType

import os as _os
ENGS = ("pe", "act", "dve", "pool", "sp")
SAME_ENGINE_NOSYNC = tuple(_os.environ.get("KNOSYNC", "pe").split(","))
DQ = ("sp", "act", "pool")

D = 1024
L = 2048
NB = 16
NTT = 4
DEPTH = 4
DFF = 2816
NJ = 22
ALPHA = (2 * DEPTH) ** 0.25
EPS = 1e-5
NEG = -30000.0


class Prog:
    NSLOT = 8

    def __init__(self, nc):
        self.nc = nc
        self.es = ExitStack()
        self.ops = {e: [] for e in ENGS}
        self.sem = {"s_" + e: self.es.enter_context(nc.semaphore("s_" + e)) for e in ENGS}
        self.cnt = {e: 0 for e in ENGS}
        self.dcnt = {}
        for q in DQ:
            for i in range(self.NSLOT):
                n = "d_%s%d" % (q, i)
                self.sem[n] = self.es.enter_context(nc.semaphore(n))
                self.dcnt[n] = 0
        self.dnext = {q: 0 for q in DQ}
        self.seen = {e: {} for e in ENGS}
        self.lastw = {}
        self.readers = {}
        self.stack = [self.es]
        self.uid = 0

    def sb(self, name, shape, dt=F32):
        self.uid += 1
        return self.stack[-1].enter_context(self.nc.sbuf_tensor("%s_%d" % (name, self.uid), list(shape), dt))

    def ps(self, name, shape, dt=F32):
        self.uid += 1
        return self.stack[-1].enter_context(self.nc.psum_tensor("%s_%d" % (name, self.uid), list(shape), dt))

    @contextmanager
    def phase(self):
        es = ExitStack()
        self.stack.append(es)
        try:
            yield
        finally:
            self.barrier()
            self.flush()
            self.stack.pop()
            es.close()

    def _need(self, eng, tok, waits):
        if tok is None:
            return
        sname, val, teng = tok
        if teng == eng and eng in SAME_ENGINE_NOSYNC:
            return
        if self.seen[eng].get(sname, 0) >= val:
            return
        self.seen[eng][sname] = val
        waits.append((sname, val))

    def _deps(self, eng, reads, writes):
        waits = []
        for k in reads:
            self._need(eng, self.lastw.get(k), waits)
        for k in writes:
            self._need(eng, self.lastw.get(k), waits)
            for t in self.readers.get(k, ()):
                self._need(eng, t, waits)
        return waits

    def _commit(self, tok, reads, writes):
        for k in reads:
            self.readers.setdefault(k, []).append(tok)
        for k in writes:
            self.lastw[k] = tok
            self.readers[k] = []

    @staticmethod
    def _isps(k):
        k0 = k[0] if isinstance(k, tuple) else k
        return isinstance(k0, str) and k0.startswith("ps")

    def op(self, eng, fn, reads=(), writes=()):
        psr = [k for k in reads if self._isps(k)]
        if psr:
            reads = [k for k in reads if not self._isps(k)]
            writes = list(writes) + psr
        waits = self._deps(eng, reads, writes)
        self.cnt[eng] += 1
        tok = ("s_" + eng, self.cnt[eng], eng)
        self.ops[eng].append((waits, fn, ("s_" + eng, 1)))
        self._commit(tok, reads, writes)
        return tok

    def dma(self, q, out, in_, reads=(), writes=(), **kw):
        waits = self._deps(q, reads, writes)
        s = self.dnext[q]
        self.dnext[q] = (s + 1) % self.NSLOT
        sname = "d_%s%d" % (q, s)
        prev = self.dcnt[sname]
        if prev > 0 and self.seen[q].get(sname, 0) < prev:
            self.seen[q][sname] = prev
            waits.append((sname, prev))
        self.dcnt[sname] = prev + 16
        tok = (sname, prev + 16, "dma_" + q)
        self.ops[q].append((waits, lambda e: e.dma_start(out=out, in_=in_, **kw), (sname, 16)))
        self._commit(tok, reads, writes)
        return tok

    def barrier(self):
        for e in ENGS:
            waits = []
            for n, v in self.dcnt.items():
                if v > 0 and self.seen[e].get(n, 0) < v:
                    self.seen[e][n] = v
                    waits.append((n, v))
            for e2 in ENGS:
                n = "s_" + e2
                v = self.cnt[e2]
                if e2 != e and v > 0 and self.seen[e].get(n, 0) < v:
                    self.seen[e][n] = v
                    waits.append((n, v))
            if waits:
                self.ops[e].append((waits, None, None))
        self.lastw = {}
        self.readers = {}

    def flush(self):
        nc = self.nc
        handles = {"pe": "tensor", "act": "scalar", "dve": "vector", "pool": "gpsimd", "sp": "sync"}
        if not any(self.ops[e] for e in ENGS):
            return
        with nc.Block() as block:
            for e in ENGS:
                lst = self.ops[e]
                if not lst:
                    continue

                def body(eng, lst=lst):
                    for waits, fn, inc in lst:
                        for sname, val in waits:
                            eng.wait_ge(self.sem[sname], val)
                        if fn is not None:
                            fn(eng).then_inc(self.sem[inc[0]], inc[1])
                getattr(block, handles[e])(body)
        self.ops = {e: [] for e in ENGS}

    def finish(self):
        self.barrier()
        self.flush()
        self.es.close()


def host_consts():
    c = {}
    c["ident"] = np.eye(128, dtype=np.float32)
    s = np.arange(128)[:, None]
    t = np.arange(512)[None, :]
    mf = np.stack([(t >= jj * 128 + s) for jj in range(4)], 1).astype(np.float32)
    mb = np.stack([(t <= jj * 128 + s) for jj in range(4)], 1).astype(np.float32)
    s1 = np.arange(128)[:, None]
    t1 = np.arange(128)[None, :]
    c["tri"] = np.stack([(t1 >= s1), (t1 <= s1)], 1).astype(np.float32)
    slopes = 2.0 ** (-8.0 * np.arange(1, 9) / 8)
    kk = np.arange(128)[:, None]
    qq = np.arange(128)[None, :]
    ab = np.zeros((128, 8, 3, 128), np.float32)
    for h in range(8):
        for r in range(3):
            dist = np.abs(qq - (kk + (r - 1) * 128))
            ab[:, h, r, :] = np.where(dist <= 128, -slopes[h] * dist * 8.0, NEG)
    c["abias"] = ab
    sel = np.zeros((40, 8, 128), np.float32)
    for h in range(8):
        sel[h, h, :] = 1.0
        sel[32 + h, h, :] = 1.0
    c["sel"] = sel
    return c


CONST_SHAPES = {"ident": [128, 128], "tri": [128, 2, 128],
                "abias": [128, 8, 3, 128], "sel": [40, 8, 128]}

WEIGHT_SHAPES = {
    "ev_w_in": [2, 1024, 1280], "ev_sink": [2, 8], "s5_a_re": [2, 2, 32, 64], "s5_a_im": [2, 2, 32, 64],
    "s5_log_dt": [2, 2, 32], "s5_b_re": [2, 2, 32, 64, 16], "s5_b_im": [2, 2, 32, 64, 16],
    "s5_c_re": [2, 2, 32, 16, 64], "s5_c_im": [2, 2, 32, 16, 64], "s5_d": [2, 512],
    "s5_w_glu": [2, 512, 512], "ev_w_out": [2, 1024, 1024], "od_w_in": [2, 1024, 4128],
    "od_gate_b": [2, 32], "od_conv_w": [2, 3, 2048], "od_conv_b": [2, 2048], "od_norm_g": [2, 1024],
    "od_w_out": [2, 1024, 1024], "ffn_w_up": [4, 1024, 5632], "ffn_conv_w": [4, 3, 2816],
    "ffn_conv_b": [4, 2816], "ffn_w_down": [4, 2816, 1024], "ln_g": [4, 2, 1024], "ln_b": [4, 2, 1024],
}


class Ctx:
    pass


def kc_view(w2d):
    return w2d.rearrange("(kc k) m -> k kc m", k=128)


def emit_const_setup(C):
    P = C.P
    W = C.W
    C.ident = P.sb("ident", [128, 128])
    C.identb = P.sb("identb", [128, 128], BF16)
    C.onesf = P.sb("onesf", [128, 128])
    C.ones1k = P.sb("ones1k", [128, 128])
    C.onesb = P.sb("onesb", [128, 128], BF16)
    C.sel = P.sb("sel", [40, 8, 128])
    C.one8 = P.sb("one8", [128, 1])
    P.dma("sp", C.ident[:], C.consts["ident"], writes=["ident"])
    P.dma("sp", C.sel[:], C.consts["sel"], writes=["sel"])
    P.op("dve", lambda e: e.tensor_copy(out=C.identb[:], in_=C.ident[:]), reads=["ident"], writes=["identb"])
    P.op("pool", lambda e: e.memset(C.onesf[:], 1.0 / 128), writes=["onesf"])
    P.op("pool", lambda e: e.memset(C.ones1k[:], 1.0 / 1024), writes=["ones1k"])
    P.op("pool", lambda e: e.memset(C.onesb[:], 1.0), writes=["onesb"])
    P.op("pool", lambda e: e.memset(C.one8[:], 1.0), writes=["one8"])
    C.epsc = P.sb("epsc", [128, 1])
    P.op("pool", lambda e: e.memset(C.epsc[:], EPS), writes=["epsc"])
    C.lng = P.sb("lng", [128, 4, 2, 8])
    C.lnb = P.sb("lnb", [128, 4, 2, 8])
    import os
    for l in range(4 if os.environ.get("KNOLN") is None else 0):
        for j in range(2):
            P.dma("sp", C.lng[:, l, j, :], W["ln_g"][l, j].rearrange("(c p) -> p c", p=128), writes=["lng"],
                  allow_slow_non_contiguous=True)
            P.dma("act", C.lnb[:, l, j, :], W["ln_b"][l, j].rearrange("(c p) -> p c", p=128), writes=["lnb"],
                  allow_slow_non_contiguous=True)


def load_x_tm(C, x_seq):
    P = C.P
    with P.phase():
        stg = [P.sb("xstg", [128, 1024]) for _ in range(2)]
        pt = [P.ps("xps", [128, 4, 128]) for _ in range(2)]
        for nb in range(NB):
            s = stg[nb % 2]
            P.dma("sp" if nb % 2 == 0 else "act", s[:], x_seq[nb * 128:(nb + 1) * 128, :], writes=[("xstg", nb % 2)])
            for half in range(2):
                pp = pt[half]
                for i in range(4):
                    c = half * 4 + i
                    P.op("pe", lambda e, pp=pp, i=i, c=c, s=s: e.transpose(pp[:, i, :], s[:, c * 128:(c + 1) * 128], C.ident[:]),
                         reads=[("xstg", nb % 2), "ident"], writes=[("psxp", half)])
                P.op("act", lambda e, pp=pp, half=half, nb=nb: e.activation(
                    out=C.xres[:, half * 4:half * 4 + 4, nb * 128:(nb + 1) * 128], in_=pp[:], func=AF.Copy),
                    reads=[("psxp", half)], writes=[("xres", nb, half)])
                P.op("dve", lambda e, pp=pp, half=half, nb=nb: e.tensor_copy(
                    out=C.xbf[:, half * 4:half * 4 + 4, nb * 128:(nb + 1) * 128], in_=pp[:]),
                    reads=[("psxp", half)], writes=[("xbf", nb, half)])


def load_x_fm(C, xT_seq):
    P = C.P
    with P.phase():
        for c in range(8):
            P.dma("sp" if c % 2 == 0 else "act", C.xres[:, c, :], xT_seq[c * 128:(c + 1) * 128, :], writes=[("xres", c)])
            P.op("dve" if c % 2 == 0 else "pool", lambda e, c=c: e.tensor_copy(out=C.xbf[:, c, :], in_=C.xres[:, c, :]),
                 reads=[("xres", c)], writes=[("xbf", c)])


def store_x_fm(C, xT_seq):
    P = C.P
    with P.phase():
        for c in range(8):
            P.dma("sp" if c % 2 == 0 else "act", xT_seq[c * 128:(c + 1) * 128, :], C.xres[:, c, :], reads=[("xres", c)])


def store_x_tm(C, y_seq):
    P = C.P
    with P.phase():
        stg = [P.sb("ystg", [128, 1024]) for _ in range(2)]
        pt = [P.ps("yps", [128, 4, 128]) for _ in range(2)]
        for nb in range(NB):
            s = stg[nb % 2]
            for half in range(2):
                pp = pt[half]
                for i in range(4):
                    c = half * 4 + i
                    P.op("pe", lambda e, pp=pp, i=i, c=c, nb=nb: e.transpose(pp[:, i, :], C.xres[:, c, nb * 128:(nb + 1) * 128], C.ident[:]),
                         reads=["ident"], writes=[("psyp", half)])
                eng = "act" if half == 0 else "dve"
                if eng == "act":
                    P.op("act", lambda e, pp=pp, s=s, half=half: e.activation(out=s[:, half * 512:(half + 1) * 512], in_=pp[:].rearrange("p a b -> p (a b)"), func=AF.Copy),
                         reads=[("psyp", half)], writes=[("ystg", nb % 2, half)])
                else:
                    P.op("dve", lambda e, pp=pp, s=s, half=half: e.tensor_copy(out=s[:, half * 512:(half + 1) * 512], in_=pp[:].rearrange("p a b -> p (a b)")),
                         reads=[("psyp", half)], writes=[("ystg", nb % 2, half)])
            P.dma("sp" if nb % 2 == 0 else "act", y_seq[nb * 128:(nb + 1) * 128, :], s[:],
                  reads=[("ystg", nb % 2, 0), ("ystg", nb % 2, 1)])


def emit_ln(C, l, j, tt, T):
    P = C.P
    ts = slice(tt * 512, (tt + 1) * 512)
    rk = [("xres", c, tt) for c in range(8)]
    for c in range(8):
        P.op("pe", lambda e, c=c: e.matmul(T["psm"][:], lhsT=C.ones1k[:], rhs=C.xres[:, c, ts], start=(c == 0), stop=(c == 7)),
             reads=[rk[c], "ones1k"], writes=["psm"])
    for c in range(8):
        sq = T["sq"][c % 2]
        P.op("act", lambda e, c=c, sq=sq: e.activation(out=sq[:], in_=C.xres[:, c, ts], func=AF.Square),
             reads=[rk[c]], writes=[("sq", c % 2)])
        P.op("pe", lambda e, c=c, sq=sq: e.matmul(T["psv"][:], lhsT=C.ones1k[:], rhs=sq[:], start=(c == 0), stop=(c == 7)),
             reads=[("sq", c % 2), "ones1k"], writes=["psv"])
    mean, rstd = T["mean"], T["rstd"]
    P.op("act", lambda e: e.activation(out=mean[:], in_=T["psm"][:], func=AF.Copy), reads=["psm"], writes=["mean"])
    P.op("dve", lambda e: e.tensor_tensor(out=rstd[:], in0=mean[:], in1=mean[:], op=ALU.mult), reads=["mean"], writes=["rstd"])
    P.op("dve", lambda e: e.tensor_tensor(out=rstd[:], in0=T["psv"][:], in1=rstd[:], op=ALU.subtract), reads=["psv", "rstd"], writes=["rstd"])
    P.op("act", lambda e: e.activation(out=rstd[:], in_=rstd[:], func=AF.Sqrt, bias=C.epsc[:, 0:1], scale=1.0), reads=["rstd", "epsc"], writes=["rstd"])
    P.op("dve", lambda e: e.reciprocal(out=rstd[:], in_=rstd[:]), reads=["rstd"], writes=["rstd"])
    for c in range(8):
        tmp = T["tmp"][c % 2]
        P.op("pool", lambda e, c=c, tmp=tmp: e.tensor_tensor(out=tmp[:], in0=C.xres[:, c, ts], in1=mean[:], op=ALU.subtract),
             reads=[rk[c], "mean"], writes=[("lntmp", c % 2)])
        P.op("dve", lambda e, c=c, tmp=tmp: e.tensor_tensor(out=tmp[:], in0=tmp[:], in1=rstd[:], op=ALU.mult),
             reads=[("lntmp", c % 2), "rstd"], writes=[("lntmp", c % 2)])
        P.op("act", lambda e, c=c, tmp=tmp: e.activation(out=C.xres[:, c, ts], in_=tmp[:], func=AF.Identity,
                                                         bias=C.lnb[:, l, j, c:c + 1], scale=C.lng[:, l, j, c:c + 1]),
             reads=[("lntmp", c % 2), "lng", "lnb"], writes=[rk[c]])
        P.op("pool", lambda e, c=c: e.tensor_copy(out=C.xbf[:, c, ts], in_=C.xres[:, c, ts]),
             reads=[rk[c]], writes=[("xbf", c, tt)])


def ln_temps(P):
    return {"sq": [P.sb("lnsq", [128, 512]) for _ in range(2)], "tmp": [P.sb("lntmp", [128, 512]) for _ in range(2)],
            "mean": P.sb("lnmean", [128, 512]), "rstd": P.sb("lnrstd", [128, 512]),
            "psm": P.ps("psm", [128, 512]), "psv": P.ps("psv", [128, 512])}


def emit_conv3(C, P, asb, out, wv, bv, ncol, key_in, key_out):
    P.op("dve", lambda e: e.tensor_scalar(out=out, in0=asb[:, 1:ncol + 1], scalar1=wv[:, 1:2], scalar2=bv, op0=ALU.mult, op1=ALU.add),
         reads=[key_in], writes=[key_out])
    P.op("dve", lambda e: e.scalar_tensor_tensor(out=out, in0=asb[:, 0:ncol], scalar=wv[:, 0:1], in1=out, op0=ALU.mult, op1=ALU.add),
         reads=[key_in, key_out], writes=[key_out])
    P.op("dve", lambda e: e.scalar_tensor_tensor(out=out, in0=asb[:, 2:ncol + 2], scalar=wv[:, 2:3], in1=out, op0=ALU.mult, op1=ALU.add),
         reads=[key_in, key_out], writes=[key_out])


def emit_ffn(C, l):
    P = C.P
    W = C.W
    wup = kc_view(W["ffn_w_up"][l])
    wdn = W["ffn_w_down"][l].rearrange("(j p) m -> p j m", p=128)
    HL = 1024
    with P.phase():
        cw = P.sb("fcw", [128, NJ, 3])
        cb = P.sb("fcb", [128, NJ])
        for k in range(3):
            P.dma("sp", cw[:, :, k], W["ffn_conv_w"][l, k].rearrange("(j p) -> p j", p=128), writes=["fcw"], allow_slow_non_contiguous=True)
        P.dma("act", cb[:], W["ffn_conv_b"][l].rearrange("(j p) -> p j", p=128), writes=["fcw"], allow_slow_non_contiguous=True)
        hT = P.sb("hT", [128, NJ, HL], BF16)
        wa = [P.sb("wa", [128, 8, 128], BF16) for _ in range(2)]
        wg = [P.sb("wg", [128, 8, 128], BF16) for _ in range(2)]
        wd = [P.sb("wd", [128, NJ, 128], BF16) for _ in range(2)]
        asb = [P.sb("asb", [128, HL + 2]) for _ in range(2)]
        cv = [P.sb("cv", [128, HL]) for _ in range(2)]
        psa = [P.ps("psa", [128, 512]) for _ in range(2)]
        psg = [P.ps("psg", [128, 512]) for _ in range(2)]
        psh = P.ps("psh", [128, 512])
        psd = [P.ps("psd", [128, 512]) for _ in range(1)]
        T = {"sq": [cv[0][:, 0:512], cv[0][:, 512:1024]], "tmp": [cv[1][:, 0:512], cv[1][:, 512:1024]],
             "mean": asb[0][:, 0:512], "rstd": asb[0][:, 512:1024], "psm": psa[0], "psv": psa[1]}
        T = {k: (v if isinstance(v, list) else v) for k, v in T.items()}
        it = 0
        for half in range(2):
            t0 = half * HL
            for j in range(NJ):
                b = it % 2
                it += 1
                P.dma("pool", wa[b][:], wup[:, :, j * 128:(j + 1) * 128], writes=[("wa", b)])
                P.dma("pool", wg[b][:], wup[:, :, DFF + j * 128:DFF + (j + 1) * 128], writes=[("wg", b)])
                A = asb[b]
                hcols = []
                if half == 0:
                    P.op("pool", lambda e, A=A: e.memset(A[:, 0:1], 0.0), writes=[("asbh0", b)])
                    hcols.append((HL + 1, t0 + HL))
                else:
                    P.op("pool", lambda e, A=A: e.memset(A[:, HL + 1:HL + 2], 0.0), writes=[("asbh1", b)])
                    hcols.append((0, t0 - 1))
                for (dst, tok) in hcols:
                    for kc in range(8):
                        P.op("pe", lambda e, kc=kc, tok=tok, b=b: e.matmul(psh[:, 0:1], lhsT=wa[b][:, kc, :], rhs=C.xbf[:, kc, tok:tok + 1], start=(kc == 0), stop=(kc == 7)),
                             reads=[("wa", b), ("xbf", kc, tok // 512)], writes=["psh"])
                    P.op("act", lambda e, A=A, dst=dst: e.activation(out=A[:, dst:dst + 1], in_=psh[:, 0:1], func=AF.Copy),
                         reads=["psh"], writes=[("asbh%d" % (1 if dst > 0 else 0), b)])
                for q in range(2):
                    tt = half * 2 + q
                    for kc in range(8):
                        P.op("pe", lambda e, kc=kc, tt=tt, b=b, q=q: e.matmul(psa[q][:], lhsT=wa[b][:, kc, :], rhs=C.xbf[:, kc, tt * 512:(tt + 1) * 512], start=(kc == 0), stop=(kc == 7)),
                             reads=[("wa", b), ("xbf", kc, tt)], writes=[("psa", q)])
                    P.op("act", lambda e, A=A, q=q: e.activation(out=A[:, 1 + q * 512:1 + (q + 1) * 512], in_=psa[q][:], func=AF.Copy),
                         reads=[("psa", q)], writes=[("asb", b, q)])
                    for kc in range(8):
                        P.op("pe", lambda e, kc=kc, tt=tt, b=b, q=q: e.matmul(psg[q][:], lhsT=wg[b][:, kc, :], rhs=C.xbf[:, kc, tt * 512:(tt + 1) * 512], start=(kc == 0), stop=(kc == 7)),
                             reads=[("wg", b), ("xbf", kc, tt)], writes=[("psg", q)])
                kin = ("asball", b)
                P.op("dve", lambda e, b=b, j=j: e.tensor_scalar(out=cv[b][:], in0=asb[b][:, 1:HL + 1], scalar1=cw[:, j, 1:2], scalar2=cb[:, j:j + 1], op0=ALU.mult, op1=ALU.add),
                     reads=[("asb", b, 0), ("asb", b, 1), "fcw"], writes=[("cv", b)])
                P.op("dve", lambda e, b=b, j=j: e.scalar_tensor_tensor(out=cv[b][:], in0=asb[b][:, 0:HL], scalar=cw[:, j, 0:1], in1=cv[b][:], op0=ALU.mult, op1=ALU.add),
                     reads=[("asb", b, 0), ("asb", b, 1), ("asbh0", b), ("cv", b)], writes=[("cv", b)])
                P.op("dve", lambda e, b=b, j=j: e.scalar_tensor_tensor(out=cv[b][:], in0=asb[b][:, 2:HL + 2], scalar=cw[:, j, 2:3], in1=cv[b][:], op0=ALU.mult, op1=ALU.add),
                     reads=[("asb", b, 0), ("asb", b, 1), ("asbh1", b), ("cv", b)], writes=[("cv", b)])
                P.op("act", lambda e, b=b: e.activation(out=cv[b][:], in_=cv[b][:], func=AF.Gelu_apprx_tanh),
                     reads=[("cv", b)], writes=[("cv", b)])
                for q in range(2):
                    P.op("dve", lambda e, b=b, q=q, j=j: e.tensor_tensor(out=hT[:, j, q * 512:(q + 1) * 512], in0=cv[b][:, q * 512:(q + 1) * 512], in1=psg[q][:], op=ALU.mult),
                         reads=[("cv", b), ("psg", q)], writes=[("hT", j)])
            for m in range(8):
                b = m % 2
                P.dma("pool", wd[b][:], wdn[:, :, m * 128:(m + 1) * 128], writes=[("wd", b)])
                for q in range(2):
                    tt = half * 2 + q
                    for j in range(NJ):
                        P.op("pe", lambda e, j=j, b=b, q=q: e.matmul(psd[0][:], lhsT=wd[b][:, j, :], rhs=hT[:, j, q * 512:(q + 1) * 512], start=(j == 0), stop=(j == NJ - 1)),
                             reads=[("wd", b), ("hT", j)], writes=["psd"])
                    P.op("dve", lambda e, m=m, tt=tt: e.scalar_tensor_tensor(out=C.xres[:, m, tt * 512:(tt + 1) * 512], in0=C.xres[:, m, tt * 512:(tt + 1) * 512], scalar=ALPHA, in1=psd[0][:], op0=ALU.mult, op1=ALU.add),
                         reads=["psd", ("xres", m, tt)], writes=[("xres", m, tt)])
        P.barrier()
        for tt in range(NTT):
            emit_ln(C, l, 1, tt, T)


def emit_mixer_out(C, l, yT, wout_dram, nk, T):
    P = C.P
    wv = wout_dram.rearrange("(kc k) m -> k kc m", k=128)
    wo = [P.sb("wo", [128, nk, 128], BF16) for _ in range(2)]
    pso = [P.ps("pso", [128, 512]) for _ in range(2)]
    for m in range(8):
        b = m % 2
        P.dma("pool", wo[b][:], wv[:, :, m * 128:(m + 1) * 128], writes=[("wo", b)])
        for tt in range(NTT):
            pb = pso[tt % 2]
            for k in range(nk):
                P.op("pe", lambda e, k=k, b=b, tt=tt, pb=pb: e.matmul(pb[:], lhsT=wo[b][:, k, :], rhs=yT[:, k, tt * 512:(tt + 1) * 512], start=(k == 0), stop=(k == nk - 1)),
                     reads=[("wo", b), ("yT", k)], writes=[("pso", tt % 2)])
            P.op("dve", lambda e, m=m, tt=tt, pb=pb: e.scalar_tensor_tensor(out=C.xres[:, m, tt * 512:(tt + 1) * 512], in0=C.xres[:, m, tt * 512:(tt + 1) * 512], scalar=ALPHA, in1=pb[:], op0=ALU.mult, op1=ALU.add),
                 reads=[("pso", tt % 2), ("xres", m, tt)], writes=[("xres", m, tt)])
    for tt in range(NTT):
        emit_ln(C, l, 0, tt, T)


def emit_mlstm(C, l):
    P = C.P
    W = C.W
    j = l // 2
    win = kc_view(W["od_w_in"][j])
    SCALE = 128.0 ** -0.5
    RB = (0, 32)
    with P.phase():
        hTf = P.sb("hTfin", [128, 8, L], BF16)
        with P.phase():
            gb = P.sb("gb", [40, 2])
            for d in range(2):
                P.dma("sp", gb[RB[d]:RB[d] + 8, :], W["od_gate_b"][j][16 * d:16 * d + 16].rearrange("(q h) -> h q", h=8), writes=["gb"], allow_slow_non_contiguous=True)
            cwq = P.sb("mcw", [128, 16, 3])
            cbq = P.sb("mcb", [128, 16])
            for k in range(3):
                P.dma("sp", cwq[:, :, k], W["od_conv_w"][j, k].rearrange("(c p) -> p c", p=128), writes=["mcw"], allow_slow_non_contiguous=True)
            P.dma("act", cbq[:], W["od_conv_b"][j].rearrange("(c p) -> p c", p=128), writes=["mcw"], allow_slow_non_contiguous=True)
            ng = P.sb("ng", [128, 8])
            P.dma("sp", ng[:], W["od_norm_g"][j].rearrange("(c p) -> p c", p=128), writes=["ng"], allow_slow_non_contiguous=True)
            tri = P.sb("tri", [128, 2, 128], BF16)
            P.dma("pool", tri[:], C.consts["tri"], writes=["tri"])
            nmrel = P.sb("gnm", [40, L])
            negm = P.sb("gm", [40, L])
            colT = P.sb("colT", [128, 2, NB, 8])
            with P.phase():
                gw = P.sb("gw", [128, 8, 32], BF16)
                P.dma("pool", gw[:], win[:, :, 4096:4128], writes=["gw"])
                ci = P.sb("gci", [40, L])
                t1 = P.sb("gt1", [40, L])
                t2 = P.sb("gt2", [40, L])
                tot = P.sb("gtot", [40, 1])
                psg = [P.ps("psgate", [128, 512]) for _ in range(2)]
                pct = P.ps("pspct", [128, 2, NB, 16])
                n = 0
                for q in range(4):
                    d = q // 2
                    dst = ci if q % 2 == 0 else nmrel
                    for tt in range(NTT):
                        pb = psg[n % 2]
                        for kc in range(8):
                            P.op("pe", lambda e, kc=kc, q=q, tt=tt, pb=pb, d=d: e.matmul(pb[RB[d]:RB[d] + 8, :], lhsT=gw[:, kc, q * 8:(q + 1) * 8], rhs=C.xbf[:, kc, tt * 512:(tt + 1) * 512], start=(kc == 0), stop=(kc == 7)),
                                 reads=["gw", ("xbf", kc, tt)], writes=[("psgate", n % 2)])
                        P.op("act", lambda e, q=q, tt=tt, pb=pb, d=d, dst=dst: e.activation(out=dst[RB[d]:RB[d] + 8, tt * 512:(tt + 1) * 512], in_=pb[RB[d]:RB[d] + 8, :], func=AF.Identity, bias=gb[RB[d]:RB[d] + 8, (q % 2):(q % 2) + 1], scale=1.0),
                             reads=[("psgate", n % 2), "gb"], writes=[("gpre", q)])
                        n += 1
                for d in range(2):
                    r = slice(RB[d], RB[d] + 8)
                    kg, kf = ("gpre", 2 * d), ("gpre", 2 * d + 1)
                    K = lambda s, d=d: (s, d)
                    P.op("act", lambda e, r=r: e.activation(out=t1[r, :], in_=nmrel[r, :], func=AF.Exp, scale=-1.0), reads=[kf], writes=[K("t1")])
                    P.op("act", lambda e, r=r: e.activation(out=nmrel[r, :], in_=t1[r, :], func=AF.Ln, bias=1.0, scale=1.0), reads=[K("t1")], writes=[K("lf")])
                    P.op("dve", lambda e, r=r: e.tensor_tensor_scan(out=negm[r, :], data0=C.one8[r, 0:1].to_broadcast([8, L]), data1=nmrel[r, :], initial=0.0, op0=ALU.mult, op1=ALU.add),
                         reads=[K("lf"), "one8"], writes=[K("G")])
                    if d == 1:
                        P.op("dve", lambda e, r=r: e.tensor_copy(out=tot[r, :], in_=negm[r, L - 1:L]), reads=[K("G")], writes=[K("tot")])
                        P.op("dve", lambda e, r=r: e.tensor_tensor(out=t1[r, :], in0=nmrel[r, :], in1=negm[r, :], op=ALU.subtract), reads=[K("lf"), K("G"), K("t1")], writes=[K("t1")])
                        P.op("dve", lambda e, r=r: e.tensor_scalar(out=negm[r, :], in0=t1[r, :], scalar1=tot[r, 0:1], scalar2=None, op0=ALU.add), reads=[K("t1"), K("tot")], writes=[K("G")])
                    P.op("dve", lambda e, r=r: e.tensor_tensor(out=ci[r, :], in0=ci[r, :], in1=negm[r, :], op=ALU.add), reads=[kg, K("G")], writes=[K("c")])
                    if d == 0:
                        P.op("dve", lambda e, r=r: e.tensor_tensor_scan(out=t1[r, :], data0=C.one8[r, 0:1].to_broadcast([8, L]), data1=ci[r, :], initial=-1e30, op0=ALU.mult, op1=ALU.max),
                             reads=[K("c"), "one8", K("t1")], writes=[K("t1")])
                        cm, kcm = t1, K("t1")
                    else:
                        P.op("dve", lambda e, r=r: e.tensor_copy(out=t1[r, :], in_=ci[r, :]), reads=[K("c"), K("t1")], writes=[K("t1")])
                        src, dst, ks, kd = t1, t2, K("t1"), K("t2")
                        s = 1
                        while s < L:
                            P.op("dve", lambda e, src=src, dst=dst, s=s, r=r: e.tensor_tensor(out=dst[r, 0:L - s], in0=src[r, 0:L - s], in1=src[r, s:L], op=ALU.max), reads=[ks], writes=[kd])
                            P.op("dve", lambda e, src=src, dst=dst, s=s, r=r: e.tensor_copy(out=dst[r, L - s:L], in_=src[r, L - s:L]), reads=[ks], writes=[kd])
                            src, dst, ks, kd = dst, src, kd, ks
                            s *= 2
                        cm, kcm = src, ks
                    P.op("dve", lambda e, cm=cm, r=r: e.tensor_scalar(out=nmrel[r, :], in0=cm[r, :], scalar1=-1.0, scalar2=0.0, op0=ALU.mult, op1=ALU.min),
                         reads=[kcm, K("lf")], writes=[("gnm", d)])
                    P.op("dve", lambda e, r=r: e.tensor_tensor(out=negm[r, :], in0=negm[r, :], in1=nmrel[r, :], op=ALU.add), reads=[K("G"), ("gnm", d)], writes=[("gm", d)])
                    for nb in range(NB):
                        P.op("pe", lambda e, d=d, nb=nb, r=r: e.transpose(pct[:, d, nb, 0:8], ci[r, nb * 128:(nb + 1) * 128], C.ident[r, r]),
                             reads=[K("c"), "ident"], writes=["pspct"])
                P.op("dve", lambda e: e.tensor_copy(out=colT[:], in_=pct[:, :, :, 0:8]), reads=["pspct"], writes=["colT"])
            wb = [P.sb("wbuf", [128, 8, 128], BF16) for _ in range(2)]
            qT = P.sb("qT", [128, L], BF16)
            kT = P.sb("kT", [128, L], BF16)
            vt = P.sb("vt", [128, NB, 128], BF16)
            sgo = P.sb("sgo", [128, 512], BF16)
            bufA = P.sb("bufA", [128, L + 2])
            bufB = P.sb("bufB", [128, L])
            Dt = [P.sb("Dt", [128, 512], BF16) for _ in range(3)]
            Pt = [P.sb("Pt", [128, 512], BF16) for _ in range(3)]
            asb, cvt = bufA, bufB
            Rbc = [bufB[:, 0:512], bufB[:, 512:1024]]
            Mex = [bufB[:, 1024:1536], bufB[:, 1536:2048]]
            hsum, e1, e2 = bufA[:, 0:512], bufA[:, 512:1024], bufA[:, 1024:1536]
            pss = [P.ps("pss", [128, 512]) for _ in range(3)]
            psn = P.ps("psn", [128, 512])
            psdn = P.ps("psdn", [128, 512])
            psx = [P.ps("psx", [128, 512]) for _ in range(2)]
            nx = 0
            nw = 0
            for h in range(8):
                P.op("pool", lambda e: e.memset(asb[:, 0:1], 0.0), writes=["masbh"])
                P.op("pool", lambda e: e.memset(asb[:, L + 1:L + 2], 0.0), writes=["masbh"])
                for (col0, dstT, ci_, kd_) in ((h * 128, qT, h, "qT"), (1024 + h * 128, kT, 8 + h, "kT")):
                    wt = wb[nw % 2]
                    wkey = ("wbuf", nw % 2)
                    nw += 1
                    P.dma("pool", wt[:], win[:, :, col0:col0 + 128], writes=[wkey])
                    for tt in range(NTT):
                        pb = psx[nx % 2]
                        for kc in range(8):
                            P.op("pe", lambda e, kc=kc, tt=tt, pb=pb, wt=wt: e.matmul(pb[:], lhsT=wt[:, kc, :], rhs=C.xbf[:, kc, tt * 512:(tt + 1) * 512], start=(kc == 0), stop=(kc == 7)),
                                 reads=[wkey, ("xbf", kc, tt)], writes=[("psx", nx % 2)])
                        P.op("act", lambda e, tt=tt, pb=pb: e.activation(out=asb[:, 1 + tt * 512:1 + (tt + 1) * 512], in_=pb[:], func=AF.Copy),
                             reads=[("psx", nx % 2)], writes=["masb"])
                        nx += 1
                    P.op("dve", lambda e, ci_=ci_: e.tensor_scalar(out=cvt[:], in0=asb[:, 1:L + 1], scalar1=cwq[:, ci_, 1:2], scalar2=cbq[:, ci_:ci_ + 1], op0=ALU.mult, op1=ALU.add),
                         reads=["masb", "mcw"], writes=["mcv"])
                    P.op("dve", lambda e, ci_=ci_: e.scalar_tensor_tensor(out=cvt[:], in0=asb[:, 0:L], scalar=cwq[:, ci_, 0:1], in1=cvt[:], op0=ALU.mult, op1=ALU.add),
                         reads=["masb", "masbh", "mcv"], writes=["mcv"])
                    P.op("dve", lambda e, ci_=ci_: e.scalar_tensor_tensor(out=cvt[:], in0=asb[:, 2:L + 2], scalar=cwq[:, ci_, 2:3], in1=cvt[:], op0=ALU.mult, op1=ALU.add),
                         reads=["masb", "masbh", "mcv"], writes=["mcv"])
                    P.op("act", lambda e, dstT=dstT: e.activation(out=dstT[:], in_=cvt[:], func=AF.Silu), reads=["mcv"], writes=[kd_])
                wt = wb[nw % 2]
                wkey = ("wbuf", nw % 2)
                nw += 1
                P.dma("pool", wt[:], win[:, :, 2048 + h * 128:2048 + (h + 1) * 128], writes=[wkey])
                for nb in range(NB):
                    pb = psx[nx % 2]
                    for kc in range(8):
                        P.op("pe", lambda e, kc=kc, nb=nb, pb=pb, wt=wt: e.matmul(pb[:, 0:128], lhsT=C.xbf[:, kc, nb * 128:(nb + 1) * 128], rhs=wt[:, kc, :], start=(kc == 0), stop=(kc == 7)),
                             reads=[wkey, ("xbf", kc, nb // 4)], writes=[("psx", nx % 2)])
                    if nb % 2 == 0:
                        P.op("act", lambda e, nb=nb, pb=pb: e.activation(out=vt[:, nb, :], in_=pb[:, 0:128], func=AF.Copy), reads=[("psx", nx % 2)], writes=["vt"])
                    else:
                        P.op("dve", lambda e, nb=nb, pb=pb: e.tensor_copy(out=vt[:, nb, :], in_=pb[:, 0:128]), reads=[("psx", nx % 2)], writes=["vt"])
                    nx += 1
                wo = wb[nw % 2]
                wokey = ("wbuf", nw % 2)
                nw += 1
                P.dma("pool", wo[:], win[:, :, 3072 + h * 128:3072 + (h + 1) * 128], writes=[wokey])
                P.barrier()
                it = 0
                for tt in range(NTT):
                    ts = slice(tt * 512, (tt + 1) * 512)
                    for d in range(2):
                        r = slice(RB[d], RB[d] + 8)
                        pb = psx[nx % 2]
                        P.op("pe", lambda e, ts=ts, pb=pb, h=h, r=r: e.matmul(pb[:], lhsT=C.sel[r, h, :], rhs=nmrel[r, ts], start=True, stop=True),
                             reads=["sel", ("gnm", d)], writes=[("psx", nx % 2)])
                        P.op("dve", lambda e, d=d, pb=pb: e.tensor_copy(out=Rbc[d], in_=pb[:]), reads=[("psx", nx % 2)], writes=[("Rbc", d)])
                        nx += 1
                        pb = psx[nx % 2]
                        P.op("pe", lambda e, ts=ts, pb=pb, h=h, r=r: e.matmul(pb[:], lhsT=C.sel[r, h, :], rhs=negm[r, ts], start=True, stop=True),
                             reads=["sel", ("gm", d)], writes=[("psx", nx % 2)])
                        P.op("act", lambda e, d=d, pb=pb: e.activation(out=Mex[d], in_=pb[:], func=AF.Exp), reads=[("psx", nx % 2)], writes=[("Mex", d)])
                        nx += 1
                        if d == 0:
                            jl = list(range(0, 4 * tt + 4))
                        else:
                            jl = list(range(NB - 1, 4 * tt - 1, -1))
                        def front(ji, jb, b):
                            jj = jb - 4 * tt
                            if d == 0:
                                c0, c1 = (max(jj, 0) * 128, 512)
                                dc = c0 if 0 <= jj < 4 else None
                            else:
                                c0, c1 = (0, (min(jj, 3) + 1) * 128)
                                dc = c1 - 128 if 0 <= jj < 4 else None
                            cs = slice(c0, c1)
                            qs = slice(tt * 512 + c0, tt * 512 + c1)
                            P.op("pe", lambda e, jb=jb, b=b, cs=cs, qs=qs: e.matmul(pss[b][:, cs], lhsT=kT[:, jb * 128:(jb + 1) * 128], rhs=qT[:, qs], start=True, stop=True),
                                 reads=["qT", "kT"], writes=[("pss", b)])
                            P.op("act", lambda e, jb=jb, b=b, d=d, h=h, cs=cs: e.activation(out=Dt[b][:, cs], in_=Rbc[d][:, cs], func=AF.Exp, bias=colT[:, d, jb, h:h + 1], scale=1.0),
                                 reads=[("Rbc", d), "colT"], writes=[("Dt", b)])
                            if dc is not None:
                                P.op("pool", lambda e, b=b, d=d, dc=dc: e.tensor_tensor(out=Dt[b][:, dc:dc + 128], in0=Dt[b][:, dc:dc + 128], in1=tri[:, d, :], op=ALU.mult),
                                     reads=[("Dt", b), "tri"], writes=[("Dt", b)])
                            P.op("dve", lambda e, b=b, cs=cs: e.scalar_tensor_tensor(out=Pt[b][:, cs], in0=pss[b][:, cs], scalar=SCALE, in1=Dt[b][:, cs], op0=ALU.mult, op1=ALU.mult),
                                 reads=[("pss", b), ("Dt", b)], writes=[("Pt", b)])
                            return cs

                        def back(ji, jb, b, cs):
                            P.op("pe", lambda e, jb=jb, b=b, ji=ji, jl=jl, cs=cs: e.matmul(psn[:, cs], lhsT=vt[:, jb, :], rhs=Pt[b][:, cs], start=(ji == 0), stop=(ji == len(jl) - 1)),
                                 reads=["vt", ("Pt", b)], writes=["psn"])
                            P.op("pe", lambda e, jb=jb, b=b, ji=ji, jl=jl, cs=cs: e.matmul(psdn[:, cs], lhsT=C.onesb[:], rhs=Pt[b][:, cs], start=(ji == 0), stop=(ji == len(jl) - 1)),
                                 reads=["onesb", ("Pt", b)], writes=["psdn"])

                        bufs = [(it + ji) % 3 for ji in range(len(jl))]
                        it += len(jl)
                        csl = {0: front(0, jl[0], bufs[0])}
                        if len(jl) > 1:
                            csl[1] = front(1, jl[1], bufs[1])
                        for ji, jb in enumerate(jl):
                            if ji + 2 < len(jl):
                                csl[ji + 2] = front(ji + 2, jl[ji + 2], bufs[ji + 2])
                            back(ji, jb, bufs[ji], csl[ji])
                        P.op("act", lambda e: e.activation(out=e1, in_=psdn[:], func=AF.Abs), reads=["psdn"], writes=["e1"])
                        P.op("dve", lambda e, d=d: e.tensor_tensor(out=e1, in0=e1, in1=Mex[d], op=ALU.max), reads=["e1", ("Mex", d)], writes=["e1"])
                        P.op("dve", lambda e: e.reciprocal(out=e1, in_=e1), reads=["e1"], writes=["e1"])
                        if d == 0:
                            P.op("dve", lambda e: e.tensor_tensor(out=hsum, in0=psn[:], in1=e1, op=ALU.mult), reads=["psn", "e1"], writes=["hsum"])
                        else:
                            P.op("dve", lambda e: e.tensor_tensor(out=e2, in0=psn[:], in1=e1, op=ALU.mult), reads=["psn", "e1"], writes=["e2"])
                            P.op("pool", lambda e: e.tensor_tensor(out=hsum, in0=hsum, in1=e2, op=ALU.add), reads=["e2", "hsum"], writes=["hsum"])
                    pb = psx[nx % 2]
                    for kc in range(8):
                        P.op("pe", lambda e, kc=kc, ts=ts, pb=pb, wo=wo: e.matmul(pb[:], lhsT=wo[:, kc, :], rhs=C.xbf[:, kc, ts], start=(kc == 0), stop=(kc == 7)),
                             reads=[wokey, ("xbf", kc, tt)], writes=[("psx", nx % 2)])
                    P.op("act", lambda e, pb=pb: e.activation(out=sgo[:], in_=pb[:], func=AF.Sigmoid), reads=[("psx", nx % 2)], writes=["sgo"])
                    nx += 1
                    P.op("pe", lambda e: e.matmul(psn[:], lhsT=C.onesf[:], rhs=hsum, start=True, stop=True), reads=["hsum", "onesf"], writes=["psn"])
                    P.op("act", lambda e: e.activation(out=e1, in_=hsum, func=AF.Square), reads=["hsum", "e1"], writes=["e1"])
                    P.op("pe", lambda e: e.matmul(psdn[:], lhsT=C.onesf[:], rhs=e1, start=True, stop=True), reads=["e1", "onesf"], writes=["psdn"])
                    P.op("act", lambda e: e.activation(out=e2, in_=psn[:], func=AF.Copy), reads=["psn", "e2"], writes=["e2"])
                    P.op("dve", lambda e: e.tensor_tensor(out=e1, in0=e2, in1=e2, op=ALU.mult), reads=["e2", "e1"], writes=["e1"])
                    P.op("dve", lambda e: e.tensor_tensor(out=e1, in0=psdn[:], in1=e1, op=ALU.subtract), reads=["psdn", "e1"], writes=["e1"])
                    P.op("act", lambda e: e.activation(out=e1, in_=e1, func=AF.Sqrt, bias=C.epsc[:, 0:1], scale=1.0), reads=["e1", "epsc"], writes=["e1"])
                    P.op("dve", lambda e: e.reciprocal(out=e1, in_=e1), reads=["e1"], writes=["e1"])
                    P.op("pool", lambda e: e.tensor_tensor(out=e2, in0=hsum, in1=e2, op=ALU.subtract), reads=["hsum", "e2"], writes=["e2"])
                    P.op("dve", lambda e: e.tensor_tensor(out=e2, in0=e2, in1=e1, op=ALU.mult), reads=["e1", "e2"], writes=["e2"])
                    P.op("dve", lambda e, ts=ts, h=h: e.scalar_tensor_tensor(out=hTf[:, h, ts], in0=e2, scalar=ng[:, h:h + 1], in1=sgo[:], op0=ALU.mult, op1=ALU.mult),
                         reads=["e2", "ng", "sgo"], writes=[("yT", h)])
                P.barrier()
        with P.phase():
            T = ln_temps(P)
            emit_mixer_out(C, l, hTf, W["od_w_out"][j], 8, T)


def emit_even(C, l):
    P = C.P
    W = C.W
    j = l // 2
    win = kc_view(W["ev_w_in"][j])
    PI = math.pi
    with P.phase():
        yA = P.sb("yA", [128, 4, L], BF16)
        hS = P.sb("hS", [128, 4, L], BF16)
        with P.phase():
            qT = P.sb("qT", [128, 4, L], BF16)
            kT = P.sb("kT", [128, L], BF16)
            vt = P.sb("vt", [128, NB, 128], BF16)
            ab8 = P.sb("ab8", [128, 8, 3, 128], BF16)
            P.dma("pool", ab8[:], C.consts["abias"], writes=["ab8"])
            esk = P.sb("esk", [128, 4])
            for hf in range(2):
                P.dma("sp", esk[hf * 64:(hf + 1) * 64, :], W["ev_sink"][j:j + 1, hf * 4:hf * 4 + 4].partition_broadcast(64), writes=["esk"])
            P.op("act", lambda e: e.activation(out=esk[:], in_=esk[:], func=AF.Exp), reads=["esk"], writes=["esk"])
            wb = [P.sb("wbuf", [128, 8, 128], BF16) for _ in range(2)]
            psx = [P.ps("psx", [128, 512]) for _ in range(2)]
            nx = 0
            nw = 0
            wq_view = win[:, :, 0:512].rearrange("k kc (hf c d) -> k kc c hf d", hf=2, c=4)
            for c in range(5):
                wt = wb[nw % 2]
                wkey = ("wbuf", nw % 2)
                nw += 1
                if c < 4:
                    for kc in range(8):
                        P.dma("pool", wt[:, kc, :].rearrange("k (hf d) -> k hf d", hf=2), wq_view[:, kc, c, :, :], writes=[wkey])
                    dst = qT[:, c, :]
                else:
                    P.dma("pool", wt[:], win[:, :, 512:640], writes=[wkey])
                    dst = kT[:, :]
                for tt in range(NTT):
                    pb = psx[nx % 2]
                    for kc in range(8):
                        P.op("pe", lambda e, kc=kc, tt=tt, pb=pb, wt=wt: e.matmul(pb[:], lhsT=wt[:, kc, :], rhs=C.xbf[:, kc, tt * 512:(tt + 1) * 512], start=(kc == 0), stop=(kc == 7)),
                             reads=[wkey, ("xbf", kc, tt)], writes=[("psx", nx % 2)])
                    if tt % 2 == 0:
                        P.op("act", lambda e, tt=tt, pb=pb, dst=dst: e.activation(out=dst[:, tt * 512:(tt + 1) * 512], in_=pb[:], func=AF.Copy), reads=[("psx", nx % 2)], writes=["qk"])
                    else:
                        P.op("dve", lambda e, tt=tt, pb=pb, dst=dst: e.tensor_copy(out=dst[:, tt * 512:(tt + 1) * 512], in_=pb[:]), reads=[("psx", nx % 2)], writes=["qk"])
                    nx += 1
            wt = wb[nw % 2]
            wkey = ("wbuf", nw % 2)
            nw += 1
            P.dma("pool", wt[:], win[:, :, 640:768], writes=[wkey])
            for nb in range(NB):
                pb = psx[nx % 2]
                for kc in range(8):
                    P.op("pe", lambda e, kc=kc, nb=nb, pb=pb, wt=wt: e.matmul(pb[:, 0:128], lhsT=C.xbf[:, kc, nb * 128:(nb + 1) * 128], rhs=wt[:, kc, :], start=(kc == 0), stop=(kc == 7)),
                         reads=[wkey, ("xbf", kc, nb // 4)], writes=[("psx", nx % 2)])
                if nb % 2 == 0:
                    P.op("act", lambda e, nb=nb, pb=pb: e.activation(out=vt[:, nb, :], in_=pb[:, 0:128], func=AF.Copy), reads=[("psx", nx % 2)], writes=["vt"])
                else:
                    P.op("dve", lambda e, nb=nb, pb=pb: e.tensor_copy(out=vt[:, nb, :], in_=pb[:, 0:128]), reads=[("psx", nx % 2)], writes=["vt"])
                nx += 1
            PT = [P.sb("PT", [128, 3, 128], BF16) for _ in range(2)]
            dn = P.sb("dn", [128, 128])
            pss = [P.ps("pss", [128, 512]) for _ in range(2)]
            psn = P.ps("psn", [128, 512])
            psdn = P.ps("psdn", [128, 512])
            units = [(c, qb, hf) for c in range(4) for qb in range(NB) for hf in range(2)]

            def afront(u, b):
                c, qb, hf = u
                qs = slice(qb * 128, (qb + 1) * 128)
                h = hf * 4 + c
                r = slice(hf * 64, (hf + 1) * 64)
                rl = [r3 for r3 in range(3) if 0 <= qb + r3 - 1 < NB]
                for r3 in rl:
                    kb = qb + r3 - 1
                    P.op("pe", lambda e, b=b, r3=r3, kb=kb, r=r, c=c, qs=qs: e.matmul(pss[b][:, r3 * 128:(r3 + 1) * 128], lhsT=kT[r, kb * 128:(kb + 1) * 128], rhs=qT[r, c, qs], start=True, stop=False),
                         reads=["qk"], writes=[("pss", b)])
                    P.op("pe", lambda e, b=b, r3=r3, h=h: e.matmul(pss[b][:, r3 * 128:(r3 + 1) * 128], lhsT=C.identb[:], rhs=ab8[:, h, r3, :], start=False, stop=True),
                         reads=["identb", "ab8"], writes=[("pss", b)])
                c0, c1 = rl[0] * 128, (rl[-1] + 1) * 128
                P.op("act", lambda e, b=b, c0=c0, c1=c1: e.activation(out=PT[b][:].rearrange("p a b -> p (a b)")[:, c0:c1], in_=pss[b][:, c0:c1], func=AF.Exp, scale=0.125),
                     reads=[("pss", b)], writes=[("PT", b)])

            def aback(u, b):
                c, qb, hf = u
                qs = slice(qb * 128, (qb + 1) * 128)
                r = slice(hf * 64, (hf + 1) * 64)
                rl = [r3 for r3 in range(3) if 0 <= qb + r3 - 1 < NB]
                for i3, r3 in enumerate(rl):
                    kb = qb + r3 - 1
                    P.op("pe", lambda e, b=b, r3=r3, kb=kb, r=r, hf=hf, i3=i3, rl=rl: e.matmul(psn[r, 0:128], lhsT=vt[:, kb, hf * 64:(hf + 1) * 64], rhs=PT[b][:, r3, :], start=(i3 == 0), stop=(i3 == len(rl) - 1)),
                         reads=["vt", ("PT", b)], writes=["psn"])
                for i3, r3 in enumerate(rl):
                    P.op("pe", lambda e, b=b, r3=r3, r=r, i3=i3, rl=rl: e.matmul(psdn[r, 0:128], lhsT=C.onesb[:, 0:64], rhs=PT[b][:, r3, :], start=(i3 == 0), stop=(i3 == len(rl) - 1)),
                         reads=["onesb", ("PT", b)], writes=["psdn"])
                if hf == 1:
                    P.op("dve", lambda e, c=c: e.tensor_scalar(out=dn[:], in0=psdn[:, 0:128], scalar1=esk[:, c:c + 1], scalar2=None, op0=ALU.add), reads=["psdn", "esk"], writes=["dn"])
                    P.op("dve", lambda e: e.reciprocal(out=dn[:], in_=dn[:]), reads=["dn"], writes=["dn"])
                    P.op("dve", lambda e, c=c, qs=qs: e.tensor_tensor(out=yA[:, c, qs], in0=psn[:, 0:128], in1=dn[:], op=ALU.mult), reads=["psn", "dn"], writes=[("yA", c)])

            afront(units[0], 0)
            for i, u in enumerate(units):
                if i + 1 < len(units):
                    afront(units[i + 1], (i + 1) % 2)
                aback(u, i % 2)
        with P.phase():
            NPW = 22
            pw_exp = list(range(1, 17)) + [32, 64, 128, 256, 512, 1024]
            pwr = P.sb("pwr", [128, 32, NPW])
            pwi = P.sb("pwi", [128, 32, NPW])
            pni = P.sb("pni", [128, 32, NPW])
            BTp = P.sb("BTp", [128, 4, 2, 2, 2, 128], BF16)
            CTp = P.sb("CTp", [128, 16, 2, 2, 64], BF16)
            dsk = P.sb("dsk", [128, 4])
            P.dma("sp", dsk[:], W["s5_d"][j].rearrange("(c p) -> p c", p=128), writes=["dsk"], allow_slow_non_contiguous=True)
            with P.phase():
                lin = P.sb("lin", [32, 2, 128])
                P.dma("sp", lin[:, 0, :], W["s5_a_re"][j].rearrange("d (gp g2) p -> (d gp) (g2 p)", g2=2), writes=["lin"])
                P.dma("act", lin[:, 1, :], W["s5_a_im"][j].rearrange("d (gp g2) p -> (d gp) (g2 p)", g2=2), writes=["lin"])
                pst = P.ps("pst", [128, 512])
                aT = P.sb("aT", [128, 2, 32])
                for i in range(2):
                    P.op("pe", lambda e, i=i: e.transpose(pst[:, i * 32:(i + 1) * 32], lin[:, i, :], C.ident[0:32, 0:32]), reads=["lin", "ident"], writes=["pst"])
                P.op("dve", lambda e: e.tensor_copy(out=aT[:].rearrange("p a b -> p (a b)"), in_=pst[:, 0:64]), reads=["pst"], writes=["aT"])
                dt = P.sb("dt", [128, 32])
                for d in range(2):
                    for g2 in range(2):
                        src = W["s5_log_dt"][j, d:d + 1].rearrange("o (gp g2) -> o g2 gp", g2=2)[:, g2, :]
                        P.dma("sp", dt[g2 * 64:(g2 + 1) * 64, d * 16:(d + 1) * 16], src.partition_broadcast(64), writes=["dt"], allow_slow_non_contiguous=True)
                P.op("act", lambda e: e.activation(out=dt[:], in_=dt[:], func=AF.Exp), reads=["dt"], writes=["dt"])
                tA = [P.sb("tA%d" % i, [128, 32]) for i in range(8)]
                mag, ang, cs, sn, zr, zi, t6, t7 = tA
                ar, ai = aT[:, 0, :], aT[:, 1, :]
                TT = lambda out, a, b, op, rk, wk: P.op("dve", lambda e: e.tensor_tensor(out=out, in0=a, in1=b, op=op), reads=rk, writes=wk)
                TT(mag[:], ar, dt[:], ALU.mult, ["aT", "dt"], ["mag"])
                P.op("act", lambda e: e.activation(out=mag[:], in_=mag[:], func=AF.Exp), reads=["mag"], writes=["mag"])
                TT(ang[:], ai, dt[:], ALU.mult, ["aT", "dt"], ["ang"])
                ki = P.sb("ki", [128, 32], mybir.dt.int32)
                kf = P.sb("kf", [128, 32])
                for (dst, shift, key) in ((sn, 0.0, "sn"), (cs, 0.5 * PI, "cs")):
                    P.op("dve", lambda e, shift=shift: e.tensor_scalar(out=kf[:], in0=ang[:], scalar1=shift, scalar2=1.0 / (2 * PI), op0=ALU.add, op1=ALU.mult), reads=["ang", "kf"], writes=["kf"])
                    P.op("dve", lambda e: e.tensor_copy(out=ki[:], in_=kf[:]), reads=["kf", "ki"], writes=["ki"])
                    P.op("dve", lambda e: e.tensor_copy(out=kf[:], in_=ki[:]), reads=["ki"], writes=["kf"])
                    P.op("dve", lambda e, dst=dst, shift=shift: e.tensor_scalar(out=dst[:], in0=ang[:], scalar1=shift, scalar2=None, op0=ALU.add), reads=["ang"], writes=[key])
                    P.op("dve", lambda e, dst=dst: e.scalar_tensor_tensor(out=dst[:], in0=kf[:], scalar=-2 * PI, in1=dst[:], op0=ALU.mult, op1=ALU.add), reads=["kf", key], writes=[key])
                    P.op("dve", lambda e, dst=dst: e.tensor_scalar(out=kf[:], in0=dst[:], scalar1=PI, scalar2=-2 * PI, op0=ALU.is_gt, op1=ALU.mult), reads=[key, "kf"], writes=["kf"])
                    P.op("dve", lambda e, dst=dst: e.tensor_tensor(out=dst[:], in0=dst[:], in1=kf[:], op=ALU.add), reads=["kf", key], writes=[key])
                    P.op("dve", lambda e, dst=dst: e.tensor_scalar(out=kf[:], in0=dst[:], scalar1=-PI, scalar2=2 * PI, op0=ALU.is_lt, op1=ALU.mult), reads=[key, "kf"], writes=["kf"])
                    P.op("dve", lambda e, dst=dst: e.tensor_tensor(out=dst[:], in0=dst[:], in1=kf[:], op=ALU.add), reads=["kf", key], writes=[key])
                P.op("act", lambda e: e.activation(out=sn[:], in_=sn[:], func=AF.Sin), reads=["sn"], writes=["sn"])
                P.op("act", lambda e: e.activation(out=cs[:], in_=cs[:], func=AF.Sin), reads=["cs"], writes=["cs"])
                TT(pwr[:, :, 0], mag[:], cs[:], ALU.mult, ["mag", "cs"], ["pw"])
                TT(pwi[:, :, 0], mag[:], sn[:], ALU.mult, ["mag", "sn"], ["pw"])
                P.op("dve", lambda e: e.tensor_scalar(out=t6[:], in0=pwr[:, :, 0], scalar1=-1.0, scalar2=None, op0=ALU.add), reads=["pw"], writes=["t6"])
                TT(zr[:], t6[:], ar, ALU.mult, ["t6", "aT"], ["zr"])
                TT(t7[:], pwi[:, :, 0], ai, ALU.mult, ["pw", "aT"], ["t7"])
                TT(zr[:], zr[:], t7[:], ALU.add, ["zr", "t7"], ["zr"])
                TT(zi[:], pwi[:, :, 0], ar, ALU.mult, ["pw", "aT"], ["zi"])
                TT(t7[:], t6[:], ai, ALU.mult, ["t6", "aT", "t7"], ["t7"])
                TT(zi[:], zi[:], t7[:], ALU.subtract, ["zi", "t7"], ["zi"])
                TT(t6[:], ar, ar, ALU.mult, ["aT", "t6"], ["t6"])
                TT(t7[:], ai, ai, ALU.mult, ["aT", "t7"], ["t7"])
                TT(t6[:], t6[:], t7[:], ALU.add, ["t6", "t7"], ["t6"])
                P.op("dve", lambda e: e.reciprocal(out=t6[:], in_=t6[:]), reads=["t6"], writes=["t6"])
                TT(zr[:], zr[:], t6[:], ALU.mult, ["zr", "t6"], ["zr"])
                TT(zi[:], zi[:], t6[:], ALU.mult, ["zi", "t6"], ["zi"])
                nzr, nzi = mag, ang
                P.op("dve", lambda e: e.tensor_scalar(out=nzr[:], in0=zr[:], scalar1=-1.0, scalar2=None, op0=ALU.mult), reads=["zr", "mag"], writes=["nzr"])
                P.op("dve", lambda e: e.tensor_scalar(out=nzi[:], in0=zi[:], scalar1=-1.0, scalar2=None, op0=ALU.mult), reads=["zi", "ang"], writes=["nzi"])
                def cmul(oi, ai_, bi_):
                    TT(t6[:], pwr[:, :, ai_], pwr[:, :, bi_], ALU.mult, ["pw", "t6"], ["t6"])
                    TT(t7[:], pwi[:, :, ai_], pwi[:, :, bi_], ALU.mult, ["pw", "t7"], ["t7"])
                    TT(pwr[:, :, oi], t6[:], t7[:], ALU.subtract, ["t6", "t7"], ["pw"])
                    TT(t6[:], pwr[:, :, ai_], pwi[:, :, bi_], ALU.mult, ["pw", "t6"], ["t6"])
                    TT(t7[:], pwi[:, :, ai_], pwr[:, :, bi_], ALU.mult, ["pw", "t7"], ["t7"])
                    TT(pwi[:, :, oi], t6[:], t7[:], ALU.add, ["t6", "t7"], ["pw"])
                for k in range(1, 16):
                    cmul(k, k - 1, 0)
                for k in range(16, NPW):
                    cmul(k, k - 1, k - 1)
                P.op("dve", lambda e: e.tensor_scalar(out=pni[:], in0=pwi[:], scalar1=-1.0, scalar2=None, op0=ALU.mult), reads=["pw"], writes=["pni"])
                Bin = P.sb("Bin", [128, 16, 16])
                Bexp = P.sb("Bexp", [128, 16, 128], BF16)
                pbt = P.ps("pbt", [128, 4, 128], BF16)
                for d in range(2):
                    for ri in range(2):
                        src = W["s5_b_re" if ri == 0 else "s5_b_im"][j, d].rearrange("(gp g2) p h -> (g2 p) gp h", g2=2)
                        for hh in range(2):
                            P.dma("sp" if hh == 0 else "act", Bin[:, hh * 8:(hh + 1) * 8, :], src[:, hh * 8:(hh + 1) * 8, :], writes=["Bin"])
                        P.op("pool", lambda e: e.memset(Bexp[:], 0.0), writes=["Bexp"])
                        for g2 in range(2):
                            for q in range(4):
                                P.op("dve", lambda e, g2=g2, q=q: e.tensor_copy(
                                    out=Bexp[g2 * 64:(g2 + 1) * 64, q:16:4, q * 32 + g2 * 16:q * 32 + g2 * 16 + 16],
                                    in_=Bin[g2 * 64:(g2 + 1) * 64, q:16:4, :]), reads=["Bin", "Bexp"], writes=["Bexp"])
                        for c in range(4):
                            for q in range(4):
                                P.op("pe", lambda e, c=c, q=q: e.transpose(pbt[:, q, :], Bexp[:, 4 * c + q, :], C.identb[:]), reads=["Bexp", "identb"], writes=["pspbt"])
                            for q in range(4):
                                rr = slice((q // 2) * 64, (q // 2) * 64 + 64)
                                P.op("dve", lambda e, c=c, q=q, d=d, ri=ri, rr=rr: e.tensor_copy(out=BTp[rr, c, q % 2, d, ri, :], in_=pbt[rr, q, :]), reads=["pspbt"], writes=["BTp"])
                Cin = P.sb("Cin", [128, 2, 64])
                Craw = P.sb("Craw", [128, 16, 2, 2, 16])
                pct = P.ps("psct", [128, 512])
                for d in range(2):
                    for ri in range(2):
                        srcC = W["s5_c_re" if ri == 0 else "s5_c_im"][j, d].rearrange("g h p -> (g h) p")
                        for t in range(4):
                            for hh in range(2):
                                P.dma("sp" if hh == 0 else "act", Cin[:, hh, :], srcC[t * 128:(t + 1) * 128, :], writes=["Cin"])
                            P.op("pe", lambda e: e.transpose(pct[:, 0:128], Cin[:].rearrange("p a b -> p (a b)"), C.ident[:]), reads=["Cin", "ident"], writes=["psct"])
                            for g2 in range(2):
                                P.op("dve", lambda e, g2=g2, t=t, d=d, ri=ri: e.tensor_copy(
                                    out=Craw[g2 * 64:(g2 + 1) * 64, 4 * t:4 * t + 4, d, ri, :],
                                    in_=pct[g2 * 64:(g2 + 1) * 64, 0:128].rearrange("p (i g h) -> p i g h", g=2, h=16)[:, :, g2, :]),
                                    reads=["psct"], writes=["Craw"])
                P.op("pool", lambda e: e.memset(CTp[:], 0.0), writes=["CTp"])
                tc1 = P.sb("tc1", [128, 16])
                for d in range(2):
                    for gp in range(16):
                        col = d * 16 + gp
                        w = gp % 2
                        for ri, (s1, s2) in enumerate(((zr, nzi), (nzi, nzr))):
                            P.op("dve", lambda e, gp=gp, d=d, s1=s1, col=col: e.tensor_scalar(out=tc1[:], in0=Craw[:, gp, d, 0, :], scalar1=s1[:, col:col + 1], scalar2=None, op0=ALU.mult),
                                 reads=["Craw", "zr", "nzr", "nzi"], writes=["tc1"])
                            for g2 in range(2):
                                rr = slice(g2 * 64, (g2 + 1) * 64)
                                P.op("dve", lambda e, gp=gp, d=d, s2=s2, col=col, rr=rr, ri=ri, w=w, g2=g2: e.scalar_tensor_tensor(
                                    out=CTp[rr, gp, d, ri, w * 32 + g2 * 16:w * 32 + g2 * 16 + 16], in0=Craw[rr, gp, d, 1, :], scalar=s2[rr, col:col + 1], in1=tc1[rr, :], op0=ALU.mult, op1=ALU.add),
                                    reads=["Craw", "tc1", "zr", "nzr", "nzi", "CTp"], writes=["CTp"])
            if C.dbgout:
                P.dma("sp", C.dbgout["pwr"], pwr[:], reads=["pw"])
                P.dma("sp", C.dbgout["pwi"], pwi[:], reads=["pw"])
                P.dma("sp", C.dbgout["BTp"], BTp[:], reads=["BTp"])
                P.dma("sp", C.dbgout["CTp"], CTp[:], reads=["CTp"])
            wb = [P.sb("wbuf", [128, 8, 128], BF16) for _ in range(2)]
            uT = P.sb("uT", [128, L], BF16)
            Xr = P.sb("Xr", [128, 16, 128])
            Xi = P.sb("Xi", [128, 16, 128])
            Xbr = P.sb("Xbr", [128, L], BF16)
            Xbi = P.sb("Xbi", [128, L], BF16)
            Sb = [[P.sb("Sb", [128, 130]) for _ in range(2)] for _ in range(2)]
            pt_ = P.sb("s5post", [128, 512])
            psy = [P.ps("psy", [128, 512]) for _ in range(4)]
            psb = [P.ps("psb", [128, 512]) for _ in range(4)]
            Xbr3 = Xbr[:].rearrange("p (n a) -> p a n", a=16)
            Xbi3 = Xbi[:].rearrange("p (n a) -> p a n", a=16)
            for i in range(2):
                for k in range(2):
                    P.op("pool", lambda e, i=i, k=k: e.memset(Sb[i][k][:], 0.0), writes=[("Sb", i, k)])
            nb_ = 0
            for c in range(4):
                wt = wb[c % 2]
                wkey = ("wbuf", c % 2)
                P.dma("pool", wt[:], win[:, :, 768 + c * 128:768 + (c + 1) * 128], writes=[wkey])
                for tt in range(NTT):
                    pb = psb[nb_ % 4]
                    for kc in range(8):
                        P.op("pe", lambda e, kc=kc, tt=tt, pb=pb, wt=wt: e.matmul(pb[:], lhsT=wt[:, kc, :], rhs=C.xbf[:, kc, tt * 512:(tt + 1) * 512], start=(kc == 0), stop=(kc == 7)),
                             reads=[wkey, ("xbf", kc, tt)], writes=[("psb", nb_ % 4)])
                    P.op("act", lambda e, tt=tt, pb=pb: e.activation(out=uT[:, tt * 512:(tt + 1) * 512], in_=pb[:], func=AF.Copy), reads=[("psb", nb_ % 4)], writes=["uT"])
                    nb_ += 1
                for q in range(4):
                    gp = 4 * c + q
                    kr = slice((q // 2) * 64, (q // 2) * 64 + 64)
                    for d in range(2):
                        col = d * 16 + gp
                        LR = lambda k, col=col: pwr[:, col, k:k + 1]
                        LI = lambda k, col=col: pwi[:, col, k:k + 1]
                        NI = lambda k, col=col: pni[:, col, k:k + 1]
                        for tt in range(NTT):
                            for ri, X in enumerate((Xr, Xi)):
                                pb = psb[nb_ % 4]
                                P.op("pe", lambda e, tt=tt, pb=pb, ri=ri, kr=kr, q=q, d=d, c=c: e.matmul(pb[:], lhsT=BTp[kr, c, q % 2, d, ri, :], rhs=uT[kr, tt * 512:(tt + 1) * 512], start=True, stop=True),
                                     reads=["BTp", "uT"], writes=[("psb", nb_ % 4)])
                                if ri == 0:
                                    P.op("act", lambda e, tt=tt, pb=pb, X=X: e.activation(out=X[:, :, tt * 32:(tt + 1) * 32], in_=pb[:].rearrange("p (n a) -> p a n", a=16), func=AF.Copy),
                                         reads=[("psb", nb_ % 4)], writes=["X"])
                                else:
                                    P.op("act", lambda e, tt=tt, pb=pb, X=X: e.activation(out=X[:, :, tt * 32:(tt + 1) * 32], in_=pb[:].rearrange("p (n a) -> p a n", a=16), func=AF.Copy),
                                         reads=[("psb", nb_ % 4)], writes=["X"])
                                nb_ += 1
                        steps = range(1, 16) if d == 0 else range(14, -1, -1)
                        for a in steps:
                            ap_ = a - 1 if d == 0 else a + 1
                            P.op("dve", lambda e, sc_=LR(0), a=a, ap_=ap_: e.scalar_tensor_tensor(out=Xr[:, a, :], in0=Xr[:, ap_, :], scalar=sc_, in1=Xr[:, a, :], op0=ALU.mult, op1=ALU.add), reads=["X", "pw"], writes=["X"])
                            P.op("dve", lambda e, sc_=LR(0), a=a, ap_=ap_: e.scalar_tensor_tensor(out=Xi[:, a, :], in0=Xi[:, ap_, :], scalar=sc_, in1=Xi[:, a, :], op0=ALU.mult, op1=ALU.add), reads=["X", "pw"], writes=["X"])
                            P.op("dve", lambda e, sc_=NI(0), a=a, ap_=ap_: e.scalar_tensor_tensor(out=Xr[:, a, :], in0=Xi[:, ap_, :], scalar=sc_, in1=Xr[:, a, :], op0=ALU.mult, op1=ALU.add), reads=["X", "pni"], writes=["X"])
                            P.op("dve", lambda e, sc_=LI(0), a=a, ap_=ap_: e.scalar_tensor_tensor(out=Xi[:, a, :], in0=Xr[:, ap_, :], scalar=sc_, in1=Xi[:, a, :], op0=ALU.mult, op1=ALU.add), reads=["X", "pw"], writes=["X"])
                        ae = 15 if d == 0 else 0
                        cur = 0
                        P.op("dve", lambda e, ae=ae: e.tensor_copy(out=Sb[0][0][:, 1:129], in_=Xr[:, ae, :]), reads=["X", ("Sb", 0, 0)], writes=[("Sb", 0, 0)])
                        P.op("dve", lambda e, ae=ae: e.tensor_copy(out=Sb[0][1][:, 1:129], in_=Xi[:, ae, :]), reads=["X", ("Sb", 0, 1)], writes=[("Sb", 0, 1)])
                        s = 1
                        lev = 0
                        while s < 128:
                            k = 15 + lev
                            A, B = Sb[cur], Sb[1 - cur]
                            ka = [("Sb", cur, 0), ("Sb", cur, 1)]
                            kb = [("Sb", 1 - cur, 0), ("Sb", 1 - cur, 1)]
                            if d == 0:
                                o_, i_, h_ = slice(1 + s, 129), slice(1, 129 - s), slice(1, 1 + s)
                            else:
                                o_, i_, h_ = slice(1, 129 - s), slice(1 + s, 129), slice(129 - s, 129)
                            P.op("dve", lambda e, sc_=LR(k), A=A, B=B, o_=o_, i_=i_, k=k: e.scalar_tensor_tensor(out=B[0][:, o_], in0=A[0][:, i_], scalar=sc_, in1=A[0][:, o_], op0=ALU.mult, op1=ALU.add), reads=ka + ["pw"], writes=[kb[0]])
                            P.op("dve", lambda e, sc_=LR(k), A=A, B=B, o_=o_, i_=i_, k=k: e.scalar_tensor_tensor(out=B[1][:, o_], in0=A[1][:, i_], scalar=sc_, in1=A[1][:, o_], op0=ALU.mult, op1=ALU.add), reads=ka + ["pw"], writes=[kb[1]])
                            P.op("dve", lambda e, sc_=NI(k), A=A, B=B, o_=o_, i_=i_, k=k: e.scalar_tensor_tensor(out=B[0][:, o_], in0=A[1][:, i_], scalar=sc_, in1=B[0][:, o_], op0=ALU.mult, op1=ALU.add), reads=ka + ["pni", kb[0]], writes=[kb[0]])
                            P.op("dve", lambda e, sc_=LI(k), A=A, B=B, o_=o_, i_=i_, k=k: e.scalar_tensor_tensor(out=B[1][:, o_], in0=A[0][:, i_], scalar=sc_, in1=B[1][:, o_], op0=ALU.mult, op1=ALU.add), reads=ka + ["pw", kb[1]], writes=[kb[1]])
                            P.op("pool", lambda e, A=A, B=B, h_=h_: e.tensor_copy(out=B[0][:, h_], in_=A[0][:, h_]), reads=[ka[0]], writes=[kb[0]])
                            P.op("pool", lambda e, A=A, B=B, h_=h_: e.tensor_copy(out=B[1][:, h_], in_=A[1][:, h_]), reads=[ka[1]], writes=[kb[1]])
                            cur = 1 - cur
                            s *= 2
                            lev += 1
                        S = Sb[cur]
                        ks = [("Sb", cur, 0), ("Sb", cur, 1)]
                        cin = slice(0, 128) if d == 0 else slice(2, 130)
                        for a in range(16):
                            k = a if d == 0 else 15 - a
                            P.op("dve", lambda e, sc_=LR(k), a=a, k=k, S=S, cin=cin: e.scalar_tensor_tensor(out=Xr[:, a, :], in0=S[0][:, cin], scalar=sc_, in1=Xr[:, a, :], op0=ALU.mult, op1=ALU.add), reads=["X", "pw"] + ks, writes=["X"])
                            P.op("dve", lambda e, sc_=LR(k), a=a, k=k, S=S, cin=cin: e.scalar_tensor_tensor(out=Xi[:, a, :], in0=S[1][:, cin], scalar=sc_, in1=Xi[:, a, :], op0=ALU.mult, op1=ALU.add), reads=["X", "pw"] + ks, writes=["X"])
                            P.op("dve", lambda e, sc_=NI(k), a=a, k=k, S=S, cin=cin: e.scalar_tensor_tensor(out=Xbr3[:, a, :], in0=S[1][:, cin], scalar=sc_, in1=Xr[:, a, :], op0=ALU.mult, op1=ALU.add), reads=["X", "pni"] + ks, writes=["Xb"])
                            P.op("dve", lambda e, sc_=LI(k), a=a, k=k, S=S, cin=cin: e.scalar_tensor_tensor(out=Xbi3[:, a, :], in0=S[0][:, cin], scalar=sc_, in1=Xi[:, a, :], op0=ALU.mult, op1=ALU.add), reads=["X", "pw"] + ks, writes=["Xb"])
                        if C.dbgout and gp == 0 and d == 0:
                            P.dma("sp", C.dbgout["Xbr"], Xbr[:], reads=["Xb"])
                            P.dma("sp", C.dbgout["Xbi"], Xbi[:], reads=["Xb"])
                        for tt in range(NTT):
                            for ri, Xb in enumerate((Xbr, Xbi)):
                                first = (q % 2 == 0 and d == 0 and ri == 0)
                                last = (q % 2 == 1 and d == 1 and ri == 1)
                                P.op("pe", lambda e, tt=tt, ri=ri, Xb=Xb, gp=gp, d=d, kr=kr, first=first, last=last: e.matmul(psy[tt][kr, :], lhsT=CTp[:, gp, d, ri, :], rhs=Xb[:, tt * 512:(tt + 1) * 512], start=first, stop=last),
                                     reads=["CTp", "Xb"], writes=[("psy", tt)])
                for tt in range(NTT):
                    ts = slice(tt * 512, (tt + 1) * 512)
                    P.op("dve", lambda e, tt=tt, ts=ts, c=c: e.scalar_tensor_tensor(out=pt_[:], in0=uT[:, ts], scalar=dsk[:, c:c + 1], in1=psy[tt][:], op0=ALU.mult, op1=ALU.add),
                         reads=["uT", "dsk", ("psy", tt)], writes=["s5post"])
                    P.op("act", lambda e, ts=ts, c=c: e.activation(out=hS[:, c, ts], in_=pt_[:], func=AF.Gelu_apprx_tanh), reads=["s5post"], writes=[("hS", c)])
            P.barrier()
            wglu = P.sb("wglu", [128, 4, 512], BF16) if False else None
        with P.phase():
            wglu = P.sb("wglu", [128, 4, 512], BF16)
            P.dma("pool", wglu[:], W["s5_w_glu"][j].rearrange("(k p) m -> p k m", p=128), writes=["wglu"])
            gs = [P.sb("gs", [128, 512], BF16) for _ in range(4)]
            psg = [P.ps("psgl", [128, 512]) for _ in range(2)]
            n = 0
            for tt in range(NTT):
                ts = slice(tt * 512, (tt + 1) * 512)
                for m in range(4):
                    pb = psg[n % 2]
                    for k in range(4):
                        P.op("pe", lambda e, k=k, m=m, ts=ts, pb=pb: e.matmul(pb[:], lhsT=wglu[:, k, m * 128:(m + 1) * 128], rhs=hS[:, k, ts], start=(k == 0), stop=(k == 3)),
                             reads=["wglu", ("hS", k, tt)], writes=[("psgl", n % 2)])
                    P.op("act", lambda e, m=m, pb=pb: e.activation(out=gs[m][:], in_=pb[:], func=AF.Sigmoid), reads=[("psgl", n % 2)], writes=[("gs", m)])
                    n += 1
                for m in range(4):
                    P.op("dve" if m % 2 == 0 else "pool", lambda e, m=m, ts=ts: e.tensor_tensor(out=hS[:, m, ts], in0=hS[:, m, ts], in1=gs[m][:], op=ALU.mult),
                         reads=[("gs", m)] + [("hS", k, tt) for k in range(4)], writes=[("hS", m, tt)])
        if C.dbgout:
            with P.phase():
                P.dma("sp", C.dbgout["yA"], yA[:], reads=[])
                P.dma("act", C.dbgout["hS"], hS[:], reads=[])
        with P.phase():
            T = ln_temps(P)
            wo = [P.sb("wo", [128, 8, 128], BF16) for _ in range(2)]
            pso = [P.ps("pso", [128, 512]) for _ in range(2)]
            wsrcA = W["ev_w_out"][j][0:512, :].rearrange("(hf c d) m -> hf d c m", hf=2, c=4)
            wsrcS = W["ev_w_out"][j][512:1024, :].rearrange("(k p) m -> p k m", p=128)
            for m in range(8):
                b = m % 2
                ms = slice(m * 128, (m + 1) * 128)
                for hf in range(2):
                    P.dma("pool", wo[b][hf * 64:(hf + 1) * 64, 0:4, :], wsrcA[hf][:, :, ms], writes=[("wo", b)])
                P.dma("pool", wo[b][:, 4:8, :], wsrcS[:, :, ms], writes=[("wo", b)])
                for tt in range(NTT):
                    ts = slice(tt * 512, (tt + 1) * 512)
                    pb = pso[tt % 2]
                    for k in range(8):
                        rhs = yA[:, k, ts] if k < 4 else hS[:, k - 4, ts]
                        P.op("pe", lambda e, k=k, b=b, pb=pb, rhs=rhs: e.matmul(pb[:], lhsT=wo[b][:, k, :], rhs=rhs, start=(k == 0), stop=(k == 7)),
                             reads=[("wo", b)], writes=[("pso", tt % 2)])
                    P.op("dve", lambda e, m=m, ts=ts, pb=pb: e.scalar_tensor_tensor(out=C.xres[:, m, ts], in0=C.xres[:, m, ts], scalar=ALPHA, in1=pb[:], op0=ALU.mult, op1=ALU.add),
                         reads=[("pso", tt % 2), ("xres", m, tt)], writes=[("xres", m, tt)])
            for tt in range(NTT):
                emit_ln(C, l, 0, tt, T)


def build(nseq, layers, in_mode, out_mode, dbg=()):
    nc = bass.Bass("TRN2", target_bir_lowering=False)
    C = Ctx()
    C.nc = nc
    C.W = {k: nc.dram_tensor(k, s, F32, kind="ExternalInput").ap() for k, s in WEIGHT_SHAPES.items()}
    C.consts = {k: nc.dram_tensor("c_" + k, s, F32, kind="ExternalInput").ap() for k, s in CONST_SHAPES.items()}
    if in_mode == "tm":
        xin = nc.dram_tensor("x", [nseq, L, D], F32, kind="ExternalInput").ap()
    else:
        xin = nc.dram_tensor("x", [nseq, D, L], F32, kind="ExternalInput").ap()
    if out_mode == "tm":
        yout = nc.dram_tensor("y", [nseq, L, D], F32, kind="ExternalOutput").ap()
    else:
        yout = nc.dram_tensor("y", [nseq, D, L], F32, kind="ExternalOutput").ap()
    C.dbgout = {}
    if dbg:
        C.dbgout = {"yA": nc.dram_tensor("dbg_yA", [128, 4, L], BF16, kind="ExternalOutput").ap(),
                    "hS": nc.dram_tensor("dbg_hS", [128, 4, L], BF16, kind="ExternalOutput").ap(),
                    "pwr": nc.dram_tensor("dbg_pwr", [128, 32, 22], F32, kind="ExternalOutput").ap(),
                    "pwi": nc.dram_tensor("dbg_pwi", [128, 32, 22], F32, kind="ExternalOutput").ap(),
                    "BTp": nc.dram_tensor("dbg_BTp", [128, 4, 2, 2, 2, 128], BF16, kind="ExternalOutput").ap(),
                    "CTp": nc.dram_tensor("dbg_CTp", [128, 16, 2, 2, 64], BF16, kind="ExternalOutput").ap(),
                    "Xbr": nc.dram_tensor("dbg_Xbr", [128, L], BF16, kind="ExternalOutput").ap(),
                    "Xbi": nc.dram_tensor("dbg_Xbi", [128, L], BF16, kind="ExternalOutput").ap()}
    P = Prog(nc)
    C.P = P
    C.xres = P.sb("xres", [128, 8, L])
    C.xbf = P.sb("xbf", [128, 8, L], BF16)
    emit_const_setup(C)
    P.barrier()
    P.flush()
    for s in range(nseq):
        (load_x_tm if in_mode == "tm" else load_x_fm)(C, xin[s])
        import os
        STG = os.environ.get("KSTAGE", "all")
        for l in layers:
            if STG in ("all", "mixer"):
                if l % 2 == 0:
                    emit_even(C, l)
                else:
                    emit_mlstm(C, l)
            if STG in ("all", "ffn"):
                emit_ffn(C, l)
        (store_x_tm if out_mode == "tm" else store_x_fm)(C, yout[s])
    P.finish()
    return nc


_CACHE = {}


def run_layers(xs, weights, layers, in_mode, out_mode, nseq):
    key = (tuple(layers), in_mode, out_mode, nseq)
    if key not in _CACHE:
        _CACHE[key] = build(nseq, layers, in_mode, out_mode)
    nc = _CACHE[key]
    consts = host_consts()
    in_maps = []
    for c in range(8):
        m = {k: np.ascontiguousarray(v, dtype=np.float32) for k, v in weights.items()}
        for k, v in consts.items():
            m["c_" + k] = v
        m["x"] = np.ascontiguousarray(xs[c])
        in_maps.append(m)
    res = run_bass_kernel_spmd(nc, in_maps, core_ids=list(range(8)))
    return [r["y"] for r in res.results]


def kernel(**inputs):
    x = np.asarray(inputs["x"], dtype=np.float32)
    weights = {k: np.asarray(v, dtype=np.float32) for k, v in inputs.items() if k != "x"}
    xs = [x[c * 4:(c + 1) * 4] for c in range(8)]
    ys = run_layers(xs, weights, [0, 1, 2, 3], "tm", "tm", 4)
    return np.concatenate(ys, axis=0).astype(np.float32)
```
